# Optimizing a Trainium2 kernel written in Bass

```python
import math
import jax
import jax.numpy as jnp
from jax import lax
import numpy as np


D_MODEL = 1024
BATCH = 8
SEQ = 4096
DEPTH = 2

CTX_LEN = 256
GRID_W = 64
EPS = 1e-6
F32 = jnp.float32

CONV_WIDTH = 512
SSM_WIDTH = 512
SSM_GROUP = 16
SSM_GROUPS = SSM_WIDTH // SSM_GROUP
SSM_STATE = 64
DT_MIN = 1e-3
DT_MAX = 1e-1
EVEN_IN = 4 * CONV_WIDTH + 2 * SSM_WIDTH
EVEN_MIX = CONV_WIDTH + SSM_WIDTH

MLA_HEADS = 16
QK_NOPE = 64
QK_ROPE = 32
QK_DIM = QK_NOPE + QK_ROPE
V_HEAD = 64
Q_LORA = 384
KV_LORA = 256
MLA_MIX = MLA_HEADS * V_HEAD
ODD_IN = Q_LORA + KV_LORA + QK_ROPE + MLA_MIX
ROPE_BASE = 10000.0
Q_BLOCK = 128

N_EVEN = (DEPTH + 1) // 2
N_ODD = DEPTH // 2

kernel_name = 'hybrid_conv_s5_mla_prefix_dit'


def rms_norm(x, w):
    xf = x.astype(F32)
    y = xf * lax.rsqrt(jnp.mean(xf * xf, axis=-1, keepdims=True) + EPS)
    return (y * w.astype(F32)).astype(x.dtype)


def modulation(cond, w, b):
    m = jax.nn.silu(cond) @ w + b
    return jnp.split(m, 3, axis=-1)


def short_conv(v, w):
    n = v.shape[1]
    vp = jnp.pad(v, ((0, 0), (1, 1), (0, 0)))
    return vp[:, :n] * w[0] + vp[:, 1:n + 1] * w[1] + vp[:, 2:] * w[2]


def s5_discretize(lam_re, lam_im, log_step, b_re, b_im):
    lr = lam_re.astype(F32)
    li = lam_im.astype(F32)
    dt = jnp.exp(log_step.astype(F32))[:, None]
    mag = jnp.exp(lr * dt)
    ar = mag * jnp.cos(li * dt)
    ai = mag * jnp.sin(li * dt)
    nr = ar - 1.0
    den = lr * lr + li * li
    fr = (nr * lr + ai * li) / den
    fi = (ai * lr - nr * li) / den
    br = b_re.astype(F32)
    bi = b_im.astype(F32)
    bbr = fr[..., None] * br - fi[..., None] * bi
    bbi = fr[..., None] * bi + fi[..., None] * br
    return ar, ai, bbr, bbi


def _complex_affine_combine(e1, e2):
    a1r, a1i, b1r, b1i = e1
    a2r, a2i, b2r, b2i = e2
    return (a1r * a2r - a1i * a2i,
            a1r * a2i + a1i * a2r,
            a2r * b1r - a2i * b1i + b2r,
            a2r * b1i + a2i * b1r + b2i)


def s5_scan(u_t, lbr, lbi, bbr, bbi, h0_re, h0_im):
    bu_re = jnp.einsum('lbgc,gpc->lbgp', u_t, bbr)
    bu_im = jnp.einsum('lbgc,gpc->lbgp', u_t, bbi)
    if h0_re is not None:
        bu_re = bu_re.at[0].add(lbr * h0_re - lbi * h0_im)
        bu_im = bu_im.at[0].add(lbr * h0_im + lbi * h0_re)
    n = u_t.shape[0]
    a_re = jnp.broadcast_to(lbr, (n, 1) + lbr.shape)
    a_im = jnp.broadcast_to(lbi, (n, 1) + lbi.shape)
    _, _, s_re, s_im = lax.associative_scan(_complex_affine_combine, (a_re, a_im, bu_re, bu_im), axis=0)
    return s_re, s_im


def s5_readout(s_re, s_im, c_re, c_im):
    return jnp.einsum('lbgp,gcp->lbgc', s_re, c_re) - jnp.einsum('lbgp,gcp->lbgc', s_im, c_im)


def to_time_groups(u):
    b_, n, _ = u.shape
    return jnp.transpose(u.reshape(b_, n, SSM_GROUPS, SSM_GROUP), (1, 0, 2, 3)).astype(F32)


def s5_glu(y_t, u_t, d_skip, glu_w, glu_b):
    n, b_, g_, cg = y_t.shape
    y = y_t + d_skip.astype(F32).reshape(g_, cg) * u_t
    y = jnp.transpose(y, (1, 0, 2, 3)).reshape(b_, n, g_ * cg)
    g = jax.nn.gelu(y)
    return g * jax.nn.sigmoid(g @ glu_w.astype(F32) + glu_b.astype(F32))


def s5_branch(u_ctx, u_lat, lam_re, lam_im, log_step, b_re, b_im, c_re, c_im, d_skip, glu_w, glu_b, need_ctx):
    uc = to_time_groups(u_ctx)
    ul = to_time_groups(u_lat)
    ys_lat = []
    ys_ctx = []
    for d in range(2):
        rev = d == 1
        lbr, lbi, bbr, bbi = s5_discretize(lam_re[d], lam_im[d], log_step[d], b_re[d], b_im[d])
        cr = c_re[d].astype(F32)
        ci = c_im[d].astype(F32)
        ucd = jnp.flip(uc, 0) if rev else uc
        uld = jnp.flip(ul, 0) if rev else ul
        sc_re, sc_im = s5_scan(ucd, lbr, lbi, bbr, bbi, None, None)
        sl_re, sl_im = s5_scan(uld, lbr, lbi, bbr, bbi, sc_re[-1], sc_im[-1])
        yl = s5_readout(sl_re, sl_im, cr, ci)
        ys_lat.append(jnp.flip(yl, 0) if rev else yl)
        if need_ctx:
            yc = s5_readout(sc_re, sc_im, cr, ci)
            ys_ctx.append(jnp.flip(yc, 0) if rev else yc)
    y_lat = s5_glu(ys_lat[0] + ys_lat[1], ul, d_skip, glu_w, glu_b)
    y_ctx = s5_glu(ys_ctx[0] + ys_ctx[1], uc, d_skip, glu_w, glu_b) if need_ctx else None
    return y_lat, y_ctx


def even_mixer(a_lat, a_ctx, w_in, conv_w, lam_re, lam_im, log_step, b_re, b_im, c_re, c_im,
               d_skip, glu_w, glu_b, w_out, need_ctx):
    cuts = [CONV_WIDTH, 2 * CONV_WIDTH, 3 * CONV_WIDTH, 4 * CONV_WIDTH, 4 * CONV_WIDTH + SSM_WIDTH]

    def conv_branch(xa, ba, ca, za):
        return ba * short_conv(ca * xa, conv_w) * jax.nn.silu(za)

    xa_l, ba_l, ca_l, za_l, us_l, zs_l = jnp.split(a_lat @ w_in, cuts, axis=-1)
    xa_c, ba_c, ca_c, za_c, us_c, zs_c = jnp.split(a_ctx @ w_in, cuts, axis=-1)
    ssm_l, ssm_c = s5_branch(us_c, us_l, lam_re, lam_im, log_step, b_re, b_im, c_re, c_im,
                             d_skip, glu_w, glu_b, need_ctx)
    mix_l = jnp.concatenate([conv_branch(xa_l, ba_l, ca_l, za_l).astype(F32),
                             ssm_l * jax.nn.silu(zs_l.astype(F32))], axis=-1)
    out_l = mix_l @ w_out.astype(F32)
    out_c = None
    if need_ctx:
        mix_c = jnp.concatenate([conv_branch(xa_c, ba_c, ca_c, za_c).astype(F32),
                                 ssm_c * jax.nn.silu(zs_c.astype(F32))], axis=-1)
        out_c = mix_c @ w_out.astype(F32)
    return out_l, out_c


def axial_rope_tables(rows):
    row = jnp.repeat(jnp.arange(rows, dtype=F32), GRID_W)
    col = jnp.tile(jnp.arange(GRID_W, dtype=F32), rows)
    n_freq = QK_ROPE // 4
    inv = jnp.power(ROPE_BASE, -jnp.arange(n_freq, dtype=F32) / n_freq)
    ang = jnp.concatenate([row[:, None] * inv, col[:, None] * inv], axis=-1)
    return jnp.cos(ang), jnp.sin(ang)


def apply_rope(x, cos, sin):
    half = QK_ROPE // 2
    x1 = x[..., :half]
    x2 = x[..., half:]
    return jnp.concatenate([x1 * cos - x2 * sin, x1 * sin + x2 * cos], axis=-1).astype(x.dtype)


def block_attention(q, k, v):
    b_, s_, h_, dq = q.shape
    nb = s_ // Q_BLOCK
    qb = jnp.transpose(q.reshape(b_, nb, Q_BLOCK, h_, dq), (1, 0, 2, 3, 4))
    scale = dq ** -0.5

    def one_block(qblk):
        s = jnp.einsum('bqhd,bkhd->bhqk', qblk, k, preferred_element_type=F32) * scale
        p = jax.nn.softmax(s, axis=-1)
        return jnp.einsum('bhqk,bkhv->bqhv', p.astype(v.dtype), v)

    out = lax.map(one_block, qb)
    return jnp.transpose(out, (1, 0, 2, 3, 4)).reshape(b_, s_, h_, v.shape[-1])


def mla_mixer(a_lat, a_ctx, w_in, q_a_norm, w_uq, kv_a_norm, w_ukv, q_norm, k_norm, w_out, rope, need_ctx):
    cuts = [Q_LORA, Q_LORA + KV_LORA, Q_LORA + KV_LORA + QK_ROPE]

    def project(h, rope_tab):
        b_, n, _ = h.shape
        cq, ckv, kr, gate = jnp.split(h @ w_in, cuts, axis=-1)
        q = (rms_norm(cq, q_a_norm) @ w_uq).reshape(b_, n, MLA_HEADS, QK_DIM)
        kv = (rms_norm(ckv, kv_a_norm) @ w_ukv).reshape(b_, n, MLA_HEADS, QK_NOPE + V_HEAD)
        q_nope = rms_norm(q[..., :QK_NOPE], q_norm[:QK_NOPE])
        q_rope = rms_norm(q[..., QK_NOPE:], q_norm[QK_NOPE:])
        k_nope = rms_norm(kv[..., :QK_NOPE], k_norm[:QK_NOPE])
        v = kv[..., QK_NOPE:]
        k_rope = rms_norm(kr, k_norm[QK_NOPE:])
        if rope_tab is not None:
            cos, sin = rope_tab
            q_rope = apply_rope(q_rope, cos[:, None, :], sin[:, None, :])
            k_rope = apply_rope(k_rope, cos, sin)
        q = jnp.concatenate([q_nope, q_rope], axis=-1)
        k = jnp.concatenate([k_nope, jnp.broadcast_to(k_rope[:, :, None, :], (b_, n, MLA_HEADS, QK_ROPE))], axis=-1)
        return q, k, v, gate

    q_l, k_l, v_l, g_l = project(a_lat, rope)
    q_c, k_c, v_c, g_c = project(a_ctx, None)
    b_, n, _ = a_lat.shape
    att_l = block_attention(q_l, jnp.concatenate([k_c, k_l], axis=1), jnp.concatenate([v_c, v_l], axis=1))
    out_l = (att_l.reshape(b_, n, MLA_MIX) * jax.nn.silu(g_l)) @ w_out
    out_c = None
    if need_ctx:
        att_c = block_attention(q_c, k_c, v_c)
        out_c = (att_c.reshape(b_, a_ctx.shape[1], MLA_MIX) * jax.nn.silu(g_c)) @ w_out
    return out_l, out_c


def setup_inputs(seed: int = 0) -> dict:
    key = jax.random.key(seed)
    ks = iter(jax.random.split(key, 32))
    nrm = lambda shape, s: jax.random.normal(next(ks), shape, F32) * s
    gain = lambda shape: 1.0 + 0.02 * jax.random.normal(next(ks), shape, F32)
    lam_im_base = jnp.pi * jnp.arange(SSM_STATE, dtype=F32)
    return {
        'x': nrm((BATCH, SEQ, D_MODEL), 1.0),
        'c': nrm((BATCH, D_MODEL), 1.0),
        'ctx': nrm((BATCH, CTX_LEN, D_MODEL), 1.0),
        'c_ctx': nrm((D_MODEL,), 1.0),
        'mod_w': nrm((DEPTH, D_MODEL, 3 * D_MODEL), 0.5 * D_MODEL ** -0.5),
        'mod_b': nrm((DEPTH, 3 * D_MODEL), 0.02),
        'norm_w': gain((DEPTH, D_MODEL)),
        'e_w_in': nrm((N_EVEN, D_MODEL, EVEN_IN), D_MODEL ** -0.5),
        'e_conv_w': nrm((N_EVEN, 3, CONV_WIDTH), 3 ** -0.5),
        'e_lam_re': -0.5 + nrm((N_EVEN, 2, SSM_GROUPS, SSM_STATE), 0.01),
        'e_lam_im': lam_im_base + nrm((N_EVEN, 2, SSM_GROUPS, SSM_STATE), 0.01),
        'e_log_step': jax.random.uniform(next(ks), (N_EVEN, 2, SSM_GROUPS), F32,
                                         minval=math.log(DT_MIN), maxval=math.log(DT_MAX)),
        'e_b_re': nrm((N_EVEN, 2, SSM_GROUPS, SSM_STATE, SSM_GROUP), (2 * SSM_GROUP) ** -0.5),
        'e_b_im': nrm((N_EVEN, 2, SSM_GROUPS, SSM_STATE, SSM_GROUP), (2 * SSM_GROUP) ** -0.5),
        'e_c_re': nrm((N_EVEN, 2, SSM_GROUPS, SSM_GROUP, SSM_STATE), SSM_STATE ** -0.5),
        'e_c_im': nrm((N_EVEN, 2, SSM_GROUPS, SSM_GROUP, SSM_STATE), SSM_STATE ** -0.5),
        'e_d': nrm((N_EVEN, SSM_WIDTH), 1.0),
        'e_glu_w': nrm((N_EVEN, SSM_WIDTH, SSM_WIDTH), SSM_WIDTH ** -0.5),
        'e_glu_b': nrm((N_EVEN, SSM_WIDTH), 0.02),
        'e_w_out': nrm((N_EVEN, EVEN_MIX, D_MODEL), EVEN_MIX ** -0.5),
        'o_w_in': nrm((N_ODD, D_MODEL, ODD_IN), D_MODEL ** -0.5),
        'o_q_a_norm': gain((N_ODD, Q_LORA)),
        'o_w_uq': nrm((N_ODD, Q_LORA, MLA_HEADS * QK_DIM), Q_LORA ** -0.5),
        'o_kv_a_norm': gain((N_ODD, KV_LORA)),
        'o_w_ukv': nrm((N_ODD, KV_LORA, MLA_HEADS * (QK_NOPE + V_HEAD)), KV_LORA ** -0.5),
        'o_q_norm': gain((N_ODD, QK_DIM)),
        'o_k_norm': gain((N_ODD, QK_DIM)),
        'o_w_out': nrm((N_ODD, MLA_MIX, D_MODEL), MLA_MIX ** -0.5),
    }


def reference(x, c, ctx, c_ctx, mod_w, mod_b, norm_w,
              e_w_in, e_conv_w, e_lam_re, e_lam_im, e_log_step, e_b_re, e_b_im, e_c_re, e_c_im,
              e_d, e_glu_w, e_glu_b, e_w_out,
              o_w_in, o_q_a_norm, o_w_uq, o_kv_a_norm, o_w_ukv, o_q_norm, o_k_norm, o_w_out):
    rows = x.shape[1] // GRID_W
    rope = axial_rope_tables(rows)
    h = x
    hc = ctx
    for i in range(DEPTH):
        need_ctx = i < DEPTH - 1
        shift, scale, gate = modulation(c, mod_w[i], mod_b[i])
        shift_c, scale_c, gate_c = modulation(c_ctx, mod_w[i], mod_b[i])
        a_lat = rms_norm(h, norm_w[i]) * (1 + scale[:, None]) + shift[:, None]
        a_ctx = rms_norm(hc, norm_w[i]) * (1 + scale_c) + shift_c
        if i % 2 == 0:
            j = i // 2
            out_l, out_c = even_mixer(a_lat, a_ctx, e_w_in[j], e_conv_w[j], e_lam_re[j], e_lam_im[j],
                                      e_log_step[j], e_b_re[j], e_b_im[j], e_c_re[j], e_c_im[j],
                                      e_d[j], e_glu_w[j], e_glu_b[j], e_w_out[j], need_ctx)
        else:
            j = i // 2
            out_l, out_c = mla_mixer(a_lat, a_ctx, o_w_in[j], o_q_a_norm[j], o_w_uq[j], o_kv_a_norm[j],
                                     o_w_ukv[j], o_q_norm[j], o_k_norm[j], o_w_out[j], rope, need_ctx)
        h = h + (gate[:, None] * out_l).astype(h.dtype)
        if need_ctx:
            hc = hc + (gate_c * out_c).astype(hc.dtype)
    return h
```

```python
import contextlib
import math
import numpy as np
import ml_dtypes
import concourse.bass as bass
import concourse.mybir as mybir
from concourse.bass_utils import run_bass_kernel_spmd

F32 = mybir.dt.float32
BF16 = mybir.dt.bfloat16
I32 = mybir.dt.int32
AF = mybir.ActivationFunctionType
ALU = mybir.AluOpType
AX = mybir.AxisListType

T, TL, TC, NT = 4352, 4096, 256, 34
D = 1024
EPS = 1e-6
NH = 16
PI = math.pi


class MK:
    NDMA = 8

    def __init__(self, nc, stack):
        self.nc = nc
        self.engs = {"pe": nc.tensor, "act": nc.scalar, "dve": nc.vector, "pool": nc.gpsimd, "sp": nc.sync}
        self.semobj = {}
        self.cnt = {}
        for e in ("pe", "act", "dve", "pool"):
            self.semobj[e] = stack.enter_context(nc.semaphore("s_" + e))
            self.cnt[e] = 0
        self.dq = {}
        for q in ("sp",):
            sems = []
            for i in range(self.NDMA):
                k = "d_%s%d" % (q, i)
                self.semobj[k] = stack.enter_context(nc.semaphore(k))
                self.cnt[k] = 0
                sems.append(k)
            self.dq[q] = [sems, 0]
        self.seen = {e: {} for e in self.engs}
        self.lastw = {}
        self.readers = {}
        self.same_engine_sync = {"act": True, "dve": True, "pool": True, "pe": False, "sp": False}
        self.ninst = 0

    def _wait(self, eng, sk, v):
        if sk == eng and not self.same_engine_sync[eng]:
            return
        if self.seen[eng].get(sk, 0) >= v:
            return
        self.engs[eng].wait_ge(self.semobj[sk], v)
        self.seen[eng][sk] = v

    def _deps(self, rd, wr):
        deps = {}
        for k in rd:
            t = self.lastw.get(k)
            if t is not None:
                deps[t[0]] = max(deps.get(t[0], 0), t[1])
        for k in wr:
            t = self.lastw.get(k)
            if t is not None:
                deps[t[0]] = max(deps.get(t[0], 0), t[1])
            for sk, v in self.readers.get(k, {}).items():
                deps[sk] = max(deps.get(sk, 0), v)
        return deps

    def _record(self, tok, rd, wr):
        for k in rd:
            r = self.readers.setdefault(k, {})
            r[tok[0]] = max(r.get(tok[0], 0), tok[1])
        for k in wr:
            self.lastw[k] = tok
            self.readers[k] = {}

    def op(self, eng, fn, rd=(), wr=()):
        for sk, v in self._deps(rd, wr).items():
            self._wait(eng, sk, v)
        inst = fn(self.engs[eng])
        self.cnt[eng] += 1
        inst.then_inc(self.semobj[eng], 1)
        self._record((eng, self.cnt[eng]), rd, wr)
        self.ninst += 1
        return inst

    def dma(self, out, in_, rd=(), wr=(), q="sp", **kw):
        sems, i = self.dq[q]
        sk = sems[i % len(sems)]
        self.dq[q][1] = i + 1
        if self.cnt[sk] > 0:
            self._wait(q, sk, self.cnt[sk])
        for dk, v in self._deps(rd, wr).items():
            self._wait(q, dk, v)
        inst = self.engs[q].dma_start(out=out, in_=in_, **kw)
        self.cnt[sk] += 16
        inst.then_inc(self.semobj[sk], 16)
        self._record((sk, self.cnt[sk]), rd, wr)
        self.ninst += 1
        return inst

    def wait_all(self, eng, keys):
        for k in keys:
            t = self.lastw.get(k)
            if t is not None:
                self._wait(eng, t[0], t[1])

    def barrier(self):
        for e in self.engs:
            for sk, v in self.cnt.items():
                if v > 0:
                    self._wait(e, sk, v)


class Ctx:
    pass


_SCOPE = [None]


def phase(g, name):
    if _SCOPE[0] is not None:
        _SCOPE[0].__exit__(None, None, None)
        _SCOPE[0] = None
    if name is not None:
        cm = g.nc.named_scope(name)
        cm.__enter__()
        _SCOPE[0] = cm


_UID = [0]


def uq(name):
    _UID[0] += 1
    return "%s_u%d" % (name, _UID[0])


def sb(g, name, shape, dt):
    return g.st.enter_context(g.nc.sbuf_tensor(uq(name), list(shape), dt))


def setup_consts(g):
    mk, nc = g.mk, g.nc
    g.identb = sb(g, "identb", [128, 128], BF16)
    g.identf = sb(g, "identf", [128, 128], F32)
    g.onesf = sb(g, "onesf", [64, 128], F32)
    g.epsb = sb(g, "epsb", [128, 1], F32)
    tmp = sb(g, "idtmp", [128, 128], F32)
    mk.op("pool", lambda e: e.memset(tmp[:], 1.0), wr=["idtmp"])
    mk.op("pool", lambda e: e.affine_select(g.identf[:], tmp[:], pattern=[[-1, 128]], compare_op=ALU.is_equal,
                                            fill=0.0, base=0, channel_multiplier=1), rd=["idtmp"], wr=["identf"])
    mk.op("pool", lambda e: e.tensor_copy(g.identb[:], g.identf[:]), rd=["identf"], wr=["identb"])
    mk.op("pool", lambda e: e.memset(g.onesf[:], 1.0), wr=["onesf"])
    mk.op("pool", lambda e: e.memset(g.epsb[:], EPS), wr=["epsb"])
    g.stg = [sb(g, "stg%d" % i, [128, 1024], F32) for i in range(4)]
    g.stg_i = 0


def load_w(g, dst, src, K, N, key, col0=0):
    mk = g.mk
    KC = (K + 127) // 128
    for kc in range(KC):
        rows = min(128, K - kc * 128)
        for n0 in range(0, N, 1024):
            w = min(1024, N - n0)
            s = g.stg[g.stg_i % 4]
            sk = "stg%d" % (g.stg_i % 4)
            g.stg_i += 1
            mk.dma(s[0:rows, 0:w], src[kc * 128:kc * 128 + rows, n0:n0 + w], wr=[sk])
            mk.op("pool", lambda e, s=s, kc=kc, n0=n0, w=w, rows=rows: e.tensor_copy(
                dst[0:rows, kc, col0 + n0:col0 + n0 + w], s[0:rows, 0:w]), rd=[sk], wr=[key])


def range_reduce(g, y, x, shift, shape, key, nm):
    mk = g.mk
    ki = sb(g, nm + "_ki", shape, I32)
    kf = sb(g, nm + "_kf", shape, F32)
    m = sb(g, nm + "_m", shape, F32)
    K = [key]
    mk.op("dve", lambda e: e.tensor_scalar(y, x, shift, None, ALU.add), rd=K, wr=K)
    mk.op("dve", lambda e: e.tensor_scalar(kf[:], y, 1.0 / (2 * PI), None, ALU.mult), rd=K, wr=K)
    mk.op("dve", lambda e: e.tensor_copy(ki[:], kf[:]), rd=K, wr=K)
    mk.op("dve", lambda e: e.tensor_copy(kf[:], ki[:]), rd=K, wr=K)
    mk.op("dve", lambda e: e.scalar_tensor_tensor(y, kf[:], -2 * PI, y, ALU.mult, ALU.add), rd=K, wr=K)
    mk.op("dve", lambda e: e.tensor_scalar(m[:], y, PI, None, ALU.is_gt), rd=K, wr=K)
    mk.op("dve", lambda e: e.scalar_tensor_tensor(y, m[:], -2 * PI, y, ALU.mult, ALU.add), rd=K, wr=K)
    mk.op("dve", lambda e: e.tensor_scalar(m[:], y, -PI, None, ALU.is_lt), rd=K, wr=K)
    mk.op("dve", lambda e: e.scalar_tensor_tensor(y, m[:], 2 * PI, y, ALU.mult, ALU.add), rd=K, wr=K)


def modulation(g, l, names):
    phase(g, "L%d_mod" % l)
    mk, nc = g.mk, g.nc
    with contextlib.ExitStack() as st:
        g2 = Ctx(); g2.st = st; g2.nc = nc; g2.mk = mk
        ccol = sb(g2, "ccol", [128, 16], F32)
        S33 = sb(g2, "S33", [128, 8, 64], F32)
        mw = [sb(g2, "mw%d" % i, [128, 3072], F32) for i in range(2)]
        mrow = sb(g2, "mrow", [64, 3072], F32)
        mb = sb(g2, "mb", [64, 3072], F32)
        nwb = sb(g2, "nwb", [128, 1024], F32)
        pm = st.enter_context(nc.psum_tensor(uq("pm"), [128, 3072], F32))
        pb = st.enter_context(nc.psum_tensor(uq("pb"), [128, 1024], F32))
        mk.dma(ccol[:, 0:8], g.c_d[0, :].rearrange("(k p) -> p k", p=128), wr=["ccol"], allow_slow_non_contiguous=True)
        mk.dma(ccol[:, 8:16], g.cctx_d[0, :].rearrange("(k p) -> p k", p=128), wr=["ccol"], allow_slow_non_contiguous=True)
        mk.dma(mb[0:1, :], g.modb_d[l:l + 1, :], wr=["mb"])
        mk.dma(mb[32:33, :], g.modb_d[l:l + 1, :], wr=["mb"])
        mk.dma(nwb[:], g.normw_d[l:l + 1, :].to_broadcast([128, 1024]), wr=["nwb"])
        mk.op("act", lambda e: e.activation(ccol[:], ccol[:], AF.Silu), rd=["ccol"], wr=["ccol"])
        mk.op("pool", lambda e: e.memset(S33[:], 0.0), wr=["S33"])
        mk.op("dve", lambda e: e.tensor_copy(S33[:, :, 0], ccol[:, 0:8]), rd=["ccol"], wr=["S33"])
        mk.op("dve", lambda e: e.tensor_copy(S33[:, :, 32], ccol[:, 8:16]), rd=["ccol"], wr=["S33"])
        for k in range(8):
            m = mw[k % 2]
            mkey = "mw%d" % (k % 2)
            mk.dma(m[:], g.modw_d[l, k * 128:(k + 1) * 128, :], wr=[mkey])
            for n in range(6):
                mk.op("pe", lambda e, m=m, n=n, k=k: e.matmul(pm[0:64, n * 512:(n + 1) * 512], lhsT=S33[:, k, :],
                                                             rhs=m[:, n * 512:(n + 1) * 512], start=(k == 0), stop=(k == 7)),
                      rd=[mkey, "S33"], wr=["pm"])
        for r in (0, 32):
            mk.op("dve", lambda e, r=r: e.tensor_tensor(mrow[r:r + 1, :], pm[r:r + 1, :], mb[r:r + 1, :], ALU.add),
                  rd=["pm", "mb"], wr=["mrow"])
        for ri, r in enumerate((0, 32)):
            for part in range(3):
                dst = names[ri * 3 + (1 if part == 0 else (0 if part == 1 else 2))]
                if dst is None:
                    continue
                for n in range(2):
                    c0 = part * 1024 + n * 512
                    mk.op("pe", lambda e, r=r, c0=c0, n=n: e.matmul(pb[:, n * 512:(n + 1) * 512], lhsT=g.onesf[r:r + 1, :],
                                                                   rhs=mrow[r:r + 1, c0:c0 + 512], start=True, stop=True),
                          rd=["mrow", "onesf"], wr=["pb%d" % n])
                dt, dk = dst
                if part == 1:
                    mk.op("dve", lambda e, dt=dt: e.scalar_tensor_tensor(dt[:], pb[:], 1.0, nwb[:], ALU.add, ALU.mult),
                          rd=["pb0", "pb1", "nwb"], wr=[dk])
                else:
                    mk.op("act", lambda e, dt=dt: e.activation(dt[:], pb[:], AF.Copy), rd=["pb0", "pb1"], wr=[dk])
        mk.barrier()


def norm_transpose(g, l, aT, weff_l, weff_c, lat_src, ctx_src, lat_key):
    mk, nc = g.mk, g.nc
    phase(g, "L%d_norm" % l)
    with contextlib.ExitStack() as st:
        g2 = Ctx(); g2.st = st; g2.nc = nc; g2.mk = mk
        xt = [sb(g2, "xt%d" % i, [128, 1024], F32) for i in range(4)]
        junk = sb(g2, "njunk", [128, 1024], BF16)
        ss = sb(g2, "nss", [128, NT], F32)
        rs = sb(g2, "nrs", [128, NT], F32)
        xn = [sb(g2, "xn%d" % i, [128, 1024], F32) for i in range(2)]
        xb = [sb(g2, "xb%d" % i, [128, 1024], BF16) for i in range(2)]
        pt = [st.enter_context(nc.psum_tensor(uq("npt%d" % i), [128, 1024], BF16)) for i in range(2)]
        for i in range(NT):
            b = i % 2
            x, xk = xt[i % 4], "xt%d" % (i % 4)
            if i < 2:
                mk.dma(x[:], ctx_src[i * 128:(i + 1) * 128, :], rd=["hc1_d"], wr=[xk])
                we, wk, sh, shk = weff_c
            else:
                mk.dma(x[:], lat_src[(i - 2) * 128:(i - 1) * 128, :], rd=[(lat_key, i - 2)], wr=[xk])
                we, wk, sh, shk = weff_l
            mk.op("act", lambda e, x=x, i=i: e.activation(junk[:], x[:], AF.Square, accum_out=ss[:, i:i + 1]),
                  rd=[xk], wr=["njunk", ("nss", i)])
            mk.op("act", lambda e, i=i: e.activation(rs[:, i:i + 1], ss[:, i:i + 1], AF.Sqrt, bias=g.epsb[:], scale=1.0 / D),
                  rd=[("nss", i), "epsb"], wr=[("nrs", i)])
            mk.op("dve", lambda e, i=i: e.reciprocal(rs[:, i:i + 1], rs[:, i:i + 1]), rd=[("nrs", i)], wr=[("nrs", i)])
            mk.op("dve", lambda e, x=x, i=i, b=b, we=we: e.scalar_tensor_tensor(xn[b][:], x[:], rs[:, i:i + 1], we[:], ALU.mult, ALU.mult),
                  rd=[xk, ("nrs", i), wk], wr=["xn%d" % b])
            mk.op("pool", lambda e, b=b, sh=sh: e.tensor_tensor(xb[b][:], xn[b][:], sh[:], ALU.add),
                  rd=["xn%d" % b, shk], wr=["xb%d" % b])
            for k in range(8):
                mk.op("pe", lambda e, b=b, k=k: e.transpose(pt[b][:, k * 128:(k + 1) * 128], xb[b][:, k * 128:(k + 1) * 128], g.identb[:]),
                      rd=["xb%d" % b, "identb"], wr=["npt%d" % b])
            mk.op("act", lambda e, b=b, i=i: e.activation(aT[:, :, i * 128:(i + 1) * 128],
                                                          pt[b][:, :].rearrange("p (k t) -> p k t", k=8), AF.Copy),
                  rd=["npt%d" % b], wr=[("aT", i)])
        mk.barrier()


def rope_tables(g):
    mk, nc = g.mk, g.nc
    g.cs = sb(g, "ropecs", [128, 32, 16], F32)
    g.sn = sb(g, "ropesn", [128, 32, 16], F32)
    with contextlib.ExitStack() as st:
        g2 = Ctx(); g2.st = st; g2.nc = nc; g2.mk = mk
        ri = sb(g2, "rri", [128, 32], I32)
        ci = sb(g2, "rci", [128, 1], I32)
        rf = sb(g2, "rrf", [128, 32], F32)
        cf = sb(g2, "rcf", [128, 1], F32)
        ang = sb(g2, "rang", [128, 32, 16], F32)
        red = sb(g2, "rred", [128, 32, 16], F32)
        K = ["rope"]
        for h in range(2):
            mk.op("pool", lambda e, h=h: e.iota(ri[h * 64:(h + 1) * 64, :], pattern=[[2, 32]], base=h, channel_multiplier=0), wr=K)
            mk.op("pool", lambda e, h=h: e.iota(ci[h * 64:(h + 1) * 64, :], pattern=[[0, 1]], base=0, channel_multiplier=1), wr=K)
        mk.op("dve", lambda e: e.tensor_copy(rf[:], ri[:]), rd=K, wr=K)
        mk.op("dve", lambda e: e.tensor_copy(cf[:], ci[:]), rd=K, wr=K)
        inv = (np.float32(10000.0) ** (-(np.arange(8, dtype=np.float32) / np.float32(8)))).astype(np.float32)
        for i in range(8):
            mk.op("dve", lambda e, i=i: e.tensor_scalar(ang[:, :, i], rf[:], float(inv[i]), None, ALU.mult), rd=K, wr=K)
            mk.op("dve", lambda e, i=i: e.tensor_scalar(ang[:, :, 8 + i], cf[:, 0:1].to_broadcast([128, 32]), float(inv[i]), None, ALU.mult), rd=K, wr=K)
        range_reduce(g2, red[:], ang[:], 0.0, [128, 32, 16], "rope", "rrs")
        mk.op("act", lambda e: e.activation(g.sn[:], red[:], AF.Sin), rd=K, wr=["ropesn"])
        range_reduce(g2, red[:], ang[:], PI / 2, [128, 32, 16], "rope", "rrc")
        mk.op("act", lambda e: e.activation(g.cs[:], red[:], AF.Sin), rd=K, wr=["ropecs"])
        mk.barrier()


def layer1(g):
    mk, nc = g.mk, g.nc
    SCALE = 96.0 ** -0.5
    with contextlib.ExitStack() as st1:
        g1 = Ctx(); g1.st = st1; g1.nc = nc; g1.mk = mk
        gate_l = sb(g1, "L1gate_l", [128, 1024], F32)
        with contextlib.ExitStack() as stB:
            gB = Ctx(); gB.st = stB; gB.nc = nc; gB.mk = mk
            cqnT = sb(gB, "cqnT", [128, 3, TL], BF16)
            ckvnT = sb(gB, "ckvnT", [128, 2, T], BF16)
            krrS = sb(gB, "krrS", [128, NT, 32], BF16)
            with contextlib.ExitStack() as st:
                g2 = Ctx(); g2.st = st; g2.nc = nc; g2.mk = mk
                bc = {}
                for nm in ("weff_l", "sh_l", "weff_c", "sh_c"):
                    bc[nm] = sb(g2, "L1" + nm, [128, 1024], F32)
                modulation(g, 1, [(bc["weff_l"], "weff_l"), (bc["sh_l"], "sh_l"), (gate_l, "gate_l"),
                                  (bc["weff_c"], "weff_c"), (bc["sh_c"], "sh_c"), None])
                aT = sb(g2, "a1T", [128, 8, T], BF16)
                norm_transpose(g, 1, aT, (bc["weff_l"], "weff_l", bc["sh_l"], "sh_l"), (bc["weff_c"], "weff_c", bc["sh_c"], "sh_c"),
                               g.y_d, g.hc1_d, "y")
                phase(g, "L1_B1")
                win = sb(g2, "win1", [128, 8, 1696], BF16)
                load_w(g, win, g.o_w_in, 1024, 1696, "win1")
                qan = sb(g2, "qan", [128, 384], F32)
                kvan = sb(g2, "kvan", [128, 256], F32)
                knr = sb(g2, "knr", [128, 32], F32)
                mk.dma(qan[:], g.o_q_a_norm[0:1, :].to_broadcast([128, 384]), wr=["qan"])
                mk.dma(kvan[:], g.o_kv_a_norm[0:1, :].to_broadcast([128, 256]), wr=["kvan"])
                mk.dma(knr[:], g.o_k_norm[0:1, 64:96].to_broadcast([128, 32]), wr=["knr"])
                inv3 = sb(g2, "inv3", [128, 3], F32)
                mk.op("pool", lambda e: e.memset(inv3[:, 0:1], 1.0 / 384), wr=["inv3"])
                mk.op("pool", lambda e: e.memset(inv3[:, 1:2], 1.0 / 256), wr=["inv3"])
                mk.op("pool", lambda e: e.memset(inv3[:, 2:3], 1.0 / 32), wr=["inv3"])
                pA_ = [st.enter_context(nc.psum_tensor(uq("pA"), [128, 1024], F32)) for _ in range(2)]
                pT_ = [st.enter_context(nc.psum_tensor(uq("pT"), [128, 1024], BF16)) for _ in range(2)]
                junk_ = [sb(g2, "bjunk", [128, 672], F32) for _ in range(2)]
                ss3_ = [sb(g2, "ss3", [128, 3], F32) for _ in range(2)]
                rs3_ = [sb(g2, "rs3", [128, 3], F32) for _ in range(2)]
                cqn_ = [sb(g2, "cqn", [128, 640], BF16) for _ in range(2)]
                krn_ = [sb(g2, "krn", [128, 32], F32) for _ in range(2)]
                krA_ = [sb(g2, "krA", [128, 32], F32) for _ in range(2)]
                krB_ = [sb(g2, "krB", [128, 32], F32) for _ in range(2)]
                for i in range(NT):
                    lat = i >= 2
                    lt = i - 2
                    tok = slice(i * 128, (i + 1) * 128)
                    b2 = i % 2
                    pA, pT, junk, ss3, rs3, cqn, krn, krA, krB = pA_[b2], pT_[b2], junk_[b2], ss3_[b2], rs3_[b2], cqn_[b2], krn_[b2], krA_[b2], krB_[b2]
                    for k in range(8):
                        mk.op("pe", lambda e, k=k: e.matmul(pA[:, 0:512], lhsT=aT[:, k, tok], rhs=win[:, k, 0:512], start=(k == 0), stop=(k == 7)),
                              rd=[("aT", i), "win1"], wr=[("pA0", b2)])
                        mk.op("pe", lambda e, k=k: e.matmul(pA[:, 512:672], lhsT=aT[:, k, tok], rhs=win[:, k, 512:672], start=(k == 0), stop=(k == 7)),
                              rd=[("aT", i), "win1"], wr=[("pA1", b2)])
                    for j, (a, b_) in enumerate(((0, 384), (384, 640), (640, 672))):
                        mk.op("act", lambda e, a=a, b_=b_, j=j: e.activation(junk[:, a:b_], pA[:, a:b_], AF.Square, accum_out=ss3[:, j:j + 1]),
                              rd=[("pA0", b2), ("pA1", b2)], wr=[("bjunk", b2), ("ss3", b2)])
                    mk.op("dve", lambda e: e.tensor_tensor(rs3[:], ss3[:], inv3[:], ALU.mult), rd=[("ss3", b2), "inv3"], wr=[("rs3", b2)])
                    mk.op("act", lambda e: e.activation(rs3[:], rs3[:], AF.Sqrt, bias=g.epsb[:]), rd=[("rs3", b2), "epsb"], wr=[("rs3", b2)])
                    mk.op("dve", lambda e: e.reciprocal(rs3[:], rs3[:]), rd=[("rs3", b2)], wr=[("rs3", b2)])
                    if lat:
                        mk.op("dve", lambda e: e.scalar_tensor_tensor(cqn[:, 0:384], pA[:, 0:384], rs3[:, 0:1], qan[:], ALU.mult, ALU.mult),
                              rd=[("pA0", b2), ("rs3", b2), "qan"], wr=[("cqn", b2)])
                    mk.op("dve", lambda e: e.scalar_tensor_tensor(cqn[:, 384:640], pA[:, 384:640], rs3[:, 1:2], kvan[:], ALU.mult, ALU.mult),
                          rd=[("pA0", b2), ("pA1", b2), ("rs3", b2), "kvan"], wr=[("cqn", b2)])
                    if lat:
                        mk.op("dve", lambda e: e.scalar_tensor_tensor(krn[:], pA[:, 640:672], rs3[:, 2:3], knr[:], ALU.mult, ALU.mult),
                              rd=[("pA1", b2), ("rs3", b2), "knr"], wr=[("krn", b2)])
                        krv = krn[:, :].rearrange("p (a b) -> p a b", a=2)
                        cosb = g.cs[:, lt, :].unsqueeze(1).to_broadcast([128, 2, 16])
                        sinb = g.sn[:, lt, :].unsqueeze(1).to_broadcast([128, 2, 16])
                        mk.op("dve", lambda e: e.tensor_tensor(krA[:, :].rearrange("p (a b) -> p a b", a=2), krv, cosb, ALU.mult),
                              rd=[("krn", b2), "ropecs"], wr=[("krA", b2)])
                        mk.op("dve", lambda e: e.tensor_tensor(krB[:, :].rearrange("p (a b) -> p a b", a=2), krv, sinb, ALU.mult),
                              rd=[("krn", b2), "ropesn"], wr=[("krB", b2)])
                        mk.op("dve", lambda e: e.tensor_tensor(krrS[:, i, 0:16], krA[:, 0:16], krB[:, 16:32], ALU.subtract), rd=[("krA", b2), ("krB", b2)], wr=[("krr", i)])
                        mk.op("dve", lambda e: e.tensor_tensor(krrS[:, i, 16:32], krB[:, 0:16], krA[:, 16:32], ALU.add), rd=[("krA", b2), ("krB", b2)], wr=[("krr", i)])
                    else:
                        mk.op("dve", lambda e: e.scalar_tensor_tensor(krrS[:, i, :], pA[:, 640:672], rs3[:, 2:3], knr[:], ALU.mult, ALU.mult),
                              rd=[("pA1", b2), ("rs3", b2), "knr"], wr=[("krr", i)])
                    c0 = 0 if lat else 3
                    for j in range(c0, 5):
                        mk.op("pe", lambda e, j=j: e.transpose(pT[:, j * 128:(j + 1) * 128], cqn[:, j * 128:(j + 1) * 128], g.identb[:]),
                              rd=[("cqn", b2), "identb"], wr=[("pT", b2)])
                    if lat:
                        mk.op("act", lambda e: e.activation(cqnT[:, :, lt * 128:(lt + 1) * 128], pT[:, 0:384].rearrange("p (k t) -> p k t", k=3), AF.Copy),
                              rd=[("pT", b2)], wr=[("cqnT", lt)])
                    mk.op("act", lambda e: e.activation(ckvnT[:, :, tok], pT[:, 384:640].rearrange("p (k t) -> p k t", k=2), AF.Copy),
                          rd=[("pT", b2)], wr=[("ckvnT", i)])
                mk.barrier()
                pA = pA_[0]
                phase(g, "L1_B3")
                sgs = [sb(g2, "sgs%d" % i, [128, 512], BF16) for i in range(2)]
                n3 = 0
                for m in range(8):
                    for blk in range(8):
                        pk = "pA%d" % (n3 % 2)
                        pg = pA[:, (n3 % 2) * 512:(n3 % 2 + 1) * 512]
                        for k in range(8):
                            mk.op("pe", lambda e, k=k, m=m, blk=blk, pg=pg: e.matmul(pg, lhsT=win[:, k, 672 + m * 128:672 + (m + 1) * 128],
                                                                                   rhs=aT[:, k, 256 + blk * 512:256 + (blk + 1) * 512],
                                                                                   start=(k == 0), stop=(k == 7)),
                                  rd=["win1"] + [("aT", 2 + blk * 4 + j) for j in range(4)], wr=[pk])
                        s_ = sgs[n3 % 2]
                        sk_ = "sgs%d" % (n3 % 2)
                        mk.op("act", lambda e, s_=s_, pg=pg: e.activation(s_[:], pg, AF.Silu), rd=[pk], wr=[sk_])
                        mk.dma(g.sg_d[m * 128:(m + 1) * 128, blk * 512:(blk + 1) * 512], s_[:], rd=[sk_], wr=["sg_d"])
                        n3 += 1
                mk.barrier()
            with contextlib.ExitStack() as st:
                g2 = Ctx(); g2.st = st; g2.nc = nc; g2.mk = mk
                phase(g, "L1_B2")
                wuq = sb(g2, "wuq", [128, 3, 1536], BF16)
                wukv = sb(g2, "wukv", [128, 2, 2048], BF16)
                load_w(g, wuq, g.o_w_uq, 384, 1536, "wuq")
                load_w(g, wukv, g.o_w_ukv, 256, 2048, "wukv")
                wq = sb(g2, "wqbc", [128, 16, 96], F32)
                wk = sb(g2, "wkbc", [128, 16, 64], F32)
                t96 = sb(g2, "t96", [128, 96], F32)
                t64 = sb(g2, "t64", [128, 64], F32)
                mk.dma(t96[:], g.o_q_norm[0:1, :].to_broadcast([128, 96]), wr=["t96"])
                mk.dma(t64[:], g.o_k_norm[0:1, 0:64].to_broadcast([128, 64]), wr=["t64"])
                mk.op("dve", lambda e: e.tensor_scalar(t96[:], t96[:], SCALE, None, ALU.mult), rd=["t96"], wr=["t96"])
                mk.op("dve", lambda e: e.tensor_copy(wq[:], t96[:, :].unsqueeze(1).to_broadcast([128, 16, 96])), rd=["t96"], wr=["wqbc"])
                mk.op("dve", lambda e: e.tensor_copy(wk[:], t64[:, :].unsqueeze(1).to_broadcast([128, 16, 64])), rd=["t64"], wr=["wkbc"])
                invq = sb(g2, "invq", [128, 32], F32)
                mk.op("pool", lambda e: e.memset(invq[:, 0:16], 1.0 / 64), wr=["invq"])
                mk.op("pool", lambda e: e.memset(invq[:, 16:32], 1.0 / 32), wr=["invq"])
                pT = st.enter_context(nc.psum_tensor(uq("pT2"), [128, 1024], BF16))
                pQ = st.enter_context(nc.psum_tensor(uq("pQ"), [128, 1536], F32))
                pKV = st.enter_context(nc.psum_tensor(uq("pKV"), [128, 2048], F32))
                sqk_t = sb(g2, "sqk_t", [128, 1024], F32)
                tk_t = sb(g2, "tk_t", [128, 1024], F32)
                sqq = sb(g2, "sqq", [128, 1536], F32)
                ssq = sb(g2, "ssq", [128, 32], F32)
                rsq = sb(g2, "rsq", [128, 32], F32)
                tq = sb(g2, "tq", [128, 1536], F32)
                qr = sb(g2, "qr", [128, 16, 32], F32)
                qA = sb(g2, "qA", [128, 16, 32], F32)
                qB = sb(g2, "qB", [128, 16, 32], F32)
                qf = sb(g2, "qf", [128, 16, 96], BF16)
                kf = sb(g2, "kf", [128, 16, 96], BF16)
                ssk = sb(g2, "ssk", [128, 16], F32)
                rsk = sb(g2, "rsk", [128, 16], F32)
                vst = [sb(g2, "vst%d" % i, [128, 1024], BF16) for i in range(2)]
                qst = [sb(g2, "qst%d" % i, [96, 16, 256], BF16) for i in range(2)]
                kst = [sb(g2, "kst%d" % i, [96, 16, 256], BF16) for i in range(2)]
                tqv = tq[:, :].rearrange("p (h d) -> p h d", h=16)
                sqv = sqq[:, :].rearrange("p (h d) -> p h d", h=16)
                qT_v = g.qT_d.rearrange("h d t -> d h t")
                kT_v = g.kT_d.rearrange("h d t -> d h t")
                def b2_proj(i):
                    lat = i >= 2
                    lt = i - 2
                    tok = slice(i * 128, (i + 1) * 128)
                    for n in range(4):
                        for k in range(2):
                            mk.op("pe", lambda e, n=n, k=k: e.matmul(pKV[:, n * 512:(n + 1) * 512], lhsT=ckvnT[:, k, tok],
                                                                     rhs=wukv[:, k, n * 512:(n + 1) * 512], start=(k == 0), stop=(k == 1)),
                                  rd=[("ckvnT", i), "wukv"], wr=["pKV"])
                    if lat:
                        for n in range(3):
                            for k in range(3):
                                mk.op("pe", lambda e, n=n, k=k: e.matmul(pQ[:, n * 512:(n + 1) * 512], lhsT=cqnT[:, k, lt * 128:(lt + 1) * 128],
                                                                         rhs=wuq[:, k, n * 512:(n + 1) * 512], start=(k == 0), stop=(k == 2)),
                                      rd=[("cqnT", lt), "wuq"], wr=["pQ"])
                def b2_chain(i):
                    lat = i >= 2
                    lt = i - 2
                    tok = slice(i * 128, (i + 1) * 128)
                    if lat:
                        mk.op("act", lambda e: e.activation(sqq[:], pQ[:, 0:1536], AF.Square), rd=["pQ"], wr=["sqq"])
                        mk.op("dve", lambda e: e.tensor_reduce(ssq[:, 0:16], sqv[:, :, 0:64], AX.X, ALU.add), rd=["sqq"], wr=["ssq"])
                        mk.op("dve", lambda e: e.tensor_reduce(ssq[:, 16:32], sqv[:, :, 64:96], AX.X, ALU.add), rd=["sqq"], wr=["ssq"])
                        mk.op("dve", lambda e: e.tensor_tensor(rsq[:], ssq[:], invq[:], ALU.mult), rd=["ssq", "invq"], wr=["rsq"])
                        mk.op("act", lambda e: e.activation(rsq[:], rsq[:], AF.Sqrt, bias=g.epsb[:]), rd=["rsq", "epsb"], wr=["rsq"])
                        mk.op("dve", lambda e: e.reciprocal(rsq[:], rsq[:]), rd=["rsq"], wr=["rsq"])
                        mk.op("dve", lambda e: e.tensor_tensor(tq[:], pQ[:, 0:1536], wq[:, :, :].rearrange("p h d -> p (h d)"), ALU.mult),
                              rd=["pQ", "wqbc"], wr=["tq"])
                        mk.op("pool", lambda e: e.tensor_tensor(qf[:, :, 0:64], tqv[:, :, 0:64], rsq[:, 0:16].unsqueeze(2).to_broadcast([128, 16, 64]), ALU.mult),
                              rd=["tq", "rsq"], wr=["qf"])
                        mk.op("dve", lambda e: e.tensor_tensor(qr[:], tqv[:, :, 64:96], rsq[:, 16:32].unsqueeze(2).to_broadcast([128, 16, 32]), ALU.mult),
                              rd=["tq", "rsq"], wr=["qr"])
                        qrv = qr[:, :, :].rearrange("p h (a b) -> p h a b", a=2)
                        cos4 = g.cs[:, lt, :].unsqueeze(1).unsqueeze(1).to_broadcast([128, 16, 2, 16])
                        sin4 = g.sn[:, lt, :].unsqueeze(1).unsqueeze(1).to_broadcast([128, 16, 2, 16])
                        mk.op("dve", lambda e: e.tensor_tensor(qA[:, :, :].rearrange("p h (a b) -> p h a b", a=2), qrv, cos4, ALU.mult),
                              rd=["qr", "ropecs"], wr=["qA"])
                        mk.op("pool", lambda e: e.tensor_tensor(qB[:, :, :].rearrange("p h (a b) -> p h a b", a=2), qrv, sin4, ALU.mult),
                              rd=["qr", "ropesn"], wr=["qB"])
                        mk.op("dve", lambda e: e.tensor_tensor(qf[:, :, 64:80], qA[:, :, 0:16], qB[:, :, 16:32], ALU.subtract), rd=["qA", "qB"], wr=["qf"])
                        mk.op("dve", lambda e: e.tensor_tensor(qf[:, :, 80:96], qB[:, :, 0:16], qA[:, :, 16:32], ALU.add), rd=["qA", "qB"], wr=["qf"])
                    kvv = pKV[:, :].rearrange("p (h d) -> p h d", h=16)
                    sqk = sqk_t[:, :].rearrange("p (h d) -> p h d", h=16)
                    tkv = tk_t[:, :].rearrange("p (h d) -> p h d", h=16)
                    mk.op("act", lambda e: e.activation(sqk, kvv[:, :, 0:64], AF.Square), rd=["pKV"], wr=["sqk_t"])
                    mk.op("dve", lambda e: e.tensor_reduce(ssk[:], sqk, AX.X, ALU.add), rd=["sqk_t"], wr=["ssk"])
                    mk.op("dve", lambda e: e.tensor_scalar(rsk[:], ssk[:], 1.0 / 64, None, ALU.mult), rd=["ssk"], wr=["rsk"])
                    mk.op("act", lambda e: e.activation(rsk[:], rsk[:], AF.Sqrt, bias=g.epsb[:]), rd=["rsk", "epsb"], wr=["rsk"])
                    mk.op("dve", lambda e: e.reciprocal(rsk[:], rsk[:]), rd=["rsk"], wr=["rsk"])
                    mk.op("dve", lambda e: e.tensor_tensor(tkv, kvv[:, :, 0:64], wk[:], ALU.mult), rd=["pKV", "wkbc"], wr=["tk_t"])
                    mk.op("pool", lambda e: e.tensor_tensor(kf[:, :, 0:64], tkv, rsk[:, :].unsqueeze(2).to_broadcast([128, 16, 64]), ALU.mult),
                          rd=["tk_t", "rsk"], wr=["kf"])
                    mk.op("pool", lambda e: e.tensor_copy(kf[:, :, 64:96], krrS[:, i, :].unsqueeze(1).to_broadcast([128, 16, 32])), rd=[("krr", i)], wr=["kf"])
                    vs = vst[i % 2]
                    vk = "vst%d" % (i % 2)
                    mk.op("act", lambda e, vs=vs: e.activation(vs[:, :].rearrange("p (h d) -> p h d", h=16), kvv[:, :, 64:128], AF.Copy),
                          rd=["pKV"], wr=[vk])
                    mk.dma(g.v_d[i * 128:(i + 1) * 128, :], vs[:], rd=[vk], wr=["v_d"])
                def b2_trans(i):
                    lat = i >= 2
                    lt = i - 2
                    tok = slice(i * 128, (i + 1) * 128)
                    if lat:
                        qs = qst[(lt // 2) % 2]
                        qsk = "qst%d" % ((lt // 2) % 2)
                        slot = lt % 2
                        for gi in range(2):
                            for hh in range(8):
                                mk.op("pe", lambda e, gi=gi, hh=hh: e.transpose(pT[0:96, hh * 128:(hh + 1) * 128], qf[:, gi * 8 + hh, :], g.identb[:]),
                                      rd=["qf", "identb"], wr=["pT"])
                            mk.op("act", lambda e, gi=gi, qs=qs, slot=slot: e.activation(
                                qs[:, gi * 8:(gi + 1) * 8, slot * 128:(slot + 1) * 128],
                                pT[0:96, :].rearrange("p (h t) -> p h t", h=8), AF.Copy), rd=["pT"], wr=[qsk])
                        if slot == 1:
                            t0 = (lt // 2) * 256
                            mk.dma(qT_v[:, :, t0:t0 + 256], qs[:, :, :], rd=[qsk], wr=["qT_d"])
                    ks = kst[(i // 2) % 2]
                    ksk = "kst%d" % ((i // 2) % 2)
                    kslot = i % 2
                    for gi in range(2):
                        for hh in range(8):
                            mk.op("pe", lambda e, gi=gi, hh=hh: e.transpose(pT[0:96, hh * 128:(hh + 1) * 128], kf[:, gi * 8 + hh, :], g.identb[:]),
                                  rd=["kf", "identb"], wr=["pT"])
                        mk.op("act", lambda e, gi=gi, ks=ks, kslot=kslot: e.activation(
                            ks[:, gi * 8:(gi + 1) * 8, kslot * 128:(kslot + 1) * 128],
                            pT[0:96, :].rearrange("p (h t) -> p h t", h=8), AF.Copy), rd=["pT"], wr=[ksk])
                    if kslot == 1:
                        t0 = (i // 2) * 256
                        mk.dma(kT_v[:, :, t0:t0 + 256], ks[:, :, :], rd=[ksk], wr=["kT_d"])
                b2_proj(0)
                for i in range(NT):
                    b2_chain(i)
                    if i + 1 < NT:
                        b2_proj(i + 1)
                    b2_trans(i)
                mk.barrier()
        with contextlib.ExitStack() as stC:
            gC = Ctx(); gC.st = stC; gC.nc = nc; gC.mk = mk
            phase(g, "L1_C")
            mixT = sb(gC, "mixT", [128, 8, TL], BF16)
            with contextlib.ExitStack() as st:
                g2 = Ctx(); g2.st = st; g2.nc = nc; g2.mk = mk
                qTh = [sb(g2, "qTh%d" % i, [96, TL], BF16) for i in range(2)]
                kTh = [sb(g2, "kTh%d" % i, [96, T], BF16) for i in range(2)]
                vh = [sb(g2, "vh%d" % i, [128, NT, 128], BF16) for i in range(2)]
                sgh = sb(g2, "sgh", [128, TL], BF16)
                NPX = 4
                pex = [sb(g2, "pex%d" % i, [128, 1024], BF16) for i in range(NPX)]
                rn = sb(g2, "rn", [128, 512], F32)
                at = sb(g2, "at", [128, 512], F32)
                pS = [st.enter_context(nc.psum_tensor(uq("pS%d" % i), [128, 1024], F32)) for i in range(3)]
                pO = [st.enter_context(nc.psum_tensor(uq("pO%d" % i), [128, 512], F32)) for i in range(2)]
                mk.op("pool", lambda e: e.memset(vh[0][:, :, 64:128], 1.0), wr=["vh0"])
                mk.op("pool", lambda e: e.memset(vh[1][:, :, 0:64], 1.0), wr=["vh1"])
                v_v = g.v_d.rearrange("(t p) d -> p t d", p=128)
                LAG = 2

                def load_head(h):
                    b = h % 2
                    lo, hi = (0, 64) if b == 0 else (64, 128)
                    mk.dma(qTh[b][:], g.qT_d[h], rd=["qT_d"], wr=["qTh%d" % b])
                    mk.dma(kTh[b][:], g.kT_d[h], rd=["kT_d"], wr=["kTh%d" % b])
                    mk.dma(vh[b][:, :, lo:hi], v_v[:, :, h * 64:(h + 1) * 64], rd=["v_d"], wr=["vh%d" % b])
                    mk.dma(sgh[lo:hi, :], g.sg_d[h * 64:(h + 1) * 64, :], rd=["sg_d"], wr=["sgh%d" % b])

                units = [(h, qb, kg) for h in range(NH) for qb in range(8) for kg in range(17)]

                def emit_qk_exp(u, idx):
                    h, qb, kg = u
                    b = h % 2
                    ps = pS[idx % 3]
                    psk = "pS%d" % (idx % 3)
                    qsl = slice(qb * 512, (qb + 1) * 512)
                    for j in range(2):
                        kt = kg * 2 + j
                        mk.op("pe", lambda e, ps=ps, j=j, kt=kt, b=b, qsl=qsl: e.matmul(ps[:, j * 512:(j + 1) * 512], lhsT=kTh[b][:, kt * 128:(kt + 1) * 128],
                                                                                       rhs=qTh[b][:, qsl], start=True, stop=True),
                              rd=["kTh%d" % b, "qTh%d" % b], wr=[psk])
                    px = pex[idx % NPX]
                    pxk = "pex%d" % (idx % NPX)
                    mk.op("act", lambda e, px=px, ps=ps: e.activation(px[:], ps[:], AF.Exp), rd=[psk], wr=[pxk])

                def emit_pv(u, idx):
                    h, qb, kg = u
                    b = h % 2
                    lo, hi = (0, 64) if b == 0 else (64, 128)
                    dlo, dhi = (64, 128) if b == 0 else (0, 64)
                    gq = h * 8 + qb
                    po = pO[gq % 2]
                    pok = "pO%d" % (gq % 2)
                    px = pex[idx % NPX]
                    pxk = "pex%d" % (idx % NPX)
                    qsl = slice(qb * 512, (qb + 1) * 512)
                    for j in range(2):
                        kt = kg * 2 + j
                        mk.op("pe", lambda e, px=px, j=j, kt=kt, b=b, po=po: e.matmul(po[:], lhsT=vh[b][:, kt, :], rhs=px[:, j * 512:(j + 1) * 512],
                                                                                     start=(kt == 0), stop=(kt == NT - 1)),
                              rd=[pxk, "vh%d" % b], wr=[pok])
                    if kg == 16:
                        mk.op("dve", lambda e: e.reciprocal(rn[lo:hi, :], po[dlo:dhi, :]), rd=[pok], wr=["rn"])
                        mk.op("dve", lambda e: e.tensor_tensor(at[lo:hi, :], po[lo:hi, :], rn[lo:hi, :], ALU.mult), rd=[pok, "rn"], wr=["at"])
                        mk.op("pool", lambda e: e.tensor_tensor(mixT[lo:hi, h // 2, qsl], at[lo:hi, :], sgh[lo:hi, qsl], ALU.mult),
                              rd=["at", "sgh%d" % b], wr=[("mixT", qb)])

                load_head(0)
                load_head(1)
                for idx, u in enumerate(units):
                    emit_qk_exp(u, idx)
                    if idx >= LAG:
                        up = units[idx - LAG]
                        emit_pv(up, idx - LAG)
                        if up[1] == 7 and up[2] == 16 and up[0] + 2 < NH:
                            load_head(up[0] + 2)
                for idx in range(len(units) - LAG, len(units)):
                    emit_pv(units[idx], idx)
                mk.barrier()
            with contextlib.ExitStack() as st:
                g2 = Ctx(); g2.st = st; g2.nc = nc; g2.mk = mk
                phase(g, "L1_D")
                wout = sb(g2, "wout1", [128, 8, 1024], BF16)
                load_w(g, wout, g.o_w_out, 1024, 1024, "wout1")
                ht = [sb(g2, "ht%d" % i, [128, 1024], F32) for i in range(4)]
                ot = [sb(g2, "ot%d" % i, [128, 1024], F32) for i in range(4)]
                pD = [st.enter_context(nc.psum_tensor(uq("pD%d" % i), [128, 1024], F32)) for i in range(2)]
                for lt in range(32):
                    b = lt % 2
                    tok = slice(lt * 128, (lt + 1) * 128)
                    hb = ht[lt % 4]
                    hbk = "ht%d" % (lt % 4)
                    mk.dma(hb[:], g.y_d[tok, :], rd=[("y", lt)], wr=[hbk])
                    for n in range(2):
                        for k in range(8):
                            mk.op("pe", lambda e, n=n, k=k: e.matmul(pD[b][:, n * 512:(n + 1) * 512], lhsT=mixT[:, k, tok], rhs=wout[:, k, n * 512:(n + 1) * 512],
                                                                     start=(k == 0), stop=(k == 7)),
                                  rd=[("mixT", lt // 4), "wout1"], wr=["pD%d" % b])
                    mk.op("dve", lambda e: e.tensor_tensor(ot[lt % 4][:], pD[b][:], gate_l[:], ALU.mult), rd=["pD%d" % b, "gate_l"], wr=["ot%d" % (lt % 4)])
                    mk.op("pool", lambda e, hb=hb: e.tensor_tensor(ot[lt % 4][:], ot[lt % 4][:], hb[:], ALU.add), rd=["ot%d" % (lt % 4), hbk], wr=["ot%d" % (lt % 4)])
                    mk.dma(g.y_d[tok, :], ot[lt % 4][:], rd=["ot%d" % (lt % 4)], wr=[("y", lt)])
                mk.barrier()


def layer0(g):
    mk, nc = g.mk, g.nc
    NS = 544
    with contextlib.ExitStack() as st0:
        g0 = Ctx(); g0.st = st0; g0.nc = nc; g0.mk = mk
        gate_l = sb(g0, "L0gate_l", [128, 1024], F32)
        gate_c = sb(g0, "L0gate_c", [128, 1024], F32)
        with contextlib.ExitStack() as stA:
            gA = Ctx(); gA.st = stA; gA.nc = nc; gA.mk = mk
            aT = sb(gA, "a0T", [128, 8, T], BF16)
            with contextlib.ExitStack() as st:
                g2 = Ctx(); g2.st = st; g2.nc = nc; g2.mk = mk
                bc = {}
                for nm in ("weff_l", "sh_l", "weff_c", "sh_c"):
                    bc[nm] = sb(g2, "L0" + nm, [128, 1024], F32)
                modulation(g, 0, [(bc["weff_l"], "weff_l"), (bc["sh_l"], "sh_l"), (gate_l, "gate_l0"),
                                  (bc["weff_c"], "weff_c"), (bc["sh_c"], "sh_c"), (gate_c, "gate_c0")])
                norm_transpose(g, 0, aT, (bc["weff_l"], "weff_l", bc["sh_l"], "sh_l"), (bc["weff_c"], "weff_c", bc["sh_c"], "sh_c"),
                               g.x_d, g.ctx_d, "xin")
            blocks = [(0, 256)] + [(256 + 512 * b, 512) for b in range(8)]
            def atk(t0, w):
                return [("aT", i) for i in range(t0 // 128, (t0 + w) // 128)]
            with contextlib.ExitStack() as st:
                g2 = Ctx(); g2.st = st; g2.nc = nc; g2.mk = mk
                phase(g, "L0_A1")
                wA = sb(g2, "wA", [128, 8, 1024], BF16)
                load_w(g, wA, g.e_w_in[:, 2048:3072], 1024, 1024, "wA")
                X = sb(g2, "X", [128, 32, 8, 16], BF16)
                ust = [sb(g2, "ust%d" % i, [128, 32, 128], BF16) for i in range(2)]
                J = sb(g2, "J", [128, 128], BF16)
                jt = sb(g2, "jt", [128, 128], F32)
                mk.op("pool", lambda e: e.memset(jt[:], 1.0), wr=["jt"])
                mk.op("pool", lambda e: e.affine_select(jt[:], jt[:], pattern=[[1, 128]], compare_op=ALU.is_equal, fill=0.0, base=-127, channel_multiplier=1),
                      rd=["jt"], wr=["jt"])
                mk.op("pool", lambda e: e.tensor_copy(J[:], jt[:]), rd=["jt"], wr=["J"])
                pX = [st.enter_context(nc.psum_tensor(uq("pX%d" % i), [128, 512], F32)) for i in range(2)]
                pU = [st.enter_context(nc.psum_tensor(uq("pU%d" % i), [128, 512], F32)) for i in range(2)]
                sgs = [sb(g2, "zsg%d" % i, [128, 512], BF16) for i in range(2)]
                Uv = g.U_d.rearrange("p (g n) -> p g n", g=32)
                Ubv = g.Ub_d.rearrange("p (g n) -> p g n", g=32)
                ublocks = [(0, 32, 0, 0)] + [(256 + 1024 * b, 128, 32 + 128 * b, 416 - 128 * b) for b in range(4)]
                nx = 0
                nu = 0
                for (tb, nb, nbase, rbase) in ublocks:
                    for j in range(8):
                        px = pX[nx % 2]
                        pxk = "pX%d" % (nx % 2)
                        nx += 1
                        for k in range(8):
                            mk.op("pe", lambda e, k=k, j=j, px=px: e.matmul(px[0:nb, :], lhsT=aT[:, k, tb + j:tb + 8 * nb:8], rhs=wA[:, k, 0:512],
                                                                           start=(k == 0), stop=(k == 7)),
                                  rd=atk(tb, 8 * nb) + ["wA"], wr=[pxk])
                        eng = "act" if j % 2 == 0 else "dve"
                        if eng == "act":
                            mk.op("act", lambda e, j=j, px=px: e.activation(X[0:nb, :, j, :], px[0:nb, :].rearrange("p (g c) -> p g c", g=32), AF.Copy), rd=[pxk], wr=["X"])
                        else:
                            mk.op("dve", lambda e, j=j, px=px: e.tensor_copy(X[0:nb, :, j, :], px[0:nb, :].rearrange("p (g c) -> p g c", g=32)), rd=[pxk], wr=["X"])
                    for rev in range(2):
                        us_ = ust[rev]
                        usk = "ust%d" % rev
                        rhs = g.identb[0:nb, 0:nb] if rev == 0 else J[0:nb, 128 - nb:128]
                        for g4 in range(8):
                            pu = pU[nu % 2]
                            puk = "pU%d" % (nu % 2)
                            nu += 1
                            for gg in range(4):
                                gi = g4 * 4 + gg
                                mk.op("pe", lambda e, gi=gi, gg=gg, pu=pu, rhs=rhs: e.matmul(pu[:, gg * nb:(gg + 1) * nb], lhsT=X[0:nb, gi, :, :].rearrange("p j c -> p (j c)"),
                                                                                          rhs=rhs, start=True, stop=True),
                                      rd=["X", "identb", "J"], wr=[puk])
                            eng = "act" if g4 % 2 == 0 else "dve"
                            src = pu[:, 0:4 * nb].rearrange("p (a n) -> p a n", a=4)
                            if eng == "act":
                                mk.op("act", lambda e, us_=us_, g4=g4, src=src: e.activation(us_[:, g4 * 4:(g4 + 1) * 4, 0:nb], src, AF.Copy), rd=[puk], wr=[usk])
                            else:
                                mk.op("dve", lambda e, us_=us_, g4=g4, src=src: e.tensor_copy(us_[:, g4 * 4:(g4 + 1) * 4, 0:nb], src), rd=[puk], wr=[usk])
                        if rev == 0:
                            mk.dma(Uv[:, :, nbase:nbase + nb], us_[:, :, 0:nb], rd=[usk], wr=["U_d"])
                        else:
                            mk.dma(Ubv[:, :, rbase:rbase + nb], us_[:, :, 0:nb], rd=[usk], wr=["Ub_d"])
                n3 = 0
                for m in range(4):
                    for (t0, w) in blocks:
                        px = pX[n3 % 2]
                        pxk = "pX%d" % (n3 % 2)
                        for k in range(8):
                            mk.op("pe", lambda e, k=k, m=m, px=px, t0=t0, w=w: e.matmul(px[:, 0:w], lhsT=wA[:, k, 512 + m * 128:512 + (m + 1) * 128],
                                                                                     rhs=aT[:, k, t0:t0 + w], start=(k == 0), stop=(k == 7)),
                                  rd=atk(t0, w) + ["wA"], wr=[pxk])
                        s_ = sgs[n3 % 2]
                        sk_ = "zsg%d" % (n3 % 2)
                        mk.op("act", lambda e, s_=s_, px=px, w=w: e.activation(s_[:, 0:w], px[:, 0:w], AF.Silu), rd=[pxk], wr=[sk_])
                        mk.dma(g.szs_d[m * 128:(m + 1) * 128, t0:t0 + w], s_[:, 0:w], rd=[sk_], wr=["szs_d"])
                        n3 += 1
                mk.barrier()
            with contextlib.ExitStack() as st:
                g2 = Ctx(); g2.st = st; g2.nc = nc; g2.mk = mk
                phase(g, "L0_A2")
                wB = sb(g2, "wB", [128, 8, 2048], BF16)
                load_w(g, wB, g.e_w_in[:, 0:2048], 1024, 2048, "wB")
                cw = sb(g2, "cw", [128, 4, 3], F32)
                for j3 in range(3):
                    mk.dma(cw[:, :, j3], g.e_conv_w[j3, :].rearrange("(c p) -> p c", p=128), wr=["cw"], allow_slow_non_contiguous=True)
                vbuf = sb(g2, "vbuf", [128, T + 4], F32)
                cv = sb(g2, "cv", [128, T], F32)
                xs = sb(g2, "xs", [128, 512], F32)
                sz = sb(g2, "sz", [128, 512], F32)
                t1 = sb(g2, "t1", [128, 512], F32)
                mst = [sb(g2, "mst%d" % i, [128, 512], BF16) for i in range(2)]
                pc = [st.enter_context(nc.psum_tensor(uq("pc%d" % i), [128, 512], F32)) for i in range(4)]
                mk.op("pool", lambda e: e.memset(vbuf[:], 0.0), wr=["vbuf"])
                def pos(t0):
                    return t0 + 1 if t0 < 256 else t0 + 3
                nm = 0
                for fc in range(4):
                    for bi, (t0, w) in enumerate(blocks):
                        pa, pb_ = pc[(bi % 2) * 2], pc[(bi % 2) * 2 + 1]
                        pak, pbk = "pc%d" % ((bi % 2) * 2), "pc%d" % ((bi % 2) * 2 + 1)
                        for k in range(8):
                            mk.op("pe", lambda e, k=k, pa=pa, t0=t0, w=w: e.matmul(pa[:, 0:w], lhsT=wB[:, k, fc * 128:(fc + 1) * 128], rhs=aT[:, k, t0:t0 + w],
                                                                                 start=(k == 0), stop=(k == 7)), rd=atk(t0, w) + ["wB"], wr=[pak])
                        for k in range(8):
                            mk.op("pe", lambda e, k=k, pb_=pb_, t0=t0, w=w: e.matmul(pb_[:, 0:w], lhsT=wB[:, k, 1024 + fc * 128:1024 + (fc + 1) * 128], rhs=aT[:, k, t0:t0 + w],
                                                                                   start=(k == 0), stop=(k == 7)), rd=atk(t0, w) + ["wB"], wr=[pbk])
                        mk.op("act", lambda e, pa=pa, w=w: e.activation(xs[:, 0:w], pa[:, 0:w], AF.Copy), rd=[pak], wr=["xs"])
                        p0 = pos(t0)
                        mk.op("dve", lambda e, pb_=pb_, w=w, p0=p0: e.tensor_tensor(vbuf[:, p0:p0 + w], pb_[:, 0:w], xs[:, 0:w], ALU.mult),
                              rd=[pbk, "xs"], wr=["vbuf"])
                    for (a, b_) in ((0, 256), (256, T)):
                        pa0 = pos(a)
                        L = b_ - a
                        mk.op("dve", lambda e, a=a, b_=b_, pa0=pa0, L=L: e.tensor_scalar(cv[:, a:b_], vbuf[:, pa0:pa0 + L], cw[:, fc, 1:2], None, ALU.mult),
                              rd=["vbuf", "cw"], wr=["cv"])
                        mk.op("dve", lambda e, a=a, b_=b_, pa0=pa0, L=L: e.scalar_tensor_tensor(cv[:, a:b_], vbuf[:, pa0 - 1:pa0 - 1 + L], cw[:, fc, 0:1], cv[:, a:b_], ALU.mult, ALU.add),
                              rd=["vbuf", "cw", "cv"], wr=["cv"])
                        mk.op("dve", lambda e, a=a, b_=b_, pa0=pa0, L=L: e.scalar_tensor_tensor(cv[:, a:b_], vbuf[:, pa0 + 1:pa0 + 1 + L], cw[:, fc, 2:3], cv[:, a:b_], ALU.mult, ALU.add),
                              rd=["vbuf", "cw", "cv"], wr=["cv"])
                    for bi, (t0, w) in enumerate(blocks):
                        pa, pb_ = pc[(bi % 2) * 2], pc[(bi % 2) * 2 + 1]
                        pak, pbk = "pc%d" % ((bi % 2) * 2), "pc%d" % ((bi % 2) * 2 + 1)
                        for k in range(8):
                            mk.op("pe", lambda e, k=k, pa=pa, t0=t0, w=w: e.matmul(pa[:, 0:w], lhsT=wB[:, k, 512 + fc * 128:512 + (fc + 1) * 128], rhs=aT[:, k, t0:t0 + w],
                                                                                 start=(k == 0), stop=(k == 7)), rd=atk(t0, w) + ["wB"], wr=[pak])
                        for k in range(8):
                            mk.op("pe", lambda e, k=k, pb_=pb_, t0=t0, w=w: e.matmul(pb_[:, 0:w], lhsT=wB[:, k, 1536 + fc * 128:1536 + (fc + 1) * 128], rhs=aT[:, k, t0:t0 + w],
                                                                                   start=(k == 0), stop=(k == 7)), rd=atk(t0, w) + ["wB"], wr=[pbk])
                        mk.op("act", lambda e, pb_=pb_, w=w: e.activation(sz[:, 0:w], pb_[:, 0:w], AF.Silu), rd=[pbk], wr=["sz"])
                        mk.op("dve", lambda e, pa=pa, w=w, t0=t0: e.tensor_tensor(t1[:, 0:w], pa[:, 0:w], cv[:, t0:t0 + w], ALU.mult), rd=[pak, "cv"], wr=["t1"])
                        ms = mst[nm % 2]
                        msk = "mst%d" % (nm % 2)
                        nm += 1
                        mk.op("pool", lambda e, ms=ms, w=w: e.tensor_tensor(ms[:, 0:w], t1[:, 0:w], sz[:, 0:w], ALU.mult), rd=["t1", "sz"], wr=[msk])
                        mk.dma(g.mix_d[fc * 128:(fc + 1) * 128, t0:t0 + w], ms[:, 0:w], rd=[msk], wr=["mix_d"])
                mk.barrier()
        if STOP == "A":
            return
        with contextlib.ExitStack() as stS:
            gS = Ctx(); gS.st = stS; gS.nc = nc; gS.mk = mk
            gT = sb(gS, "gT", [128, 4, T], BF16)
            s5_phase(g, gS, gT)
            if STOP in ("S1", "S2", "S3", "S4", "S3a", "S3b"):
                return
            glu_phase(g, gT)
        out_proj0(g, gate_l, gate_c)


def cmul(mk, eng, outr, outi, ar, ai, br, bi, t1, t2, key_r, key_w, neg_im=False):
    K = list(key_r)
    mk.op(eng, lambda e: e.tensor_tensor(t1, ar, br, ALU.mult), rd=K, wr=["cm_t1"])
    mk.op(eng, lambda e: e.tensor_tensor(t2, ai, bi, ALU.mult), rd=K, wr=["cm_t2"])
    mk.op(eng, lambda e: e.tensor_tensor(outr, t1, t2, ALU.subtract), rd=["cm_t1", "cm_t2"], wr=key_w)
    mk.op(eng, lambda e: e.tensor_tensor(t1, ar, bi, ALU.mult), rd=K + key_w, wr=["cm_t1"])
    mk.op(eng, lambda e: e.tensor_tensor(t2, ai, br, ALU.mult), rd=K + key_w, wr=["cm_t2"])
    if neg_im:
        mk.op(eng, lambda e: e.scalar_tensor_tensor(outi, t1, -1.0, t2, ALU.mult, ALU.subtract), rd=["cm_t1", "cm_t2"], wr=key_w)
    else:
        mk.op(eng, lambda e: e.tensor_tensor(outi, t1, t2, ALU.add), rd=["cm_t1", "cm_t2"], wr=key_w)


def s5_phase(g, gS, gT):
    mk, nc = g.mk, g.nc
    NS = 544
    st = gS.st
    phase(g, "L0_s5setup")
    sm = lambda name, shape, dt=F32: sb(gS, name, shape, dt)
    lr = sm("s5lr", [128, 32]); li = sm("s5li", [128, 32]); ls = sm("s5ls", [128, 32])
    ar = sm("s5ar", [128, 32]); ai = sm("s5ai", [128, 32])
    fr = sm("s5fr", [128, 32]); fi = sm("s5fi", [128, 32])
    ta = sm("s5ta", [128, 32]); tb = sm("s5tb", [128, 32]); tc_ = sm("s5tc", [128, 32]); td = sm("s5td", [128, 32])
    pwr = sm("s5pwr", [128, 16, 32]); pwi = sm("s5pwi", [128, 16, 32])
    ipr = sm("s5ipr", [128, 8, 32]); ipi = sm("s5ipi", [128, 8, 32])
    PBr = sm("s5PBr", [128, 8, 32]); PBi = sm("s5PBi", [128, 8, 32])
    PCr = sm("s5PCr", [128, 8, 32]); PCi = sm("s5PCi", [128, 8, 32])
    PYr = sm("s5PYr", [128, 8, 32]); PYi = sm("s5PYi", [128, 8, 32])
    Acoef = sm("s5Acoef", [128, 2, 2, 32])
    Bre = sm("s5Bre", [128, 32, 16]); Bim = sm("s5Bim", [128, 32, 16])
    bbr = sm("s5bbr", [128, 32, 16]); bbi = sm("s5bbi", [128, 32, 16])
    Cre = sm("s5Cre", [128, 32, 16]); Cim = sm("s5Cim", [128, 32, 16])
    Cld = sm("s5Cld", [128, 8, 64])
    big1 = sm("s5big1", [128, 32, 16]); big2 = sm("s5big2", [128, 32, 16])
    dcol = sm("s5dcol", [128, 32])
    maskF = sm("s5mF", [128, 128]); maskB = sm("s5mB", [128, 128])
    K = ["s5d"]
    for d in range(2):
        rows = slice(d * 64, (d + 1) * 64)
        mk.dma(lr[rows, :], g.e_lam_re[d].rearrange("g p -> p g"), wr=K, allow_slow_non_contiguous=True)
        mk.dma(li[rows, :], g.e_lam_im[d].rearrange("g p -> p g"), wr=K, allow_slow_non_contiguous=True)
        mk.dma(ls[rows, :], g.e_log_step[d:d + 1, :].to_broadcast([64, 32]), wr=K)
        mk.dma(Bre[rows, :, :], g.e_b_re[d].rearrange("g p c -> p g c"), wr=K)
        mk.dma(Bim[rows, :, :], g.e_b_im[d].rearrange("g p c -> p g c"), wr=K)
    for j in range(8):
        mk.dma(dcol[j * 16:(j + 1) * 16, :], g.e_d[0, :].rearrange("(g c) -> c g", c=16), wr=K, allow_slow_non_contiguous=True)
    V = lambda f: mk.op("dve", f, rd=K, wr=K)
    A_ = lambda f: mk.op("act", f, rd=K, wr=K)
    A_(lambda e: e.activation(ls[:], ls[:], AF.Exp))
    V(lambda e: e.tensor_tensor(ta[:], lr[:], ls[:], ALU.mult))
    A_(lambda e: e.activation(ta[:], ta[:], AF.Exp))
    V(lambda e: e.tensor_tensor(tb[:], li[:], ls[:], ALU.mult))
    range_reduce(gS, tc_[:], tb[:], 0.0, [128, 32], "s5d", "s5rr1")
    A_(lambda e: e.activation(ai[:], tc_[:], AF.Sin))
    range_reduce(gS, tc_[:], tb[:], PI / 2, [128, 32], "s5d", "s5rr2")
    A_(lambda e: e.activation(ar[:], tc_[:], AF.Sin))
    V(lambda e: e.tensor_tensor(ar[:], ar[:], ta[:], ALU.mult))
    V(lambda e: e.tensor_tensor(ai[:], ai[:], ta[:], ALU.mult))
    V(lambda e: e.tensor_scalar(ta[:], ar[:], -1.0, None, ALU.add))
    V(lambda e: e.tensor_tensor(tb[:], lr[:], lr[:], ALU.mult))
    V(lambda e: e.tensor_tensor(tc_[:], li[:], li[:], ALU.mult))
    V(lambda e: e.tensor_tensor(tb[:], tb[:], tc_[:], ALU.add))
    V(lambda e: e.reciprocal(tb[:], tb[:]))
    V(lambda e: e.tensor_tensor(tc_[:], ta[:], lr[:], ALU.mult))
    V(lambda e: e.tensor_tensor(td[:], ai[:], li[:], ALU.mult))
    V(lambda e: e.tensor_tensor(tc_[:], tc_[:], td[:], ALU.add))
    V(lambda e: e.tensor_tensor(fr[:], tc_[:], tb[:], ALU.mult))
    V(lambda e: e.tensor_tensor(tc_[:], ai[:], lr[:], ALU.mult))
    V(lambda e: e.tensor_tensor(td[:], ta[:], li[:], ALU.mult))
    V(lambda e: e.tensor_tensor(tc_[:], tc_[:], td[:], ALU.subtract))
    V(lambda e: e.tensor_tensor(fi[:], tc_[:], tb[:], ALU.mult))
    V(lambda e: e.memset(pwr[:, 0, :], 1.0))
    V(lambda e: e.memset(pwi[:, 0, :], 0.0))
    for k in range(1, 16):
        cmul(mk, "dve", pwr[:, k, :], pwi[:, k, :], pwr[:, k - 1, :], pwi[:, k - 1, :], ar[:], ai[:], tc_[:], td[:], K, K)
    V(lambda e: e.tensor_tensor(ta[:], ar[:], ar[:], ALU.mult))
    V(lambda e: e.tensor_tensor(tb[:], ai[:], ai[:], ALU.mult))
    V(lambda e: e.tensor_tensor(ta[:], ta[:], tb[:], ALU.add))
    V(lambda e: e.reciprocal(ta[:], ta[:]))
    V(lambda e: e.tensor_tensor(tb[:], ar[:], ta[:], ALU.mult))
    V(lambda e: e.scalar_tensor_tensor(ta[:], ai[:], -1.0, ta[:], ALU.mult, ALU.mult))
    V(lambda e: e.memset(ipr[:, 0, :], 1.0))
    V(lambda e: e.memset(ipi[:, 0, :], 0.0))
    for k in range(1, 8):
        cmul(mk, "dve", ipr[:, k, :], ipi[:, k, :], ipr[:, k - 1, :], ipi[:, k - 1, :], tb[:], ta[:], tc_[:], td[:], K, K)
    F_, B_ = slice(0, 64), slice(64, 128)
    for (dst, srcf, srcb) in ((PBr, ipr, pwr), (PBi, ipi, pwi), (PCr, pwr, ipr), (PCi, pwi, ipi)):
        V(lambda e, dst=dst, srcf=srcf: e.tensor_copy(dst[F_, :, :], srcf[F_, 0:8, :]))
        V(lambda e, dst=dst, srcb=srcb: e.tensor_copy(dst[B_, :, :], srcb[B_, 0:8, :]))
    for (dst, src) in ((PYr, pwr), (PYi, pwi)):
        V(lambda e, dst=dst, src=src: e.tensor_copy(dst[F_, :, :], src[F_, 8:16, :]))
        for k in range(8):
            V(lambda e, dst=dst, src=src, k=k: e.tensor_copy(dst[B_, k, :], src[B_, 8 - k, :]))
    V(lambda e: e.tensor_copy(Acoef[:, 0, 0, :], pwr[:, 8, :]))
    V(lambda e: e.tensor_scalar(Acoef[:, 0, 1, :], pwi[:, 8, :], -1.0, None, ALU.mult))
    V(lambda e: e.tensor_copy(Acoef[:, 1, 0, :], pwi[:, 8, :]))
    V(lambda e: e.tensor_copy(Acoef[:, 1, 1, :], pwr[:, 8, :]))
    if STOP == "S1":
        mk.barrier()
        return
    a2r = sm("s5a2r", [128, 32]); a2i = sm("s5a2i", [128, 32]); a4r = sm("s5a4r", [128, 32]); a4i = sm("s5a4i", [128, 32])
    Acoef4 = sm("s5Acoef4", [128, 2, 2, 32])
    cmul(mk, "dve", a2r[:], a2i[:], pwr[:, 8, :], pwi[:, 8, :], pwr[:, 8, :], pwi[:, 8, :], tc_[:], td[:], K, K)
    cmul(mk, "dve", a4r[:], a4i[:], a2r[:], a2i[:], a2r[:], a2i[:], tc_[:], td[:], K, K)
    V(lambda e: e.tensor_copy(Acoef4[:, 0, 0, :], a4r[:]))
    V(lambda e: e.tensor_scalar(Acoef4[:, 0, 1, :], a4i[:], -1.0, None, ALU.mult))
    V(lambda e: e.tensor_copy(Acoef4[:, 1, 0, :], a4i[:]))
    V(lambda e: e.tensor_copy(Acoef4[:, 1, 1, :], a4r[:]))
    frb = fr[:, :].unsqueeze(2).to_broadcast([128, 32, 16])
    fib = fi[:, :].unsqueeze(2).to_broadcast([128, 32, 16])
    cmul(mk, "dve", bbr[:], bbi[:], frb, fib, Bre[:], Bim[:], big1[:], big2[:], K, K)
    mk.op("pool", lambda e: e.memset(maskF[:], 1.0), wr=K)
    mk.op("pool", lambda e: e.memset(maskB[:], 1.0), wr=K)
    mk.op("pool", lambda e: e.affine_select(maskF[:, :].rearrange("p (t c) -> p t c", t=8), maskF[:, :].rearrange("p (t c) -> p t c", t=8),
                                            pattern=[[16, 8], [0, 16]], compare_op=ALU.is_ge, fill=0.0, base=15, channel_multiplier=-1), rd=K, wr=K)
    mk.op("pool", lambda e: e.affine_select(maskB[:, :].rearrange("p (t c) -> p t c", t=8), maskB[:, :].rearrange("p (t c) -> p t c", t=8),
                                            pattern=[[-16, 8], [0, 16]], compare_op=ALU.is_ge, fill=0.0, base=0, channel_multiplier=1), rd=K, wr=K)
    with contextlib.ExitStack() as stc:
        pcT = stc.enter_context(nc.psum_tensor(uq("s5pcT"), [128, 128], F32))
        for (src_d, dstC) in ((g.e_c_re, Cre), (g.e_c_im, Cim)):
            mk.dma(Cld[:], src_d.rearrange("d g c p -> (d g c) p").rearrange("(r q) p -> q r p", q=128), rd=K, wr=K)
            for r in range(8):
                d = r // 4
                mk.op("pe", lambda e, r=r: e.transpose(pcT[0:64, :], Cld[:, r, :], g.identf[:]), rd=K + ["identf"], wr=["s5pcT"])
                mk.op("dve", lambda e, r=r, d=d, dstC=dstC: e.tensor_copy(
                    dstC[d * 64:(d + 1) * 64, (r % 4) * 8:(r % 4) * 8 + 8, :].rearrange("p a b -> p (a b)"), pcT[0:64, :]), rd=["s5pcT"] + K, wr=K)
        mk.barrier()
    Zpad = sm("s5Zpad", [128, 8, 240], BF16)
    mk.op("pool", lambda e: e.memset(Zpad[:], 0.0), wr=["Zpad"])
    for j in range(8):
        mk.op("pool", lambda e, j=j: e.tensor_copy(Zpad[:, j, 112:128], g.identb[:, 16 * j:16 * j + 16]), rd=["identb"], wr=["Zpad"])
    if STOP == "S2":
        mk.barrier()
        return
    G = 16
    WzTr = sm("s5WzTr", [128, G, 128], BF16); WzTi = sm("s5WzTi", [128, G, 128], BF16)
    Mall = sm("s5Mall", [128, G, 128], BF16)
    WYr = sm("s5WYr", [128, G, 128], BF16); WYi = sm("s5WYi", [128, G, 128], BF16)
    gv = gT[:, :, :].rearrange("p c (n j) -> p c n j", j=8)
    for half in range(2):
        g0 = half * G
        gs = slice(g0, g0 + G)
        phase(g, "L0_wgen%d" % half)
        with contextlib.ExitStack() as stw:
            gw = Ctx(); gw.st = stw; gw.nc = nc; gw.mk = mk
            XBr = sb(gw, "XBr", [128, G, 8, 16], F32); XBi = sb(gw, "XBi", [128, G, 8, 16], F32)
            CTr = sb(gw, "CTr", [128, G, 8, 16], F32); CTi = sb(gw, "CTi", [128, G, 8, 16], F32)
            w1 = sb(gw, "s5w1", [128, G, 8, 16], F32); w2 = sb(gw, "s5w2", [128, G, 8, 16], F32)
            tM = sb(gw, "s5tM", [128, 128], F32); tM2 = sb(gw, "s5tM2", [128, 128], F32)
            pMt = [stw.enter_context(nc.psum_tensor(uq("s5pM%d" % i), [128, 512], F32)) for i in range(2)]
            pM = [pMt[0][:, 0:128], pMt[1][:, 0:128]]
            def pwb(Ptab):
                return Ptab[:, :, gs].rearrange("p k g -> p g k").unsqueeze(3).to_broadcast([128, G, 8, 16])
            def gcb(Ctab):
                return Ctab[:, gs, :].unsqueeze(2).to_broadcast([128, G, 8, 16])
            KW = ["s5w"]
            cmul(mk, "dve", XBr[:], XBi[:], pwb(PBr), pwb(PBi), gcb(bbr), gcb(bbi), w1[:], w2[:], K, KW)
            cmul(mk, "dve", CTr[:], CTi[:], pwb(PCr), pwb(PCi), gcb(Cre), gcb(Cim), w1[:], w2[:], K + KW, ["s5ct"], neg_im=True)
            cmul(mk, "dve", WYr[:, :, :].rearrange("p g (t c) -> p g t c", t=8), WYi[:, :, :].rearrange("p g (t c) -> p g t c", t=8),
                 pwb(PYr), pwb(PYi), gcb(Cre), gcb(Cim), w1[:], w2[:], K + KW + ["s5ct", "s5Y"], ["WY"], neg_im=True)
            if STOP == "S3a":
                mk.barrier()
                return
            Macc = sb(gw, "Macc", [128, G, 128], F32)
            fm = sb(gw, "s5fm", [128, 2], F32)
            xmr = sb(gw, "xmr", [128, G, 128], BF16); xmi = sb(gw, "xmi", [128, G, 128], BF16)
            xbr = sb(gw, "xbr", [128, G, 128], BF16); xbi = sb(gw, "xbi", [128, G, 128], BF16)
            ctr = sb(gw, "ctr", [128, G, 128], BF16); cti = sb(gw, "cti", [128, G, 128], BF16)
            fl = lambda t: t[:, :, :, :].rearrange("p g j c -> p g (j c)")
            mk.op("pool", lambda e: e.memset(fm[:], 0.0), wr=["s5fm"])
            mk.op("pool", lambda e: e.memset(fm[0:64, 0:1], 1.0), wr=["s5fm"])
            mk.op("pool", lambda e: e.memset(fm[64:128, 1:2], 1.0), wr=["s5fm"])
            mk.op("pool", lambda e: e.tensor_copy(ctr[:], fl(CTr)), rd=["s5ct"], wr=["ctb"])
            mk.op("pool", lambda e: e.tensor_copy(cti[:], fl(CTi)), rd=["s5ct"], wr=["ctb"])
            mk.op("pool", lambda e: e.tensor_copy(xbr[:], fl(XBr)), rd=KW, wr=["xbb"])
            mk.op("pool", lambda e: e.tensor_copy(xbi[:], fl(XBi)), rd=KW, wr=["xbb"])
            for d in range(2):
                mk.op("dve", lambda e, d=d: e.tensor_scalar(xmr[:], fl(XBr), fm[:, d:d + 1], None, ALU.mult), rd=KW + ["s5fm"], wr=["xm"])
                mk.op("dve", lambda e, d=d: e.tensor_scalar(xmi[:], fl(XBi), fm[:, d:d + 1], None, ALU.mult), rd=KW + ["s5fm"], wr=["xm"])
                for gl in range(G):
                    gi = g0 + gl
                    pm_ = pM[gl % 2]
                    pmk = "s5pM%d" % (gl % 2)
                    mk.op("pe", lambda e, pm_=pm_, gl=gl: e.matmul(pm_, lhsT=xmr[:, gl, :], rhs=ctr[:, gl, :], start=True, stop=False), rd=["xm", "ctb"], wr=[pmk])
                    mk.op("pe", lambda e, pm_=pm_, gl=gl: e.matmul(pm_, lhsT=xmi[:, gl, :], rhs=cti[:, gl, :], start=False, stop=True), rd=["xm", "ctb"], wr=[pmk])
                    if d == 0:
                        mk.op("dve", lambda e, gl=gl, pm_=pm_: e.tensor_tensor(Macc[:, gl, :], pm_, maskF[:], ALU.mult), rd=[pmk] + K, wr=[("Macc", gl)])
                    else:
                        mk.op("dve", lambda e, pm_=pm_: e.tensor_tensor(tM[:], pm_, maskB[:], ALU.mult), rd=[pmk] + K, wr=["tM"])
                        mk.op("pool", lambda e, gl=gl: e.tensor_tensor(tM[:], tM[:], Macc[:, gl, :], ALU.add), rd=["tM", ("Macc", gl)], wr=["tM"])
                        mk.op("dve", lambda e, gl=gl, gi=gi: e.scalar_tensor_tensor(Mall[:, gl, :], g.identf[:], dcol[:, gi:gi + 1], tM[:], ALU.mult, ALU.add),
                              rd=["tM", "identf"] + K, wr=["Mall"])
            if STOP == "S3b":
                mk.barrier()
                return
            for gl in range(G):
                mk.op("pe", lambda e, gl=gl: e.matmul(pM[0], lhsT=xbr[:, gl, :], rhs=g.identb[:], start=True, stop=True), rd=["xbb", "identb"], wr=["s5pM0"])
                mk.op("dve", lambda e, gl=gl: e.tensor_copy(WzTr[:, gl, :], pM[0]), rd=["s5pM0"], wr=["WzT"])
                mk.op("pe", lambda e, gl=gl: e.matmul(pM[1], lhsT=xbi[:, gl, :], rhs=g.identb[:], start=True, stop=True), rd=["xbb", "identb"], wr=["s5pM1"])
                mk.op("dve", lambda e, gl=gl: e.tensor_copy(WzTi[:, gl, :], pM[1]), rd=["s5pM1"], wr=["WzT"])
            mk.barrier()
        if STOP == "S3":
            return
        phase(g, "L0_rec%d" % half)
        with contextlib.ExitStack() as stq:
            gq = Ctx(); gq.st = stq; gq.nc = nc; gq.mk = mk
            Sh = sb(gq, "Sh", [128, 2, G, NS], BF16)
            pY = [stq.enter_context(nc.psum_tensor(uq("s5pY%d" % i), [128, 1024], F32)) for i in range(2)]
            pG = stq.enter_context(nc.psum_tensor(uq("s5pG"), [128, 1024], F32))
            pZ = [stq.enter_context(nc.psum_tensor(uq("s5pZ%d" % i), [128, 512], F32)) for i in range(2)]
            Uv = g.U_d.rearrange("p (g n) -> p g n", g=32)
            Ubv = g.Ub_d.rearrange("p (g n) -> p g n", g=32)
            QN = 68
            with contextlib.ExitStack() as str_:
                gr = Ctx(); gr.st = str_; gr.nc = nc; gr.mk = mk
                Uc = [sb(gr, "Uc%d" % i, [128, G, QN], BF16) for i in range(2)]
                Ubc = [sb(gr, "Ubc%d" % i, [128, G, QN], BF16) for i in range(2)]
                Zc = [sb(gr, "Zc%d" % i, [128, QN, 2, G], F32) for i in range(2)]
                NM = QN // 4
                Sw = [sb(gr, "Sw%d" % i, [128, QN, 2, G], F32) for i in range(2)]
                Wb = sb(gr, "Wb", [128, NM, 2, G], F32)
                Pw = sb(gr, "Pw", [128, NM, 2, 2 * G], F32)
                Tw = sb(gr, "Tw", [128, NM, 2, G], F32)
                Pm = sb(gr, "Pm", [128, 2, 2, G], F32)
                Tm = sb(gr, "Tm", [128, 2, G], F32)
                mk.op("dve", lambda e: e.memset(Sw[0][:, 0, :, :], 0.0), wr=["Sw0"])
                Ac = Acoef[:, :, :, gs]
                Ac4 = Acoef4[:, :, :, gs]
                Acw = Acoef[:, :, :, gs]

                def amul_add(out3, x3, y3, kx, ky, ko):
                    for ro in range(2):
                        mk.op("dve", lambda e, ro=ro: e.tensor_tensor(
                            Pw[:, :, ro, :].rearrange("p m (r g) -> p m r g", r=2),
                            Acw[:, ro, :, :].unsqueeze(1).to_broadcast([128, NM, 2, G]),
                            x3.rearrange("p m (r g) -> p m r g", r=2), ALU.mult), rd=kx + ["s5d"], wr=["Pw"])
                    Pv = Pw[:, :, :, :].rearrange("p m o (r g) -> p (m o) r g", r=2)
                    mk.op("dve", lambda e: e.tensor_tensor(Tw[:, :, :, :].rearrange("p m o g -> p (m o) g"), Pv[:, :, 0, :], Pv[:, :, 1, :], ALU.add),
                          rd=["Pw"], wr=["Tw"])
                    mk.op("dve", lambda e: e.tensor_tensor(out3, Tw[:, :, :, :].rearrange("p m o g -> p m (o g)"), y3, ALU.add), rd=["Tw"] + ky, wr=ko)

                nz = 0
                NQ = NS // QN
                for q in range(NQ):
                    zc = Zc[q % 2]
                    zck = "Zc%d" % (q % 2)
                    sw, swn = Sw[q % 2], Sw[(q + 1) % 2]
                    swk, swnk = "Sw%d" % (q % 2), "Sw%d" % ((q + 1) % 2)
                    uc, ubc = Uc[q % 2], Ubc[q % 2]
                    uck, ubck = "Uc%d" % (q % 2), "Ubc%d" % (q % 2)
                    ns = slice(q * QN, (q + 1) * QN)
                    mk.dma(uc[:], Uv[:, gs, ns], rd=["U_d"], wr=[uck])
                    mk.dma(ubc[:], Ubv[:, gs, ns], rd=["Ub_d"], wr=[ubck])
                    for g3 in range(0, G, 3):
                        pz = pZ[nz % 2]
                        zk = "s5pZ%d" % (nz % 2)
                        nz += 1
                        gls = list(range(g3, min(G, g3 + 3)))
                        for si, gl in enumerate(gls):
                            for ri, WzT in enumerate((WzTr, WzTi)):
                                slot = si * 2 + ri
                                mk.op("pe", lambda e, gl=gl, WzT=WzT, slot=slot, uc=uc, pz=pz: e.matmul(pz[0:64, slot * QN:(slot + 1) * QN], lhsT=WzT[:, gl, 0:64], rhs=uc[:, gl, :],
                                                                                                     start=True, stop=True), rd=["WzT", uck], wr=[zk])
                                mk.op("pe", lambda e, gl=gl, WzT=WzT, slot=slot, ubc=ubc, pz=pz: e.matmul(pz[64:128, slot * QN:(slot + 1) * QN], lhsT=WzT[:, gl, 64:128], rhs=ubc[:, gl, :],
                                                                                                       start=True, stop=True), rd=["WzT", ubck], wr=[zk])
                        ng = len(gls)
                        mk.op("act", lambda e, zc=zc, g3=g3, ng=ng, pz=pz: e.activation(
                            zc[:, :, :, g3:g3 + ng].rearrange("p n r g -> p g r n"),
                            pz[:, 0:ng * 2 * QN].rearrange("p (g r n) -> p g r n", g=ng, r=2), AF.Copy), rd=[zk], wr=[zck])
                    zv = zc[:, :, :, :].rearrange("p (m r) o g -> p m r (o g)", r=4)
                    swv = sw[:, :, :, :].rearrange("p (m r) o g -> p m r (o g)", r=4)
                    wv = Wb[:, :, :, :].rearrange("p m o g -> p m (o g)")
                    amul_add(wv, zv[:, :, 0, :], zv[:, :, 1, :], [zck], [zck], ["Wb"])
                    amul_add(wv, wv, zv[:, :, 2, :], ["Wb"], [zck], ["Wb"])
                    amul_add(wv, wv, zv[:, :, 3, :], ["Wb"], [zck], ["Wb"])
                    for m in range(NM):
                        cur = sw[:, 4 * m, :, :]
                        if m < NM - 1:
                            nxt, nk = sw[:, 4 * (m + 1), :, :], swk
                        else:
                            nxt, nk = swn[:, 0, :, :], swnk
                        mk.op("dve", lambda e, cur=cur: e.tensor_tensor(Pm[:], Ac4, cur.unsqueeze(1).to_broadcast([128, 2, 2, G]), ALU.mult),
                              rd=[swk, "s5d"], wr=["Pm"])
                        mk.op("dve", lambda e: e.tensor_tensor(Tm[:], Pm[:, :, 0, :], Pm[:, :, 1, :], ALU.add), rd=["Pm"], wr=["Tm"])
                        if q == NQ - 1 and m == NM - 1:
                            break
                        mk.op("dve", lambda e, nxt=nxt, m=m: e.tensor_tensor(nxt, Tm[:], Wb[:, m, :, :], ALU.add), rd=["Tm", "Wb"], wr=[nk])
                    for r in range(1, 4):
                        amul_add(swv[:, :, r, :], swv[:, :, r - 1, :], zv[:, :, r - 1, :], [swk], [zck], [swk])
                    n0 = q * QN
                    mk.op("act", lambda e, sw=sw, n0=n0: e.activation(Sh[0:64, :, :, n0:n0 + QN].rearrange("p o g n -> p n o g"), sw[0:64, :, :, :], AF.Copy),
                          rd=[swk], wr=["Sh"])
                    segs = [(0, 32, 31), (32, QN, 543)] if q == 0 else [(0, QN, 575 - n0)]
                    for (i0, i1, r0) in segs:
                        L_ = i1 - i0
                        stop = r0 - L_
                        dsl = slice(r0, stop if stop >= 0 else None, -1)
                        mk.op("pool", lambda e, sw=sw, i0=i0, i1=i1, dsl=dsl: e.tensor_copy(
                            Sh[64:128, :, :, dsl].rearrange("p o g n -> p n o g"), sw[64:128, i0:i1, :, :]), rd=[swk], wr=["Sh"])
                mk.barrier()
            if STOP == "S4":
                return
            phase(g, "L0_Y%d" % half)
            with contextlib.ExitStack() as sty:
                gy = Ctx(); gy.st = sty; gy.nc = nc; gy.mk = mk
                Uh = sb(gy, "Uh", [128, G, NS], BF16)
                Ybf = sb(gy, "Ybf", [128, 8, NS], BF16)
                ga = sb(gy, "ga", [128, NS], F32); gb = sb(gy, "gb", [128, NS], F32)
                mk.dma(Uh[:], Uv[:, gs, :], rd=["U_d"], wr=["Uh"])
                for ch in range(2):
                    chunk = half * 2 + ch
                    for g8 in range(8):
                        gl = ch * 8 + g8
                        py = pY[g8 % 2]
                        pyk = "s5pY%d" % (g8 % 2)
                        for (c0, c1) in ((0, 512), (512, NS)):
                            mk.op("pe", lambda e, gl=gl, py=py, c0=c0, c1=c1: e.matmul(py[:, c0:c1], lhsT=Mall[:, gl, :], rhs=Uh[:, gl, c0:c1], start=True, stop=False),
                                  rd=["Mall", "Uh"], wr=[pyk])
                            mk.op("pe", lambda e, gl=gl, py=py, c0=c0, c1=c1: e.matmul(py[:, c0:c1], lhsT=WYr[:, gl, :], rhs=Sh[:, 0, gl, c0:c1], start=False, stop=False),
                                  rd=["WY", "Sh"], wr=[pyk])
                            mk.op("pe", lambda e, gl=gl, py=py, c0=c0, c1=c1: e.matmul(py[:, c0:c1], lhsT=WYi[:, gl, :], rhs=Sh[:, 1, gl, c0:c1], start=False, stop=True),
                                  rd=["WY", "Sh"], wr=[pyk])
                        if g8 % 2 == 0:
                            mk.op("act", lambda e, g8=g8, py=py: e.activation(Ybf[:, g8, :], py[:, 0:NS], AF.Copy), rd=[pyk], wr=["Ybf"])
                        else:
                            mk.op("dve", lambda e, g8=g8, py=py: e.tensor_copy(Ybf[:, g8, :], py[:, 0:NS]), rd=[pyk], wr=["Ybf"])
                    for j in range(8):
                        for (c0, c1) in ((0, 512), (512, NS)):
                            for g8 in range(8):
                                mk.op("pe", lambda e, g8=g8, j=j, c0=c0, c1=c1: e.matmul(pG[:, c0:c1], lhsT=Zpad[:, j, 112 - 16 * g8:240 - 16 * g8], rhs=Ybf[:, g8, c0:c1],
                                                                                       start=(g8 == 0), stop=(g8 == 7)), rd=["Zpad", "Ybf"], wr=["s5pG"])
                        yv = pG[:, 0:NS]
                        mk.op("act", lambda e: e.activation(ga[:], yv, AF.Square), rd=["s5pG"], wr=["ga"])
                        mk.op("dve", lambda e: e.tensor_scalar(ga[:], ga[:], 0.044715, 1.0, ALU.mult, ALU.add), rd=["ga"], wr=["ga"])
                        mk.op("dve", lambda e: e.tensor_tensor(gb[:], ga[:], yv, ALU.mult), rd=["ga", "s5pG"], wr=["gb"])
                        mk.op("act", lambda e: e.activation(gb[:], gb[:], AF.Sigmoid, scale=1.5957691216057308), rd=["gb"], wr=["gb"])
                        mk.op("dve", lambda e, j=j, chunk=chunk: e.tensor_tensor(gv[:, chunk, :, j], gb[:], yv, ALU.mult), rd=["gb", "s5pG"], wr=["gT"])
                mk.barrier()


def glu_phase(g, gT):
    mk, nc = g.mk, g.nc
    phase(g, "L0_glu")
    with contextlib.ExitStack() as st:
        g2 = Ctx(); g2.st = st; g2.nc = nc; g2.mk = mk
        gw = sb(g2, "gluw", [128, 4, 512], BF16)
        load_w(g, gw, g.e_glu_w, 512, 512, "gluw")
        gbc = sb(g2, "glub", [128, 4], F32)
        mk.dma(gbc[:], g.e_glu_b[0, :].rearrange("(m p) -> p m", p=128), wr=["glub"], allow_slow_non_contiguous=True)
        pg = [st.enter_context(nc.psum_tensor(uq("glup%d" % i), [128, 512], F32)) for i in range(2)]
        sg = sb(g2, "glusg", [128, 512], F32)
        o1 = sb(g2, "gluo1", [128, 512], F32)
        zs = [sb(g2, "gluzs%d" % i, [128, 512], BF16) for i in range(2)]
        ms = [sb(g2, "glums%d" % i, [128, 512], BF16) for i in range(2)]
        blocks = [(0, 256)] + [(256 + 512 * b, 512) for b in range(8)]
        n = 0
        for m in range(4):
            for (t0, w) in blocks:
                b = n % 2
                n += 1
                mk.dma(zs[b][:, 0:w], g.szs_d[m * 128:(m + 1) * 128, t0:t0 + w], rd=["szs_d"], wr=["gluzs%d" % b])
                for k in range(4):
                    mk.op("pe", lambda e, k=k, m=m, b=b, t0=t0, w=w: e.matmul(pg[b][:, 0:w], lhsT=gw[:, k, m * 128:(m + 1) * 128], rhs=gT[:, k, t0:t0 + w],
                                                                             start=(k == 0), stop=(k == 3)), rd=["gluw", "gT"], wr=["glup%d" % b])
                mk.op("act", lambda e, b=b, m=m, w=w: e.activation(sg[:, 0:w], pg[b][:, 0:w], AF.Sigmoid, bias=gbc[:, m:m + 1]), rd=["glup%d" % b, "glub"], wr=["glusg"])
                mk.op("dve", lambda e, m=m, t0=t0, w=w: e.tensor_tensor(o1[:, 0:w], sg[:, 0:w], gT[:, m, t0:t0 + w], ALU.mult), rd=["glusg", "gT"], wr=["gluo1"])
                mk.op("pool", lambda e, b=b, w=w: e.tensor_tensor(ms[b][:, 0:w], o1[:, 0:w], zs[b][:, 0:w], ALU.mult), rd=["gluo1", "gluzs%d" % b], wr=["glums%d" % b])
                mk.dma(g.mix_d[512 + m * 128:512 + (m + 1) * 128, t0:t0 + w], ms[b][:, 0:w], rd=["glums%d" % b], wr=["mix_d"])
        mk.barrier()


def out_proj0(g, gate_l, gate_c):
    mk, nc = g.mk, g.nc
    phase(g, "L0_out")
    with contextlib.ExitStack() as st:
        g2 = Ctx(); g2.st = st; g2.nc = nc; g2.mk = mk
        wout = sb(g2, "wout0", [128, 8, 1024], BF16)
        load_w(g, wout, g.e_w_out, 1024, 1024, "wout0")
        mb = [sb(g2, "mixb%d" % i, [128, 8, 512], BF16) for i in range(2)]
        ht = [sb(g2, "h0t%d" % i, [128, 1024], F32) for i in range(4)]
        ot = [sb(g2, "o0t%d" % i, [128, 1024], F32) for i in range(4)]
        pD = [st.enter_context(nc.psum_tensor(uq("p0D%d" % i), [128, 1024], F32)) for i in range(2)]
        mv = g.mix_d.rearrange("(k p) t -> p k t", p=128)
        blocks = [(0, 256)] + [(256 + 512 * b, 512) for b in range(8)]
        ti = 0
        for bi, (t0, w) in enumerate(blocks):
            mbb = mb[bi % 2]
            mbk = "mixb%d" % (bi % 2)
            mk.dma(mbb[:, :, 0:w], mv[:, :, t0:t0 + w], rd=["mix_d"], wr=[mbk])
            for s in range(w // 128):
                i = (t0 + s * 128) // 128
                b = ti % 2
                ti += 1
                lat = i >= 2
                if lat:
                    src = g.x_d[(i - 2) * 128:(i - 1) * 128, :]
                    dst = g.y_d[(i - 2) * 128:(i - 1) * 128, :]
                    dk = ("y", i - 2)
                    gt, gk = gate_l, "gate_l0"
                else:
                    src = g.ctx_d[i * 128:(i + 1) * 128, :]
                    dst = g.hc1_d[i * 128:(i + 1) * 128, :]
                    dk = "hc1_d"
                    gt, gk = gate_c, "gate_c0"
                hb = ht[ti % 4]
                hbk = "h0t%d" % (ti % 4)
                mk.dma(hb[:], src, wr=[hbk])
                for n in range(2):
                    for k in range(8):
                        mk.op("pe", lambda e, n=n, k=k, b=b, s=s, mbb=mbb: e.matmul(pD[b][:, n * 512:(n + 1) * 512], lhsT=mbb[:, k, s * 128:(s + 1) * 128],
                                                                                  rhs=wout[:, k, n * 512:(n + 1) * 512], start=(k == 0), stop=(k == 7)),
                              rd=[mbk, "wout0"], wr=["p0D%d" % b])
                mk.op("dve", lambda e, b=b, gt=gt: e.tensor_tensor(ot[ti % 4][:], pD[b][:], gt[:], ALU.mult), rd=["p0D%d" % b, gk], wr=["o0t%d" % (ti % 4)])
                mk.op("pool", lambda e, b=b, hb=hb: e.tensor_tensor(ot[ti % 4][:], ot[ti % 4][:], hb[:], ALU.add), rd=["o0t%d" % (ti % 4), hbk], wr=["o0t%d" % (ti % 4)])
                mk.dma(dst, ot[ti % 4][:], rd=["o0t%d" % (ti % 4)], wr=[dk])
        mk.barrier()


def build(stage):
    nc = bass.Bass("TRN2", target_bir_lowering=False)
    g = Ctx()
    g.nc = nc
    dt = lambda name, shape, kind="ExternalInput", d=F32: nc.dram_tensor(name, list(shape), d, kind=kind).ap()
    scr = lambda name, shape, d=BF16: nc.dram_tensor(name, list(shape), d).ap()
    L0 = stage in ("L0", "both")
    L1 = stage in ("L1", "both")
    g.c_d = dt("c", [1, D])
    g.cctx_d = dt("c_ctx", [1, D])
    g.modw_d = dt("mod_w", [2, D, 3 * D])
    g.modb_d = dt("mod_b", [2, 3 * D])
    g.normw_d = dt("norm_w", [2, D])
    g.y_d = dt("y", [TL, D], "ExternalOutput")
    if L0:
        g.x_d = dt("x", [TL, D])
        g.ctx_d = dt("ctx", [TC, D])
        g.e_w_in = dt("e_w_in", [D, 3072])
        g.e_conv_w = dt("e_conv_w", [3, 512])
        g.e_lam_re = dt("e_lam_re", [2, 32, 64])
        g.e_lam_im = dt("e_lam_im", [2, 32, 64])
        g.e_log_step = dt("e_log_step", [2, 32])
        g.e_b_re = dt("e_b_re", [2, 32, 64, 16])
        g.e_b_im = dt("e_b_im", [2, 32, 64, 16])
        g.e_c_re = dt("e_c_re", [2, 32, 16, 64])
        g.e_c_im = dt("e_c_im", [2, 32, 16, 64])
        g.e_d = dt("e_d", [1, 512])
        g.e_glu_w = dt("e_glu_w", [512, 512])
        g.e_glu_b = dt("e_glu_b", [1, 512])
        g.e_w_out = dt("e_w_out", [D, D])
        g.U_d = scr("U_s", [128, 32 * 544])
        g.Ub_d = scr("Ub_s", [128, 32 * 544])
        g.szs_d = scr("szs_s", [512, T])
        g.mix_d = dt("mix_s", [D, T], "ExternalOutput", BF16) if DEBUG_L0 else scr("mix_s", [D, T])
    if stage == "L0":
        g.hc1_d = dt("hc1", [TC, D], "ExternalOutput")
    elif stage == "L1":
        g.h1_in = dt("h1", [TL, D])
        g.hc1_d = dt("hc1", [TC, D])
    else:
        g.hc1_d = scr("hc1_s", [TC, D], F32)
    if L1:
        g.o_w_in = dt("o_w_in", [D, 1696])
        g.o_q_a_norm = dt("o_q_a_norm", [1, 384])
        g.o_w_uq = dt("o_w_uq", [384, 1536])
        g.o_kv_a_norm = dt("o_kv_a_norm", [1, 256])
        g.o_w_ukv = dt("o_w_ukv", [256, 2048])
        g.o_q_norm = dt("o_q_norm", [1, 96])
        g.o_k_norm = dt("o_k_norm", [1, 96])
        g.o_w_out = dt("o_w_out", [D, D])
        g.qT_d = scr("qT_s", [NH, 96, TL])
        g.kT_d = scr("kT_s", [NH, 96, T])
        g.v_d = scr("v_s", [T, D])
        g.sg_d = scr("sg_s", [D, TL])
    with contextlib.ExitStack() as st:
        g.st = st
        g.mk = MK(nc, st)
        setup_consts(g)
        if L1:
            rope_tables(g)
        if stage == "L1":
            for lt in range(32):
                g.mk.dma(g.y_d[lt * 128:(lt + 1) * 128, :], g.h1_in[lt * 128:(lt + 1) * 128, :], wr=[("y", lt)])
        if L0:
            layer0(g)
        if L1:
            layer1(g)
        phase(g, None)
        g.mk.wait_all("sp", [("y", lt) for lt in range(32)] + ["hc1_d"])
        g.mk.barrier()
    return nc


L0_IN = ["e_w_in", "e_conv_w", "e_lam_re", "e_lam_im", "e_log_step", "e_b_re", "e_b_im", "e_c_re", "e_c_im", "e_d", "e_glu_w", "e_glu_b", "e_w_out"]
L1_IN = ["o_w_in", "o_q_a_norm", "o_w_uq", "o_kv_a_norm", "o_w_ukv", "o_q_norm", "o_k_norm", "o_w_out"]
_NC_CACHE = {}


def _get_nc(stage):
    if stage not in _NC_CACHE:
        _NC_CACHE[stage] = build(stage)
    return _NC_CACHE[stage]


def _common(inp, b):
    f = lambda a: np.ascontiguousarray(a, dtype=np.float32)
    return {"c": f(inp["c"][b:b + 1]), "c_ctx": f(np.asarray(inp["c_ctx"])[None, :]), "mod_w": f(inp["mod_w"]),
            "mod_b": f(inp["mod_b"]), "norm_w": f(inp["norm_w"])}


def _l0_map(inp, b):
    f = lambda a: np.ascontiguousarray(a, dtype=np.float32)
    m = {"x": f(inp["x"][b]), "ctx": f(inp["ctx"][b])}
    for k in L0_IN:
        m[k] = f(np.asarray(inp[k])[0]) if k not in ("e_d", "e_glu_b") else f(np.asarray(inp[k]))
    return m


def _l1_map(inp):
    f = lambda a: np.ascontiguousarray(a, dtype=np.float32)
    m = {}
    for k in L1_IN:
        m[k] = f(np.asarray(inp[k])[0]) if k in ("o_w_in", "o_w_uq", "o_w_ukv", "o_w_out") else f(np.asarray(inp[k]))
    return m


FUSED = True
STOP = None
DEBUG_L0 = False


def kernel(**inp):
    n = 8
    cores = list(range(n))
    if FUSED:
        nc = _get_nc("both")
        maps = []
        for b in range(n):
            m = _common(inp, b)
            m.update(_l0_map(inp, b))
            m.update(_l1_map(inp))
            maps.append(m)
        res = run_bass_kernel_spmd(nc, maps, core_ids=cores)
        return np.stack([np.asarray(r["y"], dtype=np.float32) for r in res.results], axis=0)
    nc0 = _get_nc("L0")
    maps = []
    for b in range(n):
        m = _common(inp, b)
        m.update(_l0_map(inp, b))
        maps.append(m)
    res0 = run_bass_kernel_spmd(nc0, maps, core_ids=cores)
    nc1 = _get_nc("L1")
    maps = []
    for b in range(n):
        m = _common(inp, b)
        m.update(_l1_map(inp))
        m["h1"] = np.ascontiguousarray(res0.results[b]["y"], dtype=np.float32)
        m["hc1"] = np.ascontiguousarray(res0.results[b]["hc1"], dtype=np.float32)
        maps.append(m)
    res1 = run_bass_kernel_spmd(nc1, maps, core_ids=cores)
    return np.stack([np.asarray(r["y"], dtype=np.float32) for r in res1.results], axis=0)
```

```python
import contextlib
import math
import numpy as np
import ml_dtypes
import concourse.bass as bass
import concourse.mybir as mybir
from concourse.bass_utils import run_bass_kernel_spmd

F32 = mybir.dt.float32
BF16 = mybir.dt.bfloat16
I32 = mybir.dt.int32
AF = mybir.ActivationFunctionType
ALU = mybir.AluOpType
AX = mybir.AxisListType

T, TL, TC, NT = 4352, 4096, 256, 34
D = 1024
EPS = 1e-6
NH = 16
PI = math.pi


class MK:
    NDMA = 8

    def __init__(self, nc, stack):
        self.nc = nc
        self.engs = {"pe": nc.tensor, "act": nc.scalar, "dve": nc.vector, "pool": nc.gpsimd, "sp": nc.sync}
        self.semobj = {}
        self.cnt = {}
        for e in ("pe", "act", "dve", "pool"):
            self.semobj[e] = stack.enter_context(nc.semaphore("s_" + e))
            self.cnt[e] = 0
        self.dq = {}
        for q in ("sp",):
            sems = []
            for i in range(self.NDMA):
                k = "d_%s%d" % (q, i)
                self.semobj[k] = stack.enter_context(nc.semaphore(k))
                self.cnt[k] = 0
                sems.append(k)
            self.dq[q] = [sems, 0]
        self.seen = {e: {} for e in self.engs}
        self.lastw = {}
        self.readers = {}
        self.same_engine_sync = {"act": True, "dve": True, "pool": True, "pe": False, "sp": False}
        self.ninst = 0

    def _wait(self, eng, sk, v):
        if sk == eng and not self.same_engine_sync[eng]:
            return
        if self.seen[eng].get(sk, 0) >= v:
            return
        self.engs[eng].wait_ge(self.semobj[sk], v)
        self.seen[eng][sk] = v

    def _deps(self, rd, wr):
        deps = {}
        for k in rd:
            t = self.lastw.get(k)
            if t is not None:
                deps[t[0]] = max(deps.get(t[0], 0), t[1])
        for k in wr:
            t = self.lastw.get(k)
            if t is not None:
                deps[t[0]] = max(deps.get(t[0], 0), t[1])
            for sk, v in self.readers.get(k, {}).items():
                deps[sk] = max(deps.get(sk, 0), v)
        return deps

    def _record(self, tok, rd, wr):
        for k in rd:
            r = self.readers.setdefault(k, {})
            r[tok[0]] = max(r.get(tok[0], 0), tok[1])
        for k in wr:
            self.lastw[k] = tok
            self.readers[k] = {}

    def op(self, eng, fn, rd=(), wr=()):
        for sk, v in self._deps(rd, wr).items():
            self._wait(eng, sk, v)
        inst = fn(self.engs[eng])
        self.cnt[eng] += 1
        inst.then_inc(self.semobj[eng], 1)
        self._record((eng, self.cnt[eng]), rd, wr)
        self.ninst += 1
        return inst

    def dma(self, out, in_, rd=(), wr=(), q="sp", **kw):
        sems, i = self.dq[q]
        sk = sems[i % len(sems)]
        self.dq[q][1] = i + 1
        if self.cnt[sk] > 0:
            self._wait(q, sk, self.cnt[sk])
        for dk, v in self._deps(rd, wr).items():
            self._wait(q, dk, v)
        inst = self.engs[q].dma_start(out=out, in_=in_, **kw)
        self.cnt[sk] += 16
        inst.then_inc(self.semobj[sk], 16)
        self._record((sk, self.cnt[sk]), rd, wr)
        self.ninst += 1
        return inst

    def wait_all(self, eng, keys):
        for k in keys:
            t = self.lastw.get(k)
            if t is not None:
                self._wait(eng, t[0], t[1])

    def barrier(self):
        for e in self.engs:
            for sk, v in self.cnt.items():
                if v > 0:
                    self._wait(e, sk, v)


class Ctx:
    pass


_SCOPE = [None]


def phase(g, name):
    if _SCOPE[0] is not None:
        _SCOPE[0].__exit__(None, None, None)
        _SCOPE[0] = None
    if name is not None:
        cm = g.nc.named_scope(name)
        cm.__enter__()
        _SCOPE[0] = cm


_UID = [0]


def uq(name):
    _UID[0] += 1
    return "%s_u%d" % (name, _UID[0])


def sb(g, name, shape, dt):
    return g.st.enter_context(g.nc.sbuf_tensor(uq(name), list(shape), dt))


def setup_consts(g):
    mk, nc = g.mk, g.nc
    g.identb = sb(g, "identb", [128, 128], BF16)
    g.identf = sb(g, "identf", [128, 128], F32)
    g.onesf = sb(g, "onesf", [64, 128], F32)
    g.epsb = sb(g, "epsb", [128, 1], F32)
    tmp = sb(g, "idtmp", [128, 128], F32)
    mk.op("pool", lambda e: e.memset(tmp[:], 1.0), wr=["idtmp"])
    mk.op("pool", lambda e: e.affine_select(g.identf[:], tmp[:], pattern=[[-1, 128]], compare_op=ALU.is_equal,
                                            fill=0.0, base=0, channel_multiplier=1), rd=["idtmp"], wr=["identf"])
    mk.op("pool", lambda e: e.tensor_copy(g.identb[:], g.identf[:]), rd=["identf"], wr=["identb"])
    mk.op("pool", lambda e: e.memset(g.onesf[:], 1.0), wr=["onesf"])
    mk.op("pool", lambda e: e.memset(g.epsb[:], EPS), wr=["epsb"])
    g.stg = [sb(g, "stg%d" % i, [128, 1024], F32) for i in range(4)]
    g.stg_i = 0


def load_w(g, dst, src, K, N, key, col0=0):
    mk = g.mk
    KC = (K + 127) // 128
    for kc in range(KC):
        rows = min(128, K - kc * 128)
        for n0 in range(0, N, 1024):
            w = min(1024, N - n0)
            s = g.stg[g.stg_i % 4]
            sk = "stg%d" % (g.stg_i % 4)
            g.stg_i += 1
            mk.dma(s[0:rows, 0:w], src[kc * 128:kc * 128 + rows, n0:n0 + w], wr=[sk])
            mk.op("pool", lambda e, s=s, kc=kc, n0=n0, w=w, rows=rows: e.tensor_copy(
                dst[0:rows, kc, col0 + n0:col0 + n0 + w], s[0:rows, 0:w]), rd=[sk], wr=[key])


def range_reduce(g, y, x, shift, shape, key, nm):
    mk = g.mk
    ki = sb(g, nm + "_ki", shape, I32)
    kf = sb(g, nm + "_kf", shape, F32)
    m = sb(g, nm + "_m", shape, F32)
    K = [key]
    mk.op("dve", lambda e: e.tensor_scalar(y, x, shift, None, ALU.add), rd=K, wr=K)
    mk.op("dve", lambda e: e.tensor_scalar(kf[:], y, 1.0 / (2 * PI), None, ALU.mult), rd=K, wr=K)
    mk.op("dve", lambda e: e.tensor_copy(ki[:], kf[:]), rd=K, wr=K)
    mk.op("dve", lambda e: e.tensor_copy(kf[:], ki[:]), rd=K, wr=K)
    mk.op("dve", lambda e: e.scalar_tensor_tensor(y, kf[:], -2 * PI, y, ALU.mult, ALU.add), rd=K, wr=K)
    mk.op("dve", lambda e: e.tensor_scalar(m[:], y, PI, None, ALU.is_gt), rd=K, wr=K)
    mk.op("dve", lambda e: e.scalar_tensor_tensor(y, m[:], -2 * PI, y, ALU.mult, ALU.add), rd=K, wr=K)
    mk.op("dve", lambda e: e.tensor_scalar(m[:], y, -PI, None, ALU.is_lt), rd=K, wr=K)
    mk.op("dve", lambda e: e.scalar_tensor_tensor(y, m[:], 2 * PI, y, ALU.mult, ALU.add), rd=K, wr=K)


def modulation(g, l, names):
    phase(g, "L%d_mod" % l)
    mk, nc = g.mk, g.nc
    with contextlib.ExitStack() as st:
        g2 = Ctx(); g2.st = st; g2.nc = nc; g2.mk = mk
        ccol = sb(g2, "ccol", [128, 16], F32)
        S33 = sb(g2, "S33", [128, 8, 64], F32)
        mw = [sb(g2, "mw%d" % i, [128, 3072], F32) for i in range(2)]
        mrow = sb(g2, "mrow", [64, 3072], F32)
        mb = sb(g2, "mb", [64, 3072], F32)
        nwb = sb(g2, "nwb", [128, 1024], F32)
        pm = st.enter_context(nc.psum_tensor(uq("pm"), [128, 3072], F32))
        pb = st.enter_context(nc.psum_tensor(uq("pb"), [128, 1024], F32))
        mk.dma(ccol[:, 0:8], g.c_d[0, :].rearrange("(k p) -> p k", p=128), wr=["ccol"], allow_slow_non_contiguous=True)
        mk.dma(ccol[:, 8:16], g.cctx_d[0, :].rearrange("(k p) -> p k", p=128), wr=["ccol"], allow_slow_non_contiguous=True)
        mk.dma(mb[0:1, :], g.modb_d[l:l + 1, :], wr=["mb"])
        mk.dma(mb[32:33, :], g.modb_d[l:l + 1, :], wr=["mb"])
        mk.dma(nwb[:], g.normw_d[l:l + 1, :].to_broadcast([128, 1024]), wr=["nwb"])
        mk.op("act", lambda e: e.activation(ccol[:], ccol[:], AF.Silu), rd=["ccol"], wr=["ccol"])
        mk.op("pool", lambda e: e.memset(S33[:], 0.0), wr=["S33"])
        mk.op("dve", lambda e: e.tensor_copy(S33[:, :, 0], ccol[:, 0:8]), rd=["ccol"], wr=["S33"])
        mk.op("dve", lambda e: e.tensor_copy(S33[:, :, 32], ccol[:, 8:16]), rd=["ccol"], wr=["S33"])
        for k in range(8):
            m = mw[k % 2]
            mkey = "mw%d" % (k % 2)
            mk.dma(m[:], g.modw_d[l, k * 128:(k + 1) * 128, :], wr=[mkey])
            for n in range(6):
                mk.op("pe", lambda e, m=m, n=n, k=k: e.matmul(pm[0:64, n * 512:(n + 1) * 512], lhsT=S33[:, k, :],
                                                             rhs=m[:, n * 512:(n + 1) * 512], start=(k == 0), stop=(k == 7)),
                      rd=[mkey, "S33"], wr=["pm"])
        for r in (0, 32):
            mk.op("dve", lambda e, r=r: e.tensor_tensor(mrow[r:r + 1, :], pm[r:r + 1, :], mb[r:r + 1, :], ALU.add),
                  rd=["pm", "mb"], wr=["mrow"])
        for ri, r in enumerate((0, 32)):
            for part in range(3):
                dst = names[ri * 3 + (1 if part == 0 else (0 if part == 1 else 2))]
                if dst is None:
                    continue
                for n in range(2):
                    c0 = part * 1024 + n * 512
                    mk.op("pe", lambda e, r=r, c0=c0, n=n: e.matmul(pb[:, n * 512:(n + 1) * 512], lhsT=g.onesf[r:r + 1, :],
                                                                   rhs=mrow[r:r + 1, c0:c0 + 512], start=True, stop=True),
                          rd=["mrow", "onesf"], wr=["pb%d" % n])
                dt, dk = dst
                if part == 1:
                    mk.op("dve", lambda e, dt=dt: e.scalar_tensor_tensor(dt[:], pb[:], 1.0, nwb[:], ALU.add, ALU.mult),
                          rd=["pb0", "pb1", "nwb"], wr=[dk])
                else:
                    mk.op("act", lambda e, dt=dt: e.activation(dt[:], pb[:], AF.Copy), rd=["pb0", "pb1"], wr=[dk])
        mk.barrier()


def norm_transpose(g, l, aT, weff_l, weff_c, lat_src, ctx_src, lat_key):
    mk, nc = g.mk, g.nc
    phase(g, "L%d_norm" % l)
    with contextlib.ExitStack() as st:
        g2 = Ctx(); g2.st = st; g2.nc = nc; g2.mk = mk
        xt = [sb(g2, "xt%d" % i, [128, 1024], F32) for i in range(4)]
        junk = sb(g2, "njunk", [128, 1024], BF16)
        ss = sb(g2, "nss", [128, NT], F32)
        rs = sb(g2, "nrs", [128, NT], F32)
        xn = [sb(g2, "xn%d" % i, [128, 1024], F32) for i in range(2)]
        xb = [sb(g2, "xb%d" % i, [128, 1024], BF16) for i in range(2)]
        pt = [st.enter_context(nc.psum_tensor(uq("npt%d" % i), [128, 1024], BF16)) for i in range(2)]
        for i in range(NT):
            b = i % 2
            x, xk = xt[i % 4], "xt%d" % (i % 4)
            if i < 2:
                mk.dma(x[:], ctx_src[i * 128:(i + 1) * 128, :], rd=["hc1_d"], wr=[xk])
                we, wk, sh, shk = weff_c
            else:
                mk.dma(x[:], lat_src[(i - 2) * 128:(i - 1) * 128, :], rd=[(lat_key, i - 2)], wr=[xk])
                we, wk, sh, shk = weff_l
            mk.op("act", lambda e, x=x, i=i: e.activation(junk[:], x[:], AF.Square, accum_out=ss[:, i:i + 1]),
                  rd=[xk], wr=["njunk", ("nss", i)])
            mk.op("act", lambda e, i=i: e.activation(rs[:, i:i + 1], ss[:, i:i + 1], AF.Sqrt, bias=g.epsb[:], scale=1.0 / D),
                  rd=[("nss", i), "epsb"], wr=[("nrs", i)])
            mk.op("dve", lambda e, i=i: e.reciprocal(rs[:, i:i + 1], rs[:, i:i + 1]), rd=[("nrs", i)], wr=[("nrs", i)])
            mk.op("dve", lambda e, x=x, i=i, b=b, we=we: e.scalar_tensor_tensor(xn[b][:], x[:], rs[:, i:i + 1], we[:], ALU.mult, ALU.mult),
                  rd=[xk, ("nrs", i), wk], wr=["xn%d" % b])
            mk.op("pool", lambda e, b=b, sh=sh: e.tensor_tensor(xb[b][:], xn[b][:], sh[:], ALU.add),
                  rd=["xn%d" % b, shk], wr=["xb%d" % b])
            for k in range(8):
                mk.op("pe", lambda e, b=b, k=k: e.transpose(pt[b][:, k * 128:(k + 1) * 128], xb[b][:, k * 128:(k + 1) * 128], g.identb[:]),
                      rd=["xb%d" % b, "identb"], wr=["npt%d" % b])
            mk.op("act", lambda e, b=b, i=i: e.activation(aT[:, :, i * 128:(i + 1) * 128],
                                                          pt[b][:, :].rearrange("p (k t) -> p k t", k=8), AF.Copy),
                  rd=["npt%d" % b], wr=[("aT", i)])
        mk.barrier()


def rope_tables(g):
    mk, nc = g.mk, g.nc
    g.cs = sb(g, "ropecs", [128, 32, 16], F32)
    g.sn = sb(g, "ropesn", [128, 32, 16], F32)
    with contextlib.ExitStack() as st:
        g2 = Ctx(); g2.st = st; g2.nc = nc; g2.mk = mk
        ri = sb(g2, "rri", [128, 32], I32)
        ci = sb(g2, "rci", [128, 1], I32)
        rf = sb(g2, "rrf", [128, 32], F32)
        cf = sb(g2, "rcf", [128, 1], F32)
        ang = sb(g2, "rang", [128, 32, 16], F32)
        red = sb(g2, "rred", [128, 32, 16], F32)
        K = ["rope"]
        for h in range(2):
            mk.op("pool", lambda e, h=h: e.iota(ri[h * 64:(h + 1) * 64, :], pattern=[[2, 32]], base=h, channel_multiplier=0), wr=K)
            mk.op("pool", lambda e, h=h: e.iota(ci[h * 64:(h + 1) * 64, :], pattern=[[0, 1]], base=0, channel_multiplier=1), wr=K)
        mk.op("dve", lambda e: e.tensor_copy(rf[:], ri[:]), rd=K, wr=K)
        mk.op("dve", lambda e: e.tensor_copy(cf[:], ci[:]), rd=K, wr=K)
        inv = (np.float32(10000.0) ** (-(np.arange(8, dtype=np.float32) / np.float32(8)))).astype(np.float32)
        for i in range(8):
            mk.op("dve", lambda e, i=i: e.tensor_scalar(ang[:, :, i], rf[:], float(inv[i]), None, ALU.mult), rd=K, wr=K)
            mk.op("dve", lambda e, i=i: e.tensor_scalar(ang[:, :, 8 + i], cf[:, 0:1].to_broadcast([128, 32]), float(inv[i]), None, ALU.mult), rd=K, wr=K)
        range_reduce(g2, red[:], ang[:], 0.0, [128, 32, 16], "rope", "rrs")
        mk.op("act", lambda e: e.activation(g.sn[:], red[:], AF.Sin), rd=K, wr=["ropesn"])
        range_reduce(g2, red[:], ang[:], PI / 2, [128, 32, 16], "rope", "rrc")
        mk.op("act", lambda e: e.activation(g.cs[:], red[:], AF.Sin), rd=K, wr=["ropecs"])
        mk.barrier()


def layer1(g):
    mk, nc = g.mk, g.nc
    SCALE = 96.0 ** -0.5
    with contextlib.ExitStack() as st1:
        g1 = Ctx(); g1.st = st1; g1.nc = nc; g1.mk = mk
        gate_l = sb(g1, "L1gate_l", [128, 1024], F32)
        with contextlib.ExitStack() as stB:
            gB = Ctx(); gB.st = stB; gB.nc = nc; gB.mk = mk
            cqnT = sb(gB, "cqnT", [128, 3, TL], BF16)
            ckvnT = sb(gB, "ckvnT", [128, 2, T], BF16)
            krrS = sb(gB, "krrS", [128, NT, 32], BF16)
            with contextlib.ExitStack() as st:
                g2 = Ctx(); g2.st = st; g2.nc = nc; g2.mk = mk
                bc = {}
                for nm in ("weff_l", "sh_l", "weff_c", "sh_c"):
                    bc[nm] = sb(g2, "L1" + nm, [128, 1024], F32)
                modulation(g, 1, [(bc["weff_l"], "weff_l"), (bc["sh_l"], "sh_l"), (gate_l, "gate_l"),
                                  (bc["weff_c"], "weff_c"), (bc["sh_c"], "sh_c"), None])
                aT = sb(g2, "a1T", [128, 8, T], BF16)
                norm_transpose(g, 1, aT, (bc["weff_l"], "weff_l", bc["sh_l"], "sh_l"), (bc["weff_c"], "weff_c", bc["sh_c"], "sh_c"),
                               g.y_d, g.hc1_d, "y")
                phase(g, "L1_B1")
                win = sb(g2, "win1", [128, 8, 1696], BF16)
                load_w(g, win, g.o_w_in, 1024, 1696, "win1")
                qan = sb(g2, "qan", [128, 384], F32)
                kvan = sb(g2, "kvan", [128, 256], F32)
                knr = sb(g2, "knr", [128, 32], F32)
                mk.dma(qan[:], g.o_q_a_norm[0:1, :].to_broadcast([128, 384]), wr=["qan"])
                mk.dma(kvan[:], g.o_kv_a_norm[0:1, :].to_broadcast([128, 256]), wr=["kvan"])
                mk.dma(knr[:], g.o_k_norm[0:1, 64:96].to_broadcast([128, 32]), wr=["knr"])
                inv3 = sb(g2, "inv3", [128, 3], F32)
                mk.op("pool", lambda e: e.memset(inv3[:, 0:1], 1.0 / 384), wr=["inv3"])
                mk.op("pool", lambda e: e.memset(inv3[:, 1:2], 1.0 / 256), wr=["inv3"])
                mk.op("pool", lambda e: e.memset(inv3[:, 2:3], 1.0 / 32), wr=["inv3"])
                pA_ = [st.enter_context(nc.psum_tensor(uq("pA"), [128, 1024], F32)) for _ in range(2)]
                pT_ = [st.enter_context(nc.psum_tensor(uq("pT"), [128, 1024], BF16)) for _ in range(2)]
                junk_ = [sb(g2, "bjunk", [128, 672], F32) for _ in range(2)]
                ss3_ = [sb(g2, "ss3", [128, 3], F32) for _ in range(2)]
                rs3_ = [sb(g2, "rs3", [128, 3], F32) for _ in range(2)]
                cqn_ = [sb(g2, "cqn", [128, 640], BF16) for _ in range(2)]
                krn_ = [sb(g2, "krn", [128, 32], F32) for _ in range(2)]
                krA_ = [sb(g2, "krA", [128, 32], F32) for _ in range(2)]
                krB_ = [sb(g2, "krB", [128, 32], F32) for _ in range(2)]
                for i in range(NT):
                    lat = i >= 2
                    lt = i - 2
                    tok = slice(i * 128, (i + 1) * 128)
                    b2 = i % 2
                    pA, pT, junk, ss3, rs3, cqn, krn, krA, krB = pA_[b2], pT_[b2], junk_[b2], ss3_[b2], rs3_[b2], cqn_[b2], krn_[b2], krA_[b2], krB_[b2]
                    for k in range(8):
                        mk.op("pe", lambda e, k=k: e.matmul(pA[:, 0:512], lhsT=aT[:, k, tok], rhs=win[:, k, 0:512], start=(k == 0), stop=(k == 7)),
                              rd=[("aT", i), "win1"], wr=[("pA0", b2)])
                        mk.op("pe", lambda e, k=k: e.matmul(pA[:, 512:672], lhsT=aT[:, k, tok], rhs=win[:, k, 512:672], start=(k == 0), stop=(k == 7)),
                              rd=[("aT", i), "win1"], wr=[("pA1", b2)])
                    for j, (a, b_) in enumerate(((0, 384), (384, 640), (640, 672))):
                        mk.op("act", lambda e, a=a, b_=b_, j=j: e.activation(junk[:, a:b_], pA[:, a:b_], AF.Square, accum_out=ss3[:, j:j + 1]),
                              rd=[("pA0", b2), ("pA1", b2)], wr=[("bjunk", b2), ("ss3", b2)])
                    mk.op("dve", lambda e: e.tensor_tensor(rs3[:], ss3[:], inv3[:], ALU.mult), rd=[("ss3", b2), "inv3"], wr=[("rs3", b2)])
                    mk.op("act", lambda e: e.activation(rs3[:], rs3[:], AF.Sqrt, bias=g.epsb[:]), rd=[("rs3", b2), "epsb"], wr=[("rs3", b2)])
                    mk.op("dve", lambda e: e.reciprocal(rs3[:], rs3[:]), rd=[("rs3", b2)], wr=[("rs3", b2)])
                    if lat:
                        mk.op("dve", lambda e: e.scalar_tensor_tensor(cqn[:, 0:384], pA[:, 0:384], rs3[:, 0:1], qan[:], ALU.mult, ALU.mult),
                              rd=[("pA0", b2), ("rs3", b2), "qan"], wr=[("cqn", b2)])
                    mk.op("dve", lambda e: e.scalar_tensor_tensor(cqn[:, 384:640], pA[:, 384:640], rs3[:, 1:2], kvan[:], ALU.mult, ALU.mult),
                          rd=[("pA0", b2), ("pA1", b2), ("rs3", b2), "kvan"], wr=[("cqn", b2)])
                    if lat:
                        mk.op("dve", lambda e: e.scalar_tensor_tensor(krn[:], pA[:, 640:672], rs3[:, 2:3], knr[:], ALU.mult, ALU.mult),
                              rd=[("pA1", b2), ("rs3", b2), "knr"], wr=[("krn", b2)])
                        krv = krn[:, :].rearrange("p (a b) -> p a b", a=2)
                        cosb = g.cs[:, lt, :].unsqueeze(1).to_broadcast([128, 2, 16])
                        sinb = g.sn[:, lt, :].unsqueeze(1).to_broadcast([128, 2, 16])
                        mk.op("dve", lambda e: e.tensor_tensor(krA[:, :].rearrange("p (a b) -> p a b", a=2), krv, cosb, ALU.mult),
                              rd=[("krn", b2), "ropecs"], wr=[("krA", b2)])
                        mk.op("dve", lambda e: e.tensor_tensor(krB[:, :].rearrange("p (a b) -> p a b", a=2), krv, sinb, ALU.mult),
                              rd=[("krn", b2), "ropesn"], wr=[("krB", b2)])
                        mk.op("dve", lambda e: e.tensor_tensor(krrS[:, i, 0:16], krA[:, 0:16], krB[:, 16:32], ALU.subtract), rd=[("krA", b2), ("krB", b2)], wr=[("krr", i)])
                        mk.op("dve", lambda e: e.tensor_tensor(krrS[:, i, 16:32], krB[:, 0:16], krA[:, 16:32], ALU.add), rd=[("krA", b2), ("krB", b2)], wr=[("krr", i)])
                    else:
                        mk.op("dve", lambda e: e.scalar_tensor_tensor(krrS[:, i, :], pA[:, 640:672], rs3[:, 2:3], knr[:], ALU.mult, ALU.mult),
                              rd=[("pA1", b2), ("rs3", b2), "knr"], wr=[("krr", i)])
                    c0 = 0 if lat else 3
                    for j in range(c0, 5):
                        mk.op("pe", lambda e, j=j: e.transpose(pT[:, j * 128:(j + 1) * 128], cqn[:, j * 128:(j + 1) * 128], g.identb[:]),
                              rd=[("cqn", b2), "identb"], wr=[("pT", b2)])
                    if lat:
                        mk.op("act", lambda e: e.activation(cqnT[:, :, lt * 128:(lt + 1) * 128], pT[:, 0:384].rearrange("p (k t) -> p k t", k=3), AF.Copy),
                              rd=[("pT", b2)], wr=[("cqnT", lt)])
                    mk.op("act", lambda e: e.activation(ckvnT[:, :, tok], pT[:, 384:640].rearrange("p (k t) -> p k t", k=2), AF.Copy),
                          rd=[("pT", b2)], wr=[("ckvnT", i)])
                mk.barrier()
                pA = pA_[0]
                phase(g, "L1_B3")
                sgs = [sb(g2, "sgs%d" % i, [128, 512], BF16) for i in range(2)]
                n3 = 0
                for m in range(8):
                    for blk in range(8):
                        pk = "pA%d" % (n3 % 2)
                        pg = pA[:, (n3 % 2) * 512:(n3 % 2 + 1) * 512]
                        for k in range(8):
                            mk.op("pe", lambda e, k=k, m=m, blk=blk, pg=pg: e.matmul(pg, lhsT=win[:, k, 672 + m * 128:672 + (m + 1) * 128],
                                                                                   rhs=aT[:, k, 256 + blk * 512:256 + (blk + 1) * 512],
                                                                                   start=(k == 0), stop=(k == 7)),
                                  rd=["win1"] + [("aT", 2 + blk * 4 + j) for j in range(4)], wr=[pk])
                        s_ = sgs[n3 % 2]
                        sk_ = "sgs%d" % (n3 % 2)
                        mk.op("act", lambda e, s_=s_, pg=pg: e.activation(s_[:], pg, AF.Silu), rd=[pk], wr=[sk_])
                        mk.dma(g.sg_d[m * 128:(m + 1) * 128, blk * 512:(blk + 1) * 512], s_[:], rd=[sk_], wr=["sg_d"])
                        n3 += 1
                mk.barrier()
            with contextlib.ExitStack() as st:
                g2 = Ctx(); g2.st = st; g2.nc = nc; g2.mk = mk
                phase(g, "L1_B2")
                wuq = sb(g2, "wuq", [128, 3, 1536], BF16)
                wukv = sb(g2, "wukv", [128, 2, 2048], BF16)
                load_w(g, wuq, g.o_w_uq, 384, 1536, "wuq")
                load_w(g, wukv, g.o_w_ukv, 256, 2048, "wukv")
                wq = sb(g2, "wqbc", [128, 16, 96], F32)
                wk = sb(g2, "wkbc", [128, 16, 64], F32)
                t96 = sb(g2, "t96", [128, 96], F32)
                t64 = sb(g2, "t64", [128, 64], F32)
                mk.dma(t96[:], g.o_q_norm[0:1, :].to_broadcast([128, 96]), wr=["t96"])
                mk.dma(t64[:], g.o_k_norm[0:1, 0:64].to_broadcast([128, 64]), wr=["t64"])
                mk.op("dve", lambda e: e.tensor_scalar(t96[:], t96[:], SCALE, None, ALU.mult), rd=["t96"], wr=["t96"])
                mk.op("dve", lambda e: e.tensor_copy(wq[:], t96[:, :].unsqueeze(1).to_broadcast([128, 16, 96])), rd=["t96"], wr=["wqbc"])
                mk.op("dve", lambda e: e.tensor_copy(wk[:], t64[:, :].unsqueeze(1).to_broadcast([128, 16, 64])), rd=["t64"], wr=["wkbc"])
                invq = sb(g2, "invq", [128, 32], F32)
                mk.op("pool", lambda e: e.memset(invq[:, 0:16], 1.0 / 64), wr=["invq"])
                mk.op("pool", lambda e: e.memset(invq[:, 16:32], 1.0 / 32), wr=["invq"])
                pT = st.enter_context(nc.psum_tensor(uq("pT2"), [128, 1024], BF16))
                pQ = st.enter_context(nc.psum_tensor(uq("pQ"), [128, 1536], F32))
                pKV = st.enter_context(nc.psum_tensor(uq("pKV"), [128, 2048], F32))
                sqk_t = sb(g2, "sqk_t", [128, 1024], F32)
                tk_t = sb(g2, "tk_t", [128, 1024], F32)
                sqq = sb(g2, "sqq", [128, 1536], F32)
                ssq = sb(g2, "ssq", [128, 32], F32)
                rsq = sb(g2, "rsq", [128, 32], F32)
                tq = sb(g2, "tq", [128, 1536], F32)
                qr = sb(g2, "qr", [128, 16, 32], F32)
                qA = sb(g2, "qA", [128, 16, 32], F32)
                qB = sb(g2, "qB", [128, 16, 32], F32)
                qf = sb(g2, "qf", [128, 16, 96], BF16)
                kf = sb(g2, "kf", [128, 16, 96], BF16)
                ssk = sb(g2, "ssk", [128, 16], F32)
                rsk = sb(g2, "rsk", [128, 16], F32)
                vst = [sb(g2, "vst%d" % i, [128, 1024], BF16) for i in range(2)]
                qst = [sb(g2, "qst%d" % i, [96, 16, 256], BF16) for i in range(2)]
                kst = [sb(g2, "kst%d" % i, [96, 16, 256], BF16) for i in range(2)]
                tqv = tq[:, :].rearrange("p (h d) -> p h d", h=16)
                sqv = sqq[:, :].rearrange("p (h d) -> p h d", h=16)
                qT_v = g.qT_d.rearrange("h d t -> d h t")
                kT_v = g.kT_d.rearrange("h d t -> d h t")
                for i in range(NT):
                    lat = i >= 2
                    lt = i - 2
                    tok = slice(i * 128, (i + 1) * 128)
                    for n in range(4):
                        for k in range(2):
                            mk.op("pe", lambda e, n=n, k=k: e.matmul(pKV[:, n * 512:(n + 1) * 512], lhsT=ckvnT[:, k, tok],
                                                                     rhs=wukv[:, k, n * 512:(n + 1) * 512], start=(k == 0), stop=(k == 1)),
                                  rd=[("ckvnT", i), "wukv"], wr=["pKV"])
                    if lat:
                        for n in range(3):
                            for k in range(3):
                                mk.op("pe", lambda e, n=n, k=k: e.matmul(pQ[:, n * 512:(n + 1) * 512], lhsT=cqnT[:, k, lt * 128:(lt + 1) * 128],
                                                                         rhs=wuq[:, k, n * 512:(n + 1) * 512], start=(k == 0), stop=(k == 2)),
                                      rd=[("cqnT", lt), "wuq"], wr=["pQ"])
                        mk.op("act", lambda e: e.activation(sqq[:], pQ[:, 0:1536], AF.Square), rd=["pQ"], wr=["sqq"])
                        mk.op("dve", lambda e: e.tensor_reduce(ssq[:, 0:16], sqv[:, :, 0:64], AX.X, ALU.add), rd=["sqq"], wr=["ssq"])
                        mk.op("dve", lambda e: e.tensor_reduce(ssq[:, 16:32], sqv[:, :, 64:96], AX.X, ALU.add), rd=["sqq"], wr=["ssq"])
                        mk.op("dve", lambda e: e.tensor_tensor(rsq[:], ssq[:], invq[:], ALU.mult), rd=["ssq", "invq"], wr=["rsq"])
                        mk.op("act", lambda e: e.activation(rsq[:], rsq[:], AF.Sqrt, bias=g.epsb[:]), rd=["rsq", "epsb"], wr=["rsq"])
                        mk.op("dve", lambda e: e.reciprocal(rsq[:], rsq[:]), rd=["rsq"], wr=["rsq"])
                        mk.op("dve", lambda e: e.tensor_tensor(tq[:], pQ[:, 0:1536], wq[:, :, :].rearrange("p h d -> p (h d)"), ALU.mult),
                              rd=["pQ", "wqbc"], wr=["tq"])
                        mk.op("pool", lambda e: e.tensor_tensor(qf[:, :, 0:64], tqv[:, :, 0:64], rsq[:, 0:16].unsqueeze(2).to_broadcast([128, 16, 64]), ALU.mult),
                              rd=["tq", "rsq"], wr=["qf"])
                        mk.op("dve", lambda e: e.tensor_tensor(qr[:], tqv[:, :, 64:96], rsq[:, 16:32].unsqueeze(2).to_broadcast([128, 16, 32]), ALU.mult),
                              rd=["tq", "rsq"], wr=["qr"])
                        qrv = qr[:, :, :].rearrange("p h (a b) -> p h a b", a=2)
                        cos4 = g.cs[:, lt, :].unsqueeze(1).unsqueeze(1).to_broadcast([128, 16, 2, 16])
                        sin4 = g.sn[:, lt, :].unsqueeze(1).unsqueeze(1).to_broadcast([128, 16, 2, 16])
                        mk.op("dve", lambda e: e.tensor_tensor(qA[:, :, :].rearrange("p h (a b) -> p h a b", a=2), qrv, cos4, ALU.mult),
                              rd=["qr", "ropecs"], wr=["qA"])
                        mk.op("pool", lambda e: e.tensor_tensor(qB[:, :, :].rearrange("p h (a b) -> p h a b", a=2), qrv, sin4, ALU.mult),
                              rd=["qr", "ropesn"], wr=["qB"])
                        mk.op("dve", lambda e: e.tensor_tensor(qf[:, :, 64:80], qA[:, :, 0:16], qB[:, :, 16:32], ALU.subtract), rd=["qA", "qB"], wr=["qf"])
                        mk.op("dve", lambda e: e.tensor_tensor(qf[:, :, 80:96], qB[:, :, 0:16], qA[:, :, 16:32], ALU.add), rd=["qA", "qB"], wr=["qf"])
                    kvv = pKV[:, :].rearrange("p (h d) -> p h d", h=16)
                    sqk = sqk_t[:, :].rearrange("p (h d) -> p h d", h=16)
                    tkv = tk_t[:, :].rearrange("p (h d) -> p h d", h=16)
                    mk.op("act", lambda e: e.activation(sqk, kvv[:, :, 0:64], AF.Square), rd=["pKV"], wr=["sqk_t"])
                    mk.op("dve", lambda e: e.tensor_reduce(ssk[:], sqk, AX.X, ALU.add), rd=["sqk_t"], wr=["ssk"])
                    mk.op("dve", lambda e: e.tensor_scalar(rsk[:], ssk[:], 1.0 / 64, None, ALU.mult), rd=["ssk"], wr=["rsk"])
                    mk.op("act", lambda e: e.activation(rsk[:], rsk[:], AF.Sqrt, bias=g.epsb[:]), rd=["rsk", "epsb"], wr=["rsk"])
                    mk.op("dve", lambda e: e.reciprocal(rsk[:], rsk[:]), rd=["rsk"], wr=["rsk"])
                    mk.op("dve", lambda e: e.tensor_tensor(tkv, kvv[:, :, 0:64], wk[:], ALU.mult), rd=["pKV", "wkbc"], wr=["tk_t"])
                    mk.op("pool", lambda e: e.tensor_tensor(kf[:, :, 0:64], tkv, rsk[:, :].unsqueeze(2).to_broadcast([128, 16, 64]), ALU.mult),
                          rd=["tk_t", "rsk"], wr=["kf"])
                    mk.op("pool", lambda e: e.tensor_copy(kf[:, :, 64:96], krrS[:, i, :].unsqueeze(1).to_broadcast([128, 16, 32])), rd=[("krr", i)], wr=["kf"])
                    vs = vst[i % 2]
                    vk = "vst%d" % (i % 2)
                    mk.op("act", lambda e, vs=vs: e.activation(vs[:, :].rearrange("p (h d) -> p h d", h=16), kvv[:, :, 64:128], AF.Copy),
                          rd=["pKV"], wr=[vk])
                    mk.dma(g.v_d[i * 128:(i + 1) * 128, :], vs[:], rd=[vk], wr=["v_d"])
                    if lat:
                        qs = qst[(lt // 2) % 2]
                        qsk = "qst%d" % ((lt // 2) % 2)
                        slot = lt % 2
                        for gi in range(2):
                            for hh in range(8):
                                mk.op("pe", lambda e, gi=gi, hh=hh: e.transpose(pT[0:96, hh * 128:(hh + 1) * 128], qf[:, gi * 8 + hh, :], g.identb[:]),
                                      rd=["qf", "identb"], wr=["pT"])
                            mk.op("act", lambda e, gi=gi, qs=qs, slot=slot: e.activation(
                                qs[:, gi * 8:(gi + 1) * 8, slot * 128:(slot + 1) * 128],
                                pT[0:96, :].rearrange("p (h t) -> p h t", h=8), AF.Copy), rd=["pT"], wr=[qsk])
                        if slot == 1:
                            t0 = (lt // 2) * 256
                            mk.dma(qT_v[:, :, t0:t0 + 256], qs[:, :, :], rd=[qsk], wr=["qT_d"])
                    ks = kst[(i // 2) % 2]
                    ksk = "kst%d" % ((i // 2) % 2)
                    kslot = i % 2
                    for gi in range(2):
                        for hh in range(8):
                            mk.op("pe", lambda e, gi=gi, hh=hh: e.transpose(pT[0:96, hh * 128:(hh + 1) * 128], kf[:, gi * 8 + hh, :], g.identb[:]),
                                  rd=["kf", "identb"], wr=["pT"])
                        mk.op("act", lambda e, gi=gi, ks=ks, kslot=kslot: e.activation(
                            ks[:, gi * 8:(gi + 1) * 8, kslot * 128:(kslot + 1) * 128],
                            pT[0:96, :].rearrange("p (h t) -> p h t", h=8), AF.Copy), rd=["pT"], wr=[ksk])
                    if kslot == 1:
                        t0 = (i // 2) * 256
                        mk.dma(kT_v[:, :, t0:t0 + 256], ks[:, :, :], rd=[ksk], wr=["kT_d"])
                mk.barrier()
        with contextlib.ExitStack() as stC:
            gC = Ctx(); gC.st = stC; gC.nc = nc; gC.mk = mk
            phase(g, "L1_C")
            mixT = sb(gC, "mixT", [128, 8, TL], BF16)
            with contextlib.ExitStack() as st:
                g2 = Ctx(); g2.st = st; g2.nc = nc; g2.mk = mk
                qTh = [sb(g2, "qTh%d" % i, [96, TL], BF16) for i in range(2)]
                kTh = [sb(g2, "kTh%d" % i, [96, T], BF16) for i in range(2)]
                vh = [sb(g2, "vh%d" % i, [128, NT, 128], BF16) for i in range(2)]
                sgh = sb(g2, "sgh", [128, TL], BF16)
                NPX = 4
                pex = [sb(g2, "pex%d" % i, [128, 1024], BF16) for i in range(NPX)]
                rn = sb(g2, "rn", [128, 512], F32)
                at = sb(g2, "at", [128, 512], F32)
                pS = [st.enter_context(nc.psum_tensor(uq("pS%d" % i), [128, 1024], F32)) for i in range(3)]
                pO = [st.enter_context(nc.psum_tensor(uq("pO%d" % i), [128, 512], F32)) for i in range(2)]
                mk.op("pool", lambda e: e.memset(vh[0][:, :, 64:128], 1.0), wr=["vh0"])
                mk.op("pool", lambda e: e.memset(vh[1][:, :, 0:64], 1.0), wr=["vh1"])
                v_v = g.v_d.rearrange("(t p) d -> p t d", p=128)
                LAG = 2

                def load_head(h):
                    b = h % 2
                    lo, hi = (0, 64) if b == 0 else (64, 128)
                    mk.dma(qTh[b][:], g.qT_d[h], rd=["qT_d"], wr=["qTh%d" % b])
                    mk.dma(kTh[b][:], g.kT_d[h], rd=["kT_d"], wr=["kTh%d" % b])
                    mk.dma(vh[b][:, :, lo:hi], v_v[:, :, h * 64:(h + 1) * 64], rd=["v_d"], wr=["vh%d" % b])
                    mk.dma(sgh[lo:hi, :], g.sg_d[h * 64:(h + 1) * 64, :], rd=["sg_d"], wr=["sgh%d" % b])

                units = [(h, qb, kg) for h in range(NH) for qb in range(8) for kg in range(17)]

                def emit_qk_exp(u, idx):
                    h, qb, kg = u
                    b = h % 2
                    ps = pS[idx % 3]
                    psk = "pS%d" % (idx % 3)
                    qsl = slice(qb * 512, (qb + 1) * 512)
                    for j in range(2):
                        kt = kg * 2 + j
                        mk.op("pe", lambda e, ps=ps, j=j, kt=kt, b=b, qsl=qsl: e.matmul(ps[:, j * 512:(j + 1) * 512], lhsT=kTh[b][:, kt * 128:(kt + 1) * 128],
                                                                                       rhs=qTh[b][:, qsl], start=True, stop=True),
                              rd=["kTh%d" % b, "qTh%d" % b], wr=[psk])
                    px = pex[idx % NPX]
                    pxk = "pex%d" % (idx % NPX)
                    mk.op("act", lambda e, px=px, ps=ps: e.activation(px[:], ps[:], AF.Exp), rd=[psk], wr=[pxk])

                def emit_pv(u, idx):
                    h, qb, kg = u
                    b = h % 2
                    lo, hi = (0, 64) if b == 0 else (64, 128)
                    dlo, dhi = (64, 128) if b == 0 else (0, 64)
                    gq = h * 8 + qb
                    po = pO[gq % 2]
                    pok = "pO%d" % (gq % 2)
                    px = pex[idx % NPX]
                    pxk = "pex%d" % (idx % NPX)
                    qsl = slice(qb * 512, (qb + 1) * 512)
                    for j in range(2):
                        kt = kg * 2 + j
                        mk.op("pe", lambda e, px=px, j=j, kt=kt, b=b, po=po: e.matmul(po[:], lhsT=vh[b][:, kt, :], rhs=px[:, j * 512:(j + 1) * 512],
                                                                                     start=(kt == 0), stop=(kt == NT - 1)),
                              rd=[pxk, "vh%d" % b], wr=[pok])
                    if kg == 16:
                        mk.op("dve", lambda e: e.reciprocal(rn[lo:hi, :], po[dlo:dhi, :]), rd=[pok], wr=["rn"])
                        mk.op("dve", lambda e: e.tensor_tensor(at[lo:hi, :], po[lo:hi, :], rn[lo:hi, :], ALU.mult), rd=[pok, "rn"], wr=["at"])
                        mk.op("pool", lambda e: e.tensor_tensor(mixT[lo:hi, h // 2, qsl], at[lo:hi, :], sgh[lo:hi, qsl], ALU.mult),
                              rd=["at", "sgh%d" % b], wr=[("mixT", qb)])

                load_head(0)
                load_head(1)
                for idx, u in enumerate(units):
                    emit_qk_exp(u, idx)
                    if idx >= LAG:
                        up = units[idx - LAG]
                        emit_pv(up, idx - LAG)
                        if up[1] == 7 and up[2] == 16 and up[0] + 2 < NH:
                            load_head(up[0] + 2)
                for idx in range(len(units) - LAG, len(units)):
                    emit_pv(units[idx], idx)
                mk.barrier()
            with contextlib.ExitStack() as st:
                g2 = Ctx(); g2.st = st; g2.nc = nc; g2.mk = mk
                phase(g, "L1_D")
                wout = sb(g2, "wout1", [128, 8, 1024], BF16)
                load_w(g, wout, g.o_w_out, 1024, 1024, "wout1")
                ht = [sb(g2, "ht%d" % i, [128, 1024], F32) for i in range(4)]
                ot = [sb(g2, "ot%d" % i, [128, 1024], F32) for i in range(4)]
                pD = [st.enter_context(nc.psum_tensor(uq("pD%d" % i), [128, 1024], F32)) for i in range(2)]
                for lt in range(32):
                    b = lt % 2
                    tok = slice(lt * 128, (lt + 1) * 128)
                    hb = ht[lt % 4]
                    hbk = "ht%d" % (lt % 4)
                    mk.dma(hb[:], g.y_d[tok, :], rd=[("y", lt)], wr=[hbk])
                    for n in range(2):
                        for k in range(8):
                            mk.op("pe", lambda e, n=n, k=k: e.matmul(pD[b][:, n * 512:(n + 1) * 512], lhsT=mixT[:, k, tok], rhs=wout[:, k, n * 512:(n + 1) * 512],
                                                                     start=(k == 0), stop=(k == 7)),
                                  rd=[("mixT", lt // 4), "wout1"], wr=["pD%d" % b])
                    mk.op("dve", lambda e: e.tensor_tensor(ot[lt % 4][:], pD[b][:], gate_l[:], ALU.mult), rd=["pD%d" % b, "gate_l"], wr=["ot%d" % (lt % 4)])
                    mk.op("pool", lambda e, hb=hb: e.tensor_tensor(ot[lt % 4][:], ot[lt % 4][:], hb[:], ALU.add), rd=["ot%d" % (lt % 4), hbk], wr=["ot%d" % (lt % 4)])
                    mk.dma(g.y_d[tok, :], ot[lt % 4][:], rd=["ot%d" % (lt % 4)], wr=[("y", lt)])
                mk.barrier()


def layer0(g):
    mk, nc = g.mk, g.nc
    NS = 544
    with contextlib.ExitStack() as st0:
        g0 = Ctx(); g0.st = st0; g0.nc = nc; g0.mk = mk
        gate_l = sb(g0, "L0gate_l", [128, 1024], F32)
        gate_c = sb(g0, "L0gate_c", [128, 1024], F32)
        with contextlib.ExitStack() as stA:
            gA = Ctx(); gA.st = stA; gA.nc = nc; gA.mk = mk
            aT = sb(gA, "a0T", [128, 8, T], BF16)
            with contextlib.ExitStack() as st:
                g2 = Ctx(); g2.st = st; g2.nc = nc; g2.mk = mk
                bc = {}
                for nm in ("weff_l", "sh_l", "weff_c", "sh_c"):
                    bc[nm] = sb(g2, "L0" + nm, [128, 1024], F32)
                modulation(g, 0, [(bc["weff_l"], "weff_l"), (bc["sh_l"], "sh_l"), (gate_l, "gate_l0"),
                                  (bc["weff_c"], "weff_c"), (bc["sh_c"], "sh_c"), (gate_c, "gate_c0")])
                norm_transpose(g, 0, aT, (bc["weff_l"], "weff_l", bc["sh_l"], "sh_l"), (bc["weff_c"], "weff_c", bc["sh_c"], "sh_c"),
                               g.x_d, g.ctx_d, "xin")
            blocks = [(0, 256)] + [(256 + 512 * b, 512) for b in range(8)]
            def atk(t0, w):
                return [("aT", i) for i in range(t0 // 128, (t0 + w) // 128)]
            with contextlib.ExitStack() as st:
                g2 = Ctx(); g2.st = st; g2.nc = nc; g2.mk = mk
                phase(g, "L0_A1")
                wA = sb(g2, "wA", [128, 8, 1024], BF16)
                load_w(g, wA, g.e_w_in[:, 2048:3072], 1024, 1024, "wA")
                X = sb(g2, "X", [128, 32, 8, 16], BF16)
                ust = [sb(g2, "ust%d" % i, [128, 32, 128], BF16) for i in range(2)]
                J = sb(g2, "J", [128, 128], BF16)
                jt = sb(g2, "jt", [128, 128], F32)
                mk.op("pool", lambda e: e.memset(jt[:], 1.0), wr=["jt"])
                mk.op("pool", lambda e: e.affine_select(jt[:], jt[:], pattern=[[1, 128]], compare_op=ALU.is_equal, fill=0.0, base=-127, channel_multiplier=1),
                      rd=["jt"], wr=["jt"])
                mk.op("pool", lambda e: e.tensor_copy(J[:], jt[:]), rd=["jt"], wr=["J"])
                pX = [st.enter_context(nc.psum_tensor(uq("pX%d" % i), [128, 512], F32)) for i in range(2)]
                pU = [st.enter_context(nc.psum_tensor(uq("pU%d" % i), [128, 512], F32)) for i in range(2)]
                sgs = [sb(g2, "zsg%d" % i, [128, 512], BF16) for i in range(2)]
                Uv = g.U_d.rearrange("p (g n) -> p g n", g=32)
                Ubv = g.Ub_d.rearrange("p (g n) -> p g n", g=32)
                ublocks = [(0, 32, 0, 0)] + [(256 + 1024 * b, 128, 32 + 128 * b, 416 - 128 * b) for b in range(4)]
                nx = 0
                nu = 0
                for (tb, nb, nbase, rbase) in ublocks:
                    for j in range(8):
                        px = pX[nx % 2]
                        pxk = "pX%d" % (nx % 2)
                        nx += 1
                        for k in range(8):
                            mk.op("pe", lambda e, k=k, j=j, px=px: e.matmul(px[0:nb, :], lhsT=aT[:, k, tb + j:tb + 8 * nb:8], rhs=wA[:, k, 0:512],
                                                                           start=(k == 0), stop=(k == 7)),
                                  rd=atk(tb, 8 * nb) + ["wA"], wr=[pxk])
                        eng = "act" if j % 2 == 0 else "dve"
                        if eng == "act":
                            mk.op("act", lambda e, j=j, px=px: e.activation(X[0:nb, :, j, :], px[0:nb, :].rearrange("p (g c) -> p g c", g=32), AF.Copy), rd=[pxk], wr=["X"])
                        else:
                            mk.op("dve", lambda e, j=j, px=px: e.tensor_copy(X[0:nb, :, j, :], px[0:nb, :].rearrange("p (g c) -> p g c", g=32)), rd=[pxk], wr=["X"])
                    for rev in range(2):
                        us_ = ust[rev]
                        usk = "ust%d" % rev
                        rhs = g.identb[0:nb, 0:nb] if rev == 0 else J[0:nb, 128 - nb:128]
                        for g4 in range(8):
                            pu = pU[nu % 2]
                            puk = "pU%d" % (nu % 2)
                            nu += 1
                            for gg in range(4):
                                gi = g4 * 4 + gg
                                mk.op("pe", lambda e, gi=gi, gg=gg, pu=pu, rhs=rhs: e.matmul(pu[:, gg * nb:(gg + 1) * nb], lhsT=X[0:nb, gi, :, :].rearrange("p j c -> p (j c)"),
                                                                                          rhs=rhs, start=True, stop=True),
                                      rd=["X", "identb", "J"], wr=[puk])
                            eng = "act" if g4 % 2 == 0 else "dve"
                            src = pu[:, 0:4 * nb].rearrange("p (a n) -> p a n", a=4)
                            if eng == "act":
                                mk.op("act", lambda e, us_=us_, g4=g4, src=src: e.activation(us_[:, g4 * 4:(g4 + 1) * 4, 0:nb], src, AF.Copy), rd=[puk], wr=[usk])
                            else:
                                mk.op("dve", lambda e, us_=us_, g4=g4, src=src: e.tensor_copy(us_[:, g4 * 4:(g4 + 1) * 4, 0:nb], src), rd=[puk], wr=[usk])
                        if rev == 0:
                            mk.dma(Uv[:, :, nbase:nbase + nb], us_[:, :, 0:nb], rd=[usk], wr=["U_d"])
                        else:
                            mk.dma(Ubv[:, :, rbase:rbase + nb], us_[:, :, 0:nb], rd=[usk], wr=["Ub_d"])
                n3 = 0
                for m in range(4):
                    for (t0, w) in blocks:
                        px = pX[n3 % 2]
                        pxk = "pX%d" % (n3 % 2)
                        for k in range(8):
                            mk.op("pe", lambda e, k=k, m=m, px=px, t0=t0, w=w: e.matmul(px[:, 0:w], lhsT=wA[:, k, 512 + m * 128:512 + (m + 1) * 128],
                                                                                     rhs=aT[:, k, t0:t0 + w], start=(k == 0), stop=(k == 7)),
                                  rd=atk(t0, w) + ["wA"], wr=[pxk])
                        s_ = sgs[n3 % 2]
                        sk_ = "zsg%d" % (n3 % 2)
                        mk.op("act", lambda e, s_=s_, px=px, w=w: e.activation(s_[:, 0:w], px[:, 0:w], AF.Silu), rd=[pxk], wr=[sk_])
                        mk.dma(g.szs_d[m * 128:(m + 1) * 128, t0:t0 + w], s_[:, 0:w], rd=[sk_], wr=["szs_d"])
                        n3 += 1
                mk.barrier()
            with contextlib.ExitStack() as st:
                g2 = Ctx(); g2.st = st; g2.nc = nc; g2.mk = mk
                phase(g, "L0_A2")
                wB = sb(g2, "wB", [128, 8, 2048], BF16)
                load_w(g, wB, g.e_w_in[:, 0:2048], 1024, 2048, "wB")
                cw = sb(g2, "cw", [128, 4, 3], F32)
                for j3 in range(3):
                    mk.dma(cw[:, :, j3], g.e_conv_w[j3, :].rearrange("(c p) -> p c", p=128), wr=["cw"], allow_slow_non_contiguous=True)
                vbuf = sb(g2, "vbuf", [128, T + 4], F32)
                cv = sb(g2, "cv", [128, T], F32)
                xs = sb(g2, "xs", [128, 512], F32)
                sz = sb(g2, "sz", [128, 512], F32)
                t1 = sb(g2, "t1", [128, 512], F32)
                mst = [sb(g2, "mst%d" % i, [128, 512], BF16) for i in range(2)]
                pc = [st.enter_context(nc.psum_tensor(uq("pc%d" % i), [128, 512], F32)) for i in range(4)]
                mk.op("pool", lambda e: e.memset(vbuf[:], 0.0), wr=["vbuf"])
                def pos(t0):
                    return t0 + 1 if t0 < 256 else t0 + 3
                nm = 0
                for fc in range(4):
                    for bi, (t0, w) in enumerate(blocks):
                        pa, pb_ = pc[(bi % 2) * 2], pc[(bi % 2) * 2 + 1]
                        pak, pbk = "pc%d" % ((bi % 2) * 2), "pc%d" % ((bi % 2) * 2 + 1)
                        for k in range(8):
                            mk.op("pe", lambda e, k=k, pa=pa, t0=t0, w=w: e.matmul(pa[:, 0:w], lhsT=wB[:, k, fc * 128:(fc + 1) * 128], rhs=aT[:, k, t0:t0 + w],
                                                                                 start=(k == 0), stop=(k == 7)), rd=atk(t0, w) + ["wB"], wr=[pak])
                        for k in range(8):
                            mk.op("pe", lambda e, k=k, pb_=pb_, t0=t0, w=w: e.matmul(pb_[:, 0:w], lhsT=wB[:, k, 1024 + fc * 128:1024 + (fc + 1) * 128], rhs=aT[:, k, t0:t0 + w],
                                                                                   start=(k == 0), stop=(k == 7)), rd=atk(t0, w) + ["wB"], wr=[pbk])
                        mk.op("act", lambda e, pa=pa, w=w: e.activation(xs[:, 0:w], pa[:, 0:w], AF.Copy), rd=[pak], wr=["xs"])
                        p0 = pos(t0)
                        mk.op("dve", lambda e, pb_=pb_, w=w, p0=p0: e.tensor_tensor(vbuf[:, p0:p0 + w], pb_[:, 0:w], xs[:, 0:w], ALU.mult),
                              rd=[pbk, "xs"], wr=["vbuf"])
                    for (a, b_) in ((0, 256), (256, T)):
                        pa0 = pos(a)
                        L = b_ - a
                        mk.op("dve", lambda e, a=a, b_=b_, pa0=pa0, L=L: e.tensor_scalar(cv[:, a:b_], vbuf[:, pa0:pa0 + L], cw[:, fc, 1:2], None, ALU.mult),
                              rd=["vbuf", "cw"], wr=["cv"])
                        mk.op("dve", lambda e, a=a, b_=b_, pa0=pa0, L=L: e.scalar_tensor_tensor(cv[:, a:b_], vbuf[:, pa0 - 1:pa0 - 1 + L], cw[:, fc, 0:1], cv[:, a:b_], ALU.mult, ALU.add),
                              rd=["vbuf", "cw", "cv"], wr=["cv"])
                        mk.op("dve", lambda e, a=a, b_=b_, pa0=pa0, L=L: e.scalar_tensor_tensor(cv[:, a:b_], vbuf[:, pa0 + 1:pa0 + 1 + L], cw[:, fc, 2:3], cv[:, a:b_], ALU.mult, ALU.add),
                              rd=["vbuf", "cw", "cv"], wr=["cv"])
                    for bi, (t0, w) in enumerate(blocks):
                        pa, pb_ = pc[(bi % 2) * 2], pc[(bi % 2) * 2 + 1]
                        pak, pbk = "pc%d" % ((bi % 2) * 2), "pc%d" % ((bi % 2) * 2 + 1)
                        for k in range(8):
                            mk.op("pe", lambda e, k=k, pa=pa, t0=t0, w=w: e.matmul(pa[:, 0:w], lhsT=wB[:, k, 512 + fc * 128:512 + (fc + 1) * 128], rhs=aT[:, k, t0:t0 + w],
                                                                                 start=(k == 0), stop=(k == 7)), rd=atk(t0, w) + ["wB"], wr=[pak])
                        for k in range(8):
                            mk.op("pe", lambda e, k=k, pb_=pb_, t0=t0, w=w: e.matmul(pb_[:, 0:w], lhsT=wB[:, k, 1536 + fc * 128:1536 + (fc + 1) * 128], rhs=aT[:, k, t0:t0 + w],
                                                                                   start=(k == 0), stop=(k == 7)), rd=atk(t0, w) + ["wB"], wr=[pbk])
                        mk.op("act", lambda e, pb_=pb_, w=w: e.activation(sz[:, 0:w], pb_[:, 0:w], AF.Silu), rd=[pbk], wr=["sz"])
                        mk.op("dve", lambda e, pa=pa, w=w, t0=t0: e.tensor_tensor(t1[:, 0:w], pa[:, 0:w], cv[:, t0:t0 + w], ALU.mult), rd=[pak, "cv"], wr=["t1"])
                        ms = mst[nm % 2]
                        msk = "mst%d" % (nm % 2)
                        nm += 1
                        mk.op("pool", lambda e, ms=ms, w=w: e.tensor_tensor(ms[:, 0:w], t1[:, 0:w], sz[:, 0:w], ALU.mult), rd=["t1", "sz"], wr=[msk])
                        mk.dma(g.mix_d[fc * 128:(fc + 1) * 128, t0:t0 + w], ms[:, 0:w], rd=[msk], wr=["mix_d"])
                mk.barrier()
        if STOP == "A":
            return
        with contextlib.ExitStack() as stS:
            gS = Ctx(); gS.st = stS; gS.nc = nc; gS.mk = mk
            gT = sb(gS, "gT", [128, 4, T], BF16)
            s5_phase(g, gS, gT)
            if STOP in ("S1", "S2", "S3", "S4", "S3a", "S3b"):
                return
            glu_phase(g, gT)
        out_proj0(g, gate_l, gate_c)


def cmul(mk, eng, outr, outi, ar, ai, br, bi, t1, t2, key_r, key_w, neg_im=False):
    K = list(key_r)
    mk.op(eng, lambda e: e.tensor_tensor(t1, ar, br, ALU.mult), rd=K, wr=["cm_t1"])
    mk.op(eng, lambda e: e.tensor_tensor(t2, ai, bi, ALU.mult), rd=K, wr=["cm_t2"])
    mk.op(eng, lambda e: e.tensor_tensor(outr, t1, t2, ALU.subtract), rd=["cm_t1", "cm_t2"], wr=key_w)
    mk.op(eng, lambda e: e.tensor_tensor(t1, ar, bi, ALU.mult), rd=K + key_w, wr=["cm_t1"])
    mk.op(eng, lambda e: e.tensor_tensor(t2, ai, br, ALU.mult), rd=K + key_w, wr=["cm_t2"])
    if neg_im:
        mk.op(eng, lambda e: e.scalar_tensor_tensor(outi, t1, -1.0, t2, ALU.mult, ALU.subtract), rd=["cm_t1", "cm_t2"], wr=key_w)
    else:
        mk.op(eng, lambda e: e.tensor_tensor(outi, t1, t2, ALU.add), rd=["cm_t1", "cm_t2"], wr=key_w)


def s5_phase(g, gS, gT):
    mk, nc = g.mk, g.nc
    NS = 544
    st = gS.st
    phase(g, "L0_s5setup")
    sm = lambda name, shape, dt=F32: sb(gS, name, shape, dt)
    lr = sm("s5lr", [128, 32]); li = sm("s5li", [128, 32]); ls = sm("s5ls", [128, 32])
    ar = sm("s5ar", [128, 32]); ai = sm("s5ai", [128, 32])
    fr = sm("s5fr", [128, 32]); fi = sm("s5fi", [128, 32])
    ta = sm("s5ta", [128, 32]); tb = sm("s5tb", [128, 32]); tc_ = sm("s5tc", [128, 32]); td = sm("s5td", [128, 32])
    pwr = sm("s5pwr", [128, 16, 32]); pwi = sm("s5pwi", [128, 16, 32])
    ipr = sm("s5ipr", [128, 8, 32]); ipi = sm("s5ipi", [128, 8, 32])
    PBr = sm("s5PBr", [128, 8, 32]); PBi = sm("s5PBi", [128, 8, 32])
    PCr = sm("s5PCr", [128, 8, 32]); PCi = sm("s5PCi", [128, 8, 32])
    PYr = sm("s5PYr", [128, 8, 32]); PYi = sm("s5PYi", [128, 8, 32])
    Acoef = sm("s5Acoef", [128, 2, 2, 32])
    Bre = sm("s5Bre", [128, 32, 16]); Bim = sm("s5Bim", [128, 32, 16])
    bbr = sm("s5bbr", [128, 32, 16]); bbi = sm("s5bbi", [128, 32, 16])
    Cre = sm("s5Cre", [128, 32, 16]); Cim = sm("s5Cim", [128, 32, 16])
    Cld = sm("s5Cld", [128, 8, 64])
    big1 = sm("s5big1", [128, 32, 16]); big2 = sm("s5big2", [128, 32, 16])
    dcol = sm("s5dcol", [128, 32])
    maskF = sm("s5mF", [128, 128]); maskB = sm("s5mB", [128, 128])
    K = ["s5d"]
    for d in range(2):
        rows = slice(d * 64, (d + 1) * 64)
        mk.dma(lr[rows, :], g.e_lam_re[d].rearrange("g p -> p g"), wr=K, allow_slow_non_contiguous=True)
        mk.dma(li[rows, :], g.e_lam_im[d].rearrange("g p -> p g"), wr=K, allow_slow_non_contiguous=True)
        mk.dma(ls[rows, :], g.e_log_step[d:d + 1, :].to_broadcast([64, 32]), wr=K)
        mk.dma(Bre[rows, :, :], g.e_b_re[d].rearrange("g p c -> p g c"), wr=K)
        mk.dma(Bim[rows, :, :], g.e_b_im[d].rearrange("g p c -> p g c"), wr=K)
    for j in range(8):
        mk.dma(dcol[j * 16:(j + 1) * 16, :], g.e_d[0, :].rearrange("(g c) -> c g", c=16), wr=K, allow_slow_non_contiguous=True)
    V = lambda f: mk.op("dve", f, rd=K, wr=K)
    A_ = lambda f: mk.op("act", f, rd=K, wr=K)
    A_(lambda e: e.activation(ls[:], ls[:], AF.Exp))
    V(lambda e: e.tensor_tensor(ta[:], lr[:], ls[:], ALU.mult))
    A_(lambda e: e.activation(ta[:], ta[:], AF.Exp))
    V(lambda e: e.tensor_tensor(tb[:], li[:], ls[:], ALU.mult))
    range_reduce(gS, tc_[:], tb[:], 0.0, [128, 32], "s5d", "s5rr1")
    A_(lambda e: e.activation(ai[:], tc_[:], AF.Sin))
    range_reduce(gS, tc_[:], tb[:], PI / 2, [128, 32], "s5d", "s5rr2")
    A_(lambda e: e.activation(ar[:], tc_[:], AF.Sin))
    V(lambda e: e.tensor_tensor(ar[:], ar[:], ta[:], ALU.mult))
    V(lambda e: e.tensor_tensor(ai[:], ai[:], ta[:], ALU.mult))
    V(lambda e: e.tensor_scalar(ta[:], ar[:], -1.0, None, ALU.add))
    V(lambda e: e.tensor_tensor(tb[:], lr[:], lr[:], ALU.mult))
    V(lambda e: e.tensor_tensor(tc_[:], li[:], li[:], ALU.mult))
    V(lambda e: e.tensor_tensor(tb[:], tb[:], tc_[:], ALU.add))
    V(lambda e: e.reciprocal(tb[:], tb[:]))
    V(lambda e: e.tensor_tensor(tc_[:], ta[:], lr[:], ALU.mult))
    V(lambda e: e.tensor_tensor(td[:], ai[:], li[:], ALU.mult))
    V(lambda e: e.tensor_tensor(tc_[:], tc_[:], td[:], ALU.add))
    V(lambda e: e.tensor_tensor(fr[:], tc_[:], tb[:], ALU.mult))
    V(lambda e: e.tensor_tensor(tc_[:], ai[:], lr[:], ALU.mult))
    V(lambda e: e.tensor_tensor(td[:], ta[:], li[:], ALU.mult))
    V(lambda e: e.tensor_tensor(tc_[:], tc_[:], td[:], ALU.subtract))
    V(lambda e: e.tensor_tensor(fi[:], tc_[:], tb[:], ALU.mult))
    V(lambda e: e.memset(pwr[:, 0, :], 1.0))
    V(lambda e: e.memset(pwi[:, 0, :], 0.0))
    for k in range(1, 16):
        cmul(mk, "dve", pwr[:, k, :], pwi[:, k, :], pwr[:, k - 1, :], pwi[:, k - 1, :], ar[:], ai[:], tc_[:], td[:], K, K)
    V(lambda e: e.tensor_tensor(ta[:], ar[:], ar[:], ALU.mult))
    V(lambda e: e.tensor_tensor(tb[:], ai[:], ai[:], ALU.mult))
    V(lambda e: e.tensor_tensor(ta[:], ta[:], tb[:], ALU.add))
    V(lambda e: e.reciprocal(ta[:], ta[:]))
    V(lambda e: e.tensor_tensor(tb[:], ar[:], ta[:], ALU.mult))
    V(lambda e: e.scalar_tensor_tensor(ta[:], ai[:], -1.0, ta[:], ALU.mult, ALU.mult))
    V(lambda e: e.memset(ipr[:, 0, :], 1.0))
    V(lambda e: e.memset(ipi[:, 0, :], 0.0))
    for k in range(1, 8):
        cmul(mk, "dve", ipr[:, k, :], ipi[:, k, :], ipr[:, k - 1, :], ipi[:, k - 1, :], tb[:], ta[:], tc_[:], td[:], K, K)
    F_, B_ = slice(0, 64), slice(64, 128)
    for (dst, srcf, srcb) in ((PBr, ipr, pwr), (PBi, ipi, pwi), (PCr, pwr, ipr), (PCi, pwi, ipi)):
        V(lambda e, dst=dst, srcf=srcf: e.tensor_copy(dst[F_, :, :], srcf[F_, 0:8, :]))
        V(lambda e, dst=dst, srcb=srcb: e.tensor_copy(dst[B_, :, :], srcb[B_, 0:8, :]))
    for (dst, src) in ((PYr, pwr), (PYi, pwi)):
        V(lambda e, dst=dst, src=src: e.tensor_copy(dst[F_, :, :], src[F_, 8:16, :]))
        for k in range(8):
            V(lambda e, dst=dst, src=src, k=k: e.tensor_copy(dst[B_, k, :], src[B_, 8 - k, :]))
    V(lambda e: e.tensor_copy(Acoef[:, 0, 0, :], pwr[:, 8, :]))
    V(lambda e: e.tensor_scalar(Acoef[:, 0, 1, :], pwi[:, 8, :], -1.0, None, ALU.mult))
    V(lambda e: e.tensor_copy(Acoef[:, 1, 0, :], pwi[:, 8, :]))
    V(lambda e: e.tensor_copy(Acoef[:, 1, 1, :], pwr[:, 8, :]))
    if STOP == "S1":
        mk.barrier()
        return
    a2r = sm("s5a2r", [128, 32]); a2i = sm("s5a2i", [128, 32]); a4r = sm("s5a4r", [128, 32]); a4i = sm("s5a4i", [128, 32])
    Acoef4 = sm("s5Acoef4", [128, 2, 2, 32])
    cmul(mk, "dve", a2r[:], a2i[:], pwr[:, 8, :], pwi[:, 8, :], pwr[:, 8, :], pwi[:, 8, :], tc_[:], td[:], K, K)
    cmul(mk, "dve", a4r[:], a4i[:], a2r[:], a2i[:], a2r[:], a2i[:], tc_[:], td[:], K, K)
    V(lambda e: e.tensor_copy(Acoef4[:, 0, 0, :], a4r[:]))
    V(lambda e: e.tensor_scalar(Acoef4[:, 0, 1, :], a4i[:], -1.0, None, ALU.mult))
    V(lambda e: e.tensor_copy(Acoef4[:, 1, 0, :], a4i[:]))
    V(lambda e: e.tensor_copy(Acoef4[:, 1, 1, :], a4r[:]))
    frb = fr[:, :].unsqueeze(2).to_broadcast([128, 32, 16])
    fib = fi[:, :].unsqueeze(2).to_broadcast([128, 32, 16])
    cmul(mk, "dve", bbr[:], bbi[:], frb, fib, Bre[:], Bim[:], big1[:], big2[:], K, K)
    mk.op("pool", lambda e: e.memset(maskF[:], 1.0), wr=K)
    mk.op("pool", lambda e: e.memset(maskB[:], 1.0), wr=K)
    mk.op("pool", lambda e: e.affine_select(maskF[:, :].rearrange("p (t c) -> p t c", t=8), maskF[:, :].rearrange("p (t c) -> p t c", t=8),
                                            pattern=[[16, 8], [0, 16]], compare_op=ALU.is_ge, fill=0.0, base=15, channel_multiplier=-1), rd=K, wr=K)
    mk.op("pool", lambda e: e.affine_select(maskB[:, :].rearrange("p (t c) -> p t c", t=8), maskB[:, :].rearrange("p (t c) -> p t c", t=8),
                                            pattern=[[-16, 8], [0, 16]], compare_op=ALU.is_ge, fill=0.0, base=0, channel_multiplier=1), rd=K, wr=K)
    with contextlib.ExitStack() as stc:
        pcT = stc.enter_context(nc.psum_tensor(uq("s5pcT"), [128, 128], F32))
        for (src_d, dstC) in ((g.e_c_re, Cre), (g.e_c_im, Cim)):
            mk.dma(Cld[:], src_d.rearrange("d g c p -> (d g c) p").rearrange("(r q) p -> q r p", q=128), rd=K, wr=K)
            for r in range(8):
                d = r // 4
                mk.op("pe", lambda e, r=r: e.transpose(pcT[0:64, :], Cld[:, r, :], g.identf[:]), rd=K + ["identf"], wr=["s5pcT"])
                mk.op("dve", lambda e, r=r, d=d, dstC=dstC: e.tensor_copy(
                    dstC[d * 64:(d + 1) * 64, (r % 4) * 8:(r % 4) * 8 + 8, :].rearrange("p a b -> p (a b)"), pcT[0:64, :]), rd=["s5pcT"] + K, wr=K)
        mk.barrier()
    Zpad = sm("s5Zpad", [128, 8, 240], BF16)
    mk.op("pool", lambda e: e.memset(Zpad[:], 0.0), wr=["Zpad"])
    for j in range(8):
        mk.op("pool", lambda e, j=j: e.tensor_copy(Zpad[:, j, 112:128], g.identb[:, 16 * j:16 * j + 16]), rd=["identb"], wr=["Zpad"])
    if STOP == "S2":
        mk.barrier()
        return
    G = 16
    WzTr = sm("s5WzTr", [128, G, 128], BF16); WzTi = sm("s5WzTi", [128, G, 128], BF16)
    Mall = sm("s5Mall", [128, G, 128], BF16)
    WYr = sm("s5WYr", [128, G, 128], BF16); WYi = sm("s5WYi", [128, G, 128], BF16)
    gv = gT[:, :, :].rearrange("p c (n j) -> p c n j", j=8)
    for half in range(2):
        g0 = half * G
        gs = slice(g0, g0 + G)
        phase(g, "L0_wgen%d" % half)
        with contextlib.ExitStack() as stw:
            gw = Ctx(); gw.st = stw; gw.nc = nc; gw.mk = mk
            XBr = sb(gw, "XBr", [128, G, 8, 16], F32); XBi = sb(gw, "XBi", [128, G, 8, 16], F32)
            CTr = sb(gw, "CTr", [128, G, 8, 16], F32); CTi = sb(gw, "CTi", [128, G, 8, 16], F32)
            w1 = sb(gw, "s5w1", [128, G, 8, 16], F32); w2 = sb(gw, "s5w2", [128, G, 8, 16], F32)
            tM = sb(gw, "s5tM", [128, 128], F32); tM2 = sb(gw, "s5tM2", [128, 128], F32)
            pMt = [stw.enter_context(nc.psum_tensor(uq("s5pM%d" % i), [128, 512], F32)) for i in range(2)]
            pM = [pMt[0][:, 0:128], pMt[1][:, 0:128]]
            def pwb(Ptab):
                return Ptab[:, :, gs].rearrange("p k g -> p g k").unsqueeze(3).to_broadcast([128, G, 8, 16])
            def gcb(Ctab):
                return Ctab[:, gs, :].unsqueeze(2).to_broadcast([128, G, 8, 16])
            KW = ["s5w"]
            cmul(mk, "dve", XBr[:], XBi[:], pwb(PBr), pwb(PBi), gcb(bbr), gcb(bbi), w1[:], w2[:], K, KW)
            cmul(mk, "dve", CTr[:], CTi[:], pwb(PCr), pwb(PCi), gcb(Cre), gcb(Cim), w1[:], w2[:], K + KW, ["s5ct"], neg_im=True)
            cmul(mk, "dve", WYr[:, :, :].rearrange("p g (t c) -> p g t c", t=8), WYi[:, :, :].rearrange("p g (t c) -> p g t c", t=8),
                 pwb(PYr), pwb(PYi), gcb(Cre), gcb(Cim), w1[:], w2[:], K + KW + ["s5ct", "s5Y"], ["WY"], neg_im=True)
            if STOP == "S3a":
                mk.barrier()
                return
            Macc = sb(gw, "Macc", [128, G, 128], F32)
            fm = sb(gw, "s5fm", [128, 2], F32)
            xmr = sb(gw, "xmr", [128, G, 128], BF16); xmi = sb(gw, "xmi", [128, G, 128], BF16)
            xbr = sb(gw, "xbr", [128, G, 128], BF16); xbi = sb(gw, "xbi", [128, G, 128], BF16)
            ctr = sb(gw, "ctr", [128, G, 128], BF16); cti = sb(gw, "cti", [128, G, 128], BF16)
            fl = lambda t: t[:, :, :, :].rearrange("p g j c -> p g (j c)")
            mk.op("pool", lambda e: e.memset(fm[:], 0.0), wr=["s5fm"])
            mk.op("pool", lambda e: e.memset(fm[0:64, 0:1], 1.0), wr=["s5fm"])
            mk.op("pool", lambda e: e.memset(fm[64:128, 1:2], 1.0), wr=["s5fm"])
            mk.op("pool", lambda e: e.tensor_copy(ctr[:], fl(CTr)), rd=["s5ct"], wr=["ctb"])
            mk.op("pool", lambda e: e.tensor_copy(cti[:], fl(CTi)), rd=["s5ct"], wr=["ctb"])
            mk.op("pool", lambda e: e.tensor_copy(xbr[:], fl(XBr)), rd=KW, wr=["xbb"])
            mk.op("pool", lambda e: e.tensor_copy(xbi[:], fl(XBi)), rd=KW, wr=["xbb"])
            for d in range(2):
                mk.op("dve", lambda e, d=d: e.tensor_scalar(xmr[:], fl(XBr), fm[:, d:d + 1], None, ALU.mult), rd=KW + ["s5fm"], wr=["xm"])
                mk.op("dve", lambda e, d=d: e.tensor_scalar(xmi[:], fl(XBi), fm[:, d:d + 1], None, ALU.mult), rd=KW + ["s5fm"], wr=["xm"])
                for gl in range(G):
                    gi = g0 + gl
                    pm_ = pM[gl % 2]
                    pmk = "s5pM%d" % (gl % 2)
                    mk.op("pe", lambda e, pm_=pm_, gl=gl: e.matmul(pm_, lhsT=xmr[:, gl, :], rhs=ctr[:, gl, :], start=True, stop=False), rd=["xm", "ctb"], wr=[pmk])
                    mk.op("pe", lambda e, pm_=pm_, gl=gl: e.matmul(pm_, lhsT=xmi[:, gl, :], rhs=cti[:, gl, :], start=False, stop=True), rd=["xm", "ctb"], wr=[pmk])
                    if d == 0:
                        mk.op("dve", lambda e, gl=gl, pm_=pm_: e.tensor_tensor(Macc[:, gl, :], pm_, maskF[:], ALU.mult), rd=[pmk] + K, wr=[("Macc", gl)])
                    else:
                        mk.op("dve", lambda e, pm_=pm_: e.tensor_tensor(tM[:], pm_, maskB[:], ALU.mult), rd=[pmk] + K, wr=["tM"])
                        mk.op("pool", lambda e, gl=gl: e.tensor_tensor(tM[:], tM[:], Macc[:, gl, :], ALU.add), rd=["tM", ("Macc", gl)], wr=["tM"])
                        mk.op("dve", lambda e, gl=gl, gi=gi: e.scalar_tensor_tensor(Mall[:, gl, :], g.identf[:], dcol[:, gi:gi + 1], tM[:], ALU.mult, ALU.add),
                              rd=["tM", "identf"] + K, wr=["Mall"])
            if STOP == "S3b":
                mk.barrier()
                return
            for gl in range(G):
                mk.op("pe", lambda e, gl=gl: e.matmul(pM[0], lhsT=xbr[:, gl, :], rhs=g.identb[:], start=True, stop=True), rd=["xbb", "identb"], wr=["s5pM0"])
                mk.op("dve", lambda e, gl=gl: e.tensor_copy(WzTr[:, gl, :], pM[0]), rd=["s5pM0"], wr=["WzT"])
                mk.op("pe", lambda e, gl=gl: e.matmul(pM[1], lhsT=xbi[:, gl, :], rhs=g.identb[:], start=True, stop=True), rd=["xbb", "identb"], wr=["s5pM1"])
                mk.op("dve", lambda e, gl=gl: e.tensor_copy(WzTi[:, gl, :], pM[1]), rd=["s5pM1"], wr=["WzT"])
            mk.barrier()
        if STOP == "S3":
            return
        phase(g, "L0_rec%d" % half)
        with contextlib.ExitStack() as stq:
            gq = Ctx(); gq.st = stq; gq.nc = nc; gq.mk = mk
            Sh = sb(gq, "Sh", [128, 2, G, NS], BF16)
            pY = [stq.enter_context(nc.psum_tensor(uq("s5pY%d" % i), [128, 1024], F32)) for i in range(2)]
            pG = stq.enter_context(nc.psum_tensor(uq("s5pG"), [128, 1024], F32))
            pZt = stq.enter_context(nc.psum_tensor(uq("s5pZ"), [128, 1024], F32))
            pZ = [pZt[:, 0:512], pZt[:, 512:1024]]
            Uv = g.U_d.rearrange("p (g n) -> p g n", g=32)
            Ubv = g.Ub_d.rearrange("p (g n) -> p g n", g=32)
            QN = 68
            with contextlib.ExitStack() as str_:
                gr = Ctx(); gr.st = str_; gr.nc = nc; gr.mk = mk
                Uc = [sb(gr, "Uc%d" % i, [128, G, QN], BF16) for i in range(2)]
                Ubc = [sb(gr, "Ubc%d" % i, [128, G, QN], BF16) for i in range(2)]
                Zc = [sb(gr, "Zc%d" % i, [128, QN, 2, G], F32) for i in range(2)]
                NM = QN // 4
                Sw = [sb(gr, "Sw%d" % i, [128, QN, 2, G], F32) for i in range(2)]
                Wb = sb(gr, "Wb", [128, NM, 2, G], F32)
                Pw = sb(gr, "Pw", [128, NM, 2, 2 * G], F32)
                Tw = sb(gr, "Tw", [128, NM, 2, G], F32)
                Pm = sb(gr, "Pm", [128, 2, 2, G], F32)
                Tm = sb(gr, "Tm", [128, 2, G], F32)
                mk.op("dve", lambda e: e.memset(Sw[0][:, 0, :, :], 0.0), wr=["Sw0"])
                Ac = Acoef[:, :, :, gs]
                Ac4 = Acoef4[:, :, :, gs]
                Acw = Acoef[:, :, :, gs]

                def amul_add(out3, x3, y3, kx, ky, ko):
                    for ro in range(2):
                        mk.op("dve", lambda e, ro=ro: e.tensor_tensor(
                            Pw[:, :, ro, :].rearrange("p m (r g) -> p m r g", r=2),
                            Acw[:, ro, :, :].unsqueeze(1).to_broadcast([128, NM, 2, G]),
                            x3.rearrange("p m (r g) -> p m r g", r=2), ALU.mult), rd=kx + ["s5d"], wr=["Pw"])
                    Pv = Pw[:, :, :, :].rearrange("p m o (r g) -> p (m o) r g", r=2)
                    mk.op("dve", lambda e: e.tensor_tensor(Tw[:, :, :, :].rearrange("p m o g -> p (m o) g"), Pv[:, :, 0, :], Pv[:, :, 1, :], ALU.add),
                          rd=["Pw"], wr=["Tw"])
                    mk.op("dve", lambda e: e.tensor_tensor(out3, Tw[:, :, :, :].rearrange("p m o g -> p m (o g)"), y3, ALU.add), rd=["Tw"] + ky, wr=ko)

                nz = 0
                NQ = NS // QN
                for q in range(NQ):
                    zc = Zc[q % 2]
                    zck = "Zc%d" % (q % 2)
                    sw, swn = Sw[q % 2], Sw[(q + 1) % 2]
                    swk, swnk = "Sw%d" % (q % 2), "Sw%d" % ((q + 1) % 2)
                    uc, ubc = Uc[q % 2], Ubc[q % 2]
                    uck, ubck = "Uc%d" % (q % 2), "Ubc%d" % (q % 2)
                    ns = slice(q * QN, (q + 1) * QN)
                    mk.dma(uc[:], Uv[:, gs, ns], rd=["U_d"], wr=[uck])
                    mk.dma(ubc[:], Ubv[:, gs, ns], rd=["Ub_d"], wr=[ubck])
                    for g3 in range(0, G, 3):
                        pz = pZ[nz % 2]
                        zk = "s5pZ%d" % (nz % 2)
                        nz += 1
                        gls = list(range(g3, min(G, g3 + 3)))
                        for si, gl in enumerate(gls):
                            for ri, WzT in enumerate((WzTr, WzTi)):
                                slot = si * 2 + ri
                                mk.op("pe", lambda e, gl=gl, WzT=WzT, slot=slot, uc=uc, pz=pz: e.matmul(pz[0:64, slot * QN:(slot + 1) * QN], lhsT=WzT[:, gl, 0:64], rhs=uc[:, gl, :],
                                                                                                     start=True, stop=True), rd=["WzT", uck], wr=[zk])
                                mk.op("pe", lambda e, gl=gl, WzT=WzT, slot=slot, ubc=ubc, pz=pz: e.matmul(pz[64:128, slot * QN:(slot + 1) * QN], lhsT=WzT[:, gl, 64:128], rhs=ubc[:, gl, :],
                                                                                                       start=True, stop=True), rd=["WzT", ubck], wr=[zk])
                        ng = len(gls)
                        mk.op("act", lambda e, zc=zc, g3=g3, ng=ng, pz=pz: e.activation(
                            zc[:, :, :, g3:g3 + ng].rearrange("p n r g -> p g r n"),
                            pz[:, 0:ng * 2 * QN].rearrange("p (g r n) -> p g r n", g=ng, r=2), AF.Copy), rd=[zk], wr=[zck])
                    zv = zc[:, :, :, :].rearrange("p (m r) o g -> p m r (o g)", r=4)
                    swv = sw[:, :, :, :].rearrange("p (m r) o g -> p m r (o g)", r=4)
                    wv = Wb[:, :, :, :].rearrange("p m o g -> p m (o g)")
                    amul_add(wv, zv[:, :, 0, :], zv[:, :, 1, :], [zck], [zck], ["Wb"])
                    amul_add(wv, wv, zv[:, :, 2, :], ["Wb"], [zck], ["Wb"])
                    amul_add(wv, wv, zv[:, :, 3, :], ["Wb"], [zck], ["Wb"])
                    for m in range(NM):
                        cur = sw[:, 4 * m, :, :]
                        if m < NM - 1:
                            nxt, nk = sw[:, 4 * (m + 1), :, :], swk
                        else:
                            nxt, nk = swn[:, 0, :, :], swnk
                        mk.op("dve", lambda e, cur=cur: e.tensor_tensor(Pm[:], Ac4, cur.unsqueeze(1).to_broadcast([128, 2, 2, G]), ALU.mult),
                              rd=[swk, "s5d"], wr=["Pm"])
                        mk.op("dve", lambda e: e.tensor_tensor(Tm[:], Pm[:, :, 0, :], Pm[:, :, 1, :], ALU.add), rd=["Pm"], wr=["Tm"])
                        if q == NQ - 1 and m == NM - 1:
                            break
                        mk.op("dve", lambda e, nxt=nxt, m=m: e.tensor_tensor(nxt, Tm[:], Wb[:, m, :, :], ALU.add), rd=["Tm", "Wb"], wr=[nk])
                    for r in range(1, 4):
                        amul_add(swv[:, :, r, :], swv[:, :, r - 1, :], zv[:, :, r - 1, :], [swk], [zck], [swk])
                    n0 = q * QN
                    mk.op("act", lambda e, sw=sw, n0=n0: e.activation(Sh[0:64, :, :, n0:n0 + QN].rearrange("p o g n -> p n o g"), sw[0:64, :, :, :], AF.Copy),
                          rd=[swk], wr=["Sh"])
                    segs = [(0, 32, 31), (32, QN, 543)] if q == 0 else [(0, QN, 575 - n0)]
                    for (i0, i1, r0) in segs:
                        L_ = i1 - i0
                        stop = r0 - L_
                        dsl = slice(r0, stop if stop >= 0 else None, -1)
                        mk.op("pool", lambda e, sw=sw, i0=i0, i1=i1, dsl=dsl: e.tensor_copy(
                            Sh[64:128, :, :, dsl].rearrange("p o g n -> p n o g"), sw[64:128, i0:i1, :, :]), rd=[swk], wr=["Sh"])
                mk.barrier()
            if STOP == "S4":
                return
            phase(g, "L0_Y%d" % half)
            with contextlib.ExitStack() as sty:
                gy = Ctx(); gy.st = sty; gy.nc = nc; gy.mk = mk
                Uh = sb(gy, "Uh", [128, G, NS], BF16)
                Ybf = sb(gy, "Ybf", [128, 8, NS], BF16)
                ga = sb(gy, "ga", [128, NS], F32); gb = sb(gy, "gb", [128, NS], F32)
                mk.dma(Uh[:], Uv[:, gs, :], rd=["U_d"], wr=["Uh"])
                for ch in range(2):
                    chunk = half * 2 + ch
                    for g8 in range(8):
                        gl = ch * 8 + g8
                        py = pY[g8 % 2]
                        pyk = "s5pY%d" % (g8 % 2)
                        for (c0, c1) in ((0, 512), (512, NS)):
                            mk.op("pe", lambda e, gl=gl, py=py, c0=c0, c1=c1: e.matmul(py[:, c0:c1], lhsT=Mall[:, gl, :], rhs=Uh[:, gl, c0:c1], start=True, stop=False),
                                  rd=["Mall", "Uh"], wr=[pyk])
                            mk.op("pe", lambda e, gl=gl, py=py, c0=c0, c1=c1: e.matmul(py[:, c0:c1], lhsT=WYr[:, gl, :], rhs=Sh[:, 0, gl, c0:c1], start=False, stop=False),
                                  rd=["WY", "Sh"], wr=[pyk])
                            mk.op("pe", lambda e, gl=gl, py=py, c0=c0, c1=c1: e.matmul(py[:, c0:c1], lhsT=WYi[:, gl, :], rhs=Sh[:, 1, gl, c0:c1], start=False, stop=True),
                                  rd=["WY", "Sh"], wr=[pyk])
                        if g8 % 2 == 0:
                            mk.op("act", lambda e, g8=g8, py=py: e.activation(Ybf[:, g8, :], py[:, 0:NS], AF.Copy), rd=[pyk], wr=["Ybf"])
                        else:
                            mk.op("dve", lambda e, g8=g8, py=py: e.tensor_copy(Ybf[:, g8, :], py[:, 0:NS]), rd=[pyk], wr=["Ybf"])
                    for j in range(8):
                        pg = (pG, pZt)[j % 2]
                        pgk = "s5pG%d" % (j % 2)
                        for (c0, c1) in ((0, 512), (512, NS)):
                            for g8 in range(8):
                                mk.op("pe", lambda e, g8=g8, j=j, c0=c0, c1=c1, pg=pg: e.matmul(pg[:, c0:c1], lhsT=Zpad[:, j, 112 - 16 * g8:240 - 16 * g8], rhs=Ybf[:, g8, c0:c1],
                                                                                              start=(g8 == 0), stop=(g8 == 7)), rd=["Zpad", "Ybf"], wr=[pgk])
                        yv = pg[:, 0:NS]
                        mk.op("act", lambda e: e.activation(ga[:], yv, AF.Square), rd=[pgk], wr=["ga"])
                        mk.op("dve", lambda e: e.tensor_scalar(ga[:], ga[:], 0.044715, 1.0, ALU.mult, ALU.add), rd=["ga"], wr=["ga"])
                        mk.op("dve", lambda e: e.tensor_tensor(gb[:], ga[:], yv, ALU.mult), rd=["ga", pgk], wr=["gb"])
                        mk.op("act", lambda e: e.activation(gb[:], gb[:], AF.Sigmoid, scale=1.5957691216057308), rd=["gb"], wr=["gb"])
                        mk.op("dve", lambda e, j=j, chunk=chunk: e.tensor_tensor(gv[:, chunk, :, j], gb[:], yv, ALU.mult), rd=["gb", pgk], wr=["gT"])
                mk.barrier()


def glu_phase(g, gT):
    mk, nc = g.mk, g.nc
    phase(g, "L0_glu")
    with contextlib.ExitStack() as st:
        g2 = Ctx(); g2.st = st; g2.nc = nc; g2.mk = mk
        gw = sb(g2, "gluw", [128, 4, 512], BF16)
        load_w(g, gw, g.e_glu_w, 512, 512, "gluw")
        gbc = sb(g2, "glub", [128, 4], F32)
        mk.dma(gbc[:], g.e_glu_b[0, :].rearrange("(m p) -> p m", p=128), wr=["glub"], allow_slow_non_contiguous=True)
        pg = [st.enter_context(nc.psum_tensor(uq("glup%d" % i), [128, 512], F32)) for i in range(2)]
        sg = sb(g2, "glusg", [128, 512], F32)
        o1 = sb(g2, "gluo1", [128, 512], F32)
        zs = [sb(g2, "gluzs%d" % i, [128, 512], BF16) for i in range(2)]
        ms = [sb(g2, "glums%d" % i, [128, 512], BF16) for i in range(2)]
        blocks = [(0, 256)] + [(256 + 512 * b, 512) for b in range(8)]
        n = 0
        for m in range(4):
            for (t0, w) in blocks:
                b = n % 2
                n += 1
                mk.dma(zs[b][:, 0:w], g.szs_d[m * 128:(m + 1) * 128, t0:t0 + w], rd=["szs_d"], wr=["gluzs%d" % b])
                for k in range(4):
                    mk.op("pe", lambda e, k=k, m=m, b=b, t0=t0, w=w: e.matmul(pg[b][:, 0:w], lhsT=gw[:, k, m * 128:(m + 1) * 128], rhs=gT[:, k, t0:t0 + w],
                                                                             start=(k == 0), stop=(k == 3)), rd=["gluw", "gT"], wr=["glup%d" % b])
                mk.op("act", lambda e, b=b, m=m, w=w: e.activation(sg[:, 0:w], pg[b][:, 0:w], AF.Sigmoid, bias=gbc[:, m:m + 1]), rd=["glup%d" % b, "glub"], wr=["glusg"])
                mk.op("dve", lambda e, m=m, t0=t0, w=w: e.tensor_tensor(o1[:, 0:w], sg[:, 0:w], gT[:, m, t0:t0 + w], ALU.mult), rd=["glusg", "gT"], wr=["gluo1"])
                mk.op("pool", lambda e, b=b, w=w: e.tensor_tensor(ms[b][:, 0:w], o1[:, 0:w], zs[b][:, 0:w], ALU.mult), rd=["gluo1", "gluzs%d" % b], wr=["glums%d" % b])
                mk.dma(g.mix_d[512 + m * 128:512 + (m + 1) * 128, t0:t0 + w], ms[b][:, 0:w], rd=["glums%d" % b], wr=["mix_d"])
        mk.barrier()


def out_proj0(g, gate_l, gate_c):
    mk, nc = g.mk, g.nc
    phase(g, "L0_out")
    with contextlib.ExitStack() as st:
        g2 = Ctx(); g2.st = st; g2.nc = nc; g2.mk = mk
        wout = sb(g2, "wout0", [128, 8, 1024], BF16)
        load_w(g, wout, g.e_w_out, 1024, 1024, "wout0")
        mb = [sb(g2, "mixb%d" % i, [128, 8, 512], BF16) for i in range(2)]
        ht = [sb(g2, "h0t%d" % i, [128, 1024], F32) for i in range(4)]
        ot = [sb(g2, "o0t%d" % i, [128, 1024], F32) for i in range(4)]
        pD = [st.enter_context(nc.psum_tensor(uq("p0D%d" % i), [128, 1024], F32)) for i in range(2)]
        mv = g.mix_d.rearrange("(k p) t -> p k t", p=128)
        blocks = [(0, 256)] + [(256 + 512 * b, 512) for b in range(8)]
        ti = 0
        for bi, (t0, w) in enumerate(blocks):
            mbb = mb[bi % 2]
            mbk = "mixb%d" % (bi % 2)
            mk.dma(mbb[:, :, 0:w], mv[:, :, t0:t0 + w], rd=["mix_d"], wr=[mbk])
            for s in range(w // 128):
                i = (t0 + s * 128) // 128
                b = ti % 2
                ti += 1
                lat = i >= 2
                if lat:
                    src = g.x_d[(i - 2) * 128:(i - 1) * 128, :]
                    dst = g.y_d[(i - 2) * 128:(i - 1) * 128, :]
                    dk = ("y", i - 2)
                    gt, gk = gate_l, "gate_l0"
                else:
                    src = g.ctx_d[i * 128:(i + 1) * 128, :]
                    dst = g.hc1_d[i * 128:(i + 1) * 128, :]
                    dk = "hc1_d"
                    gt, gk = gate_c, "gate_c0"
                hb = ht[ti % 4]
                hbk = "h0t%d" % (ti % 4)
                mk.dma(hb[:], src, wr=[hbk])
                for n in range(2):
                    for k in range(8):
                        mk.op("pe", lambda e, n=n, k=k, b=b, s=s, mbb=mbb: e.matmul(pD[b][:, n * 512:(n + 1) * 512], lhsT=mbb[:, k, s * 128:(s + 1) * 128],
                                                                                  rhs=wout[:, k, n * 512:(n + 1) * 512], start=(k == 0), stop=(k == 7)),
                              rd=[mbk, "wout0"], wr=["p0D%d" % b])
                mk.op("dve", lambda e, b=b, gt=gt: e.tensor_tensor(ot[ti % 4][:], pD[b][:], gt[:], ALU.mult), rd=["p0D%d" % b, gk], wr=["o0t%d" % (ti % 4)])
                mk.op("pool", lambda e, b=b, hb=hb: e.tensor_tensor(ot[ti % 4][:], ot[ti % 4][:], hb[:], ALU.add), rd=["o0t%d" % (ti % 4), hbk], wr=["o0t%d" % (ti % 4)])
                mk.dma(dst, ot[ti % 4][:], rd=["o0t%d" % (ti % 4)], wr=[dk])
        mk.barrier()


def build(stage):
    nc = bass.Bass("TRN2", target_bir_lowering=False)
    g = Ctx()
    g.nc = nc
    dt = lambda name, shape, kind="ExternalInput", d=F32: nc.dram_tensor(name, list(shape), d, kind=kind).ap()
    scr = lambda name, shape, d=BF16: nc.dram_tensor(name, list(shape), d).ap()
    L0 = stage in ("L0", "both")
    L1 = stage in ("L1", "both")
    g.c_d = dt("c", [1, D])
    g.cctx_d = dt("c_ctx", [1, D])
    g.modw_d = dt("mod_w", [2, D, 3 * D])
    g.modb_d = dt("mod_b", [2, 3 * D])
    g.normw_d = dt("norm_w", [2, D])
    g.y_d = dt("y", [TL, D], "ExternalOutput")
    if L0:
        g.x_d = dt("x", [TL, D])
        g.ctx_d = dt("ctx", [TC, D])
        g.e_w_in = dt("e_w_in", [D, 3072])
        g.e_conv_w = dt("e_conv_w", [3, 512])
        g.e_lam_re = dt("e_lam_re", [2, 32, 64])
        g.e_lam_im = dt("e_lam_im", [2, 32, 64])
        g.e_log_step = dt("e_log_step", [2, 32])
        g.e_b_re = dt("e_b_re", [2, 32, 64, 16])
        g.e_b_im = dt("e_b_im", [2, 32, 64, 16])
        g.e_c_re = dt("e_c_re", [2, 32, 16, 64])
        g.e_c_im = dt("e_c_im", [2, 32, 16, 64])
        g.e_d = dt("e_d", [1, 512])
        g.e_glu_w = dt("e_glu_w", [512, 512])
        g.e_glu_b = dt("e_glu_b", [1, 512])
        g.e_w_out = dt("e_w_out", [D, D])
        g.U_d = scr("U_s", [128, 32 * 544])
        g.Ub_d = scr("Ub_s", [128, 32 * 544])
        g.szs_d = scr("szs_s", [512, T])
        g.mix_d = dt("mix_s", [D, T], "ExternalOutput", BF16) if DEBUG_L0 else scr("mix_s", [D, T])
    if stage == "L0":
        g.hc1_d = dt("hc1", [TC, D], "ExternalOutput")
    elif stage == "L1":
        g.h1_in = dt("h1", [TL, D])
        g.hc1_d = dt("hc1", [TC, D])
    else:
        g.hc1_d = scr("hc1_s", [TC, D], F32)
    if L1:
        g.o_w_in = dt("o_w_in", [D, 1696])
        g.o_q_a_norm = dt("o_q_a_norm", [1, 384])
        g.o_w_uq = dt("o_w_uq", [384, 1536])
        g.o_kv_a_norm = dt("o_kv_a_norm", [1, 256])
        g.o_w_ukv = dt("o_w_ukv", [256, 2048])
        g.o_q_norm = dt("o_q_norm", [1, 96])
        g.o_k_norm = dt("o_k_norm", [1, 96])
        g.o_w_out = dt("o_w_out", [D, D])
        g.qT_d = scr("qT_s", [NH, 96, TL])
        g.kT_d = scr("kT_s", [NH, 96, T])
        g.v_d = scr("v_s", [T, D])
        g.sg_d = scr("sg_s", [D, TL])
    with contextlib.ExitStack() as st:
        g.st = st
        g.mk = MK(nc, st)
        setup_consts(g)
        if L1:
            rope_tables(g)
        if stage == "L1":
            for lt in range(32):
                g.mk.dma(g.y_d[lt * 128:(lt + 1) * 128, :], g.h1_in[lt * 128:(lt + 1) * 128, :], wr=[("y", lt)])
        if L0:
            layer0(g)
        if L1:
            layer1(g)
        phase(g, None)
        g.mk.wait_all("sp", [("y", lt) for lt in range(32)] + ["hc1_d"])
        g.mk.barrier()
    return nc


L0_IN = ["e_w_in", "e_conv_w", "e_lam_re", "e_lam_im", "e_log_step", "e_b_re", "e_b_im", "e_c_re", "e_c_im", "e_d", "e_glu_w", "e_glu_b", "e_w_out"]
L1_IN = ["o_w_in", "o_q_a_norm", "o_w_uq", "o_kv_a_norm", "o_w_ukv", "o_q_norm", "o_k_norm", "o_w_out"]
_NC_CACHE = {}


def _get_nc(stage):
    if stage not in _NC_CACHE:
        _NC_CACHE[stage] = build(stage)
    return _NC_CACHE[stage]


def _common(inp, b):
    f = lambda a: np.ascontiguousarray(a, dtype=np.float32)
    return {"c": f(inp["c"][b:b + 1]), "c_ctx": f(np.asarray(inp["c_ctx"])[None, :]), "mod_w": f(inp["mod_w"]),
            "mod_b": f(inp["mod_b"]), "norm_w": f(inp["norm_w"])}


def _l0_map(inp, b):
    f = lambda a: np.ascontiguousarray(a, dtype=np.float32)
    m = {"x": f(inp["x"][b]), "ctx": f(inp["ctx"][b])}
    for k in L0_IN:
        m[k] = f(np.asarray(inp[k])[0]) if k not in ("e_d", "e_glu_b") else f(np.asarray(inp[k]))
    return m


def _l1_map(inp):
    f = lambda a: np.ascontiguousarray(a, dtype=np.float32)
    m = {}
    for k in L1_IN:
        m[k] = f(np.asarray(inp[k])[0]) if k in ("o_w_in", "o_w_uq", "o_w_ukv", "o_w_out") else f(np.asarray(inp[k]))
    return m


FUSED = True
STOP = None
DEBUG_L0 = False


def kernel(**inp):
    n = 8
    cores = list(range(n))
    if FUSED:
        nc = _get_nc("both")
        maps = []
        for b in range(n):
            m = _common(inp, b)
            m.update(_l0_map(inp, b))
            m.update(_l1_map(inp))
            maps.append(m)
        res = run_bass_kernel_spmd(nc, maps, core_ids=cores)
        return np.stack([np.asarray(r["y"], dtype=np.float32) for r in res.results], axis=0)
    nc0 = _get_nc("L0")
    maps = []
    for b in range(n):
        m = _common(inp, b)
        m.update(_l0_map(inp, b))
        maps.append(m)
    res0 = run_bass_kernel_spmd(nc0, maps, core_ids=cores)
    nc1 = _get_nc("L1")
    maps = []
    for b in range(n):
        m = _common(inp, b)
        m.update(_l1_map(inp))
        m["h1"] = np.ascontiguousarray(res0.results[b]["y"], dtype=np.float32)
        m["hc1"] = np.ascontiguousarray(res0.results[b]["hc1"], dtype=np.float32)
        maps.append(m)
    res1 = run_bass_kernel_spmd(nc1, maps, core_ids=cores)
    return np.stack([np.asarray(r["y"], dtype=np.float32) for r in res1.results], axis=0)
```
